# Optimizing a Trainium2 kernel written in Bass

```python
import jax
import jax.numpy as jnp
from jax import lax
import numpy as np

D_MODEL = 2048
BATCH = 8
SEQ = 2048
DEPTH = 2
DEC_BATCH = 8
DEC_SEQ = 64
PAST_LEN = 2048

CHUNK = 64
N_BRANCH = 4
MIX_W = D_MODEL // 4
D_FF = 4 * D_MODEL
NORM_EPS = 1e-6

A_HD = 64
A_HEADS = MIX_W // A_HD
A_DECAY_RANK = 64
A_ICL_RANK = 64
A_GATE_RANK = 128
A_DECAY_SCALE = 0.6065306597
A_GN_EPS = 64e-5
A_IN = 3 * MIX_W + A_DECAY_RANK + A_ICL_RANK + A_GATE_RANK

B_HEADS = 4
B_DK = MIX_W // (2 * B_HEADS)
B_DV = MIX_W // B_HEADS
B_GATE_RANK = 16
B_GATE_TEMP = 16.0
B_IN = 2 * B_HEADS * B_DK + MIX_W + B_GATE_RANK + MIX_W

C_HD = 64
C_HEADS = MIX_W // C_HD
C_BAND_CHUNKS = 8
C_REL_CLIP = 2 * CHUNK
C_IN = 3 * MIX_W

D_HD = 128
D_HEADS = MIX_W // D_HD
D_CONV = 4
D_IN = 3 * MIX_W + 2 * D_HEADS + MIX_W

IN_TOTAL = A_IN + B_IN + C_IN + D_IN

kernel_name = 'hybrid_streaming_encoder_step'


def rms_norm(x, gain, eps=NORM_EPS):
    xf = x.astype(jnp.float32)
    xf = xf * lax.rsqrt(jnp.mean(xf * xf, axis=-1, keepdims=True) + eps)
    return (xf * gain.astype(jnp.float32)).astype(x.dtype)


def l2_normalize(x, eps=1e-6):
    x = x.astype(jnp.float32)
    return x * lax.rsqrt(jnp.sum(x * x, axis=-1, keepdims=True) + eps)


def split_cols(z, sizes):
    parts, start = [], 0
    for size in sizes:
        parts.append(z[..., start:start + size])
        start += size
    return parts


def rwkv7_mix(p, shift_prev, s0, mu, w0, w_up, a0, a_up, g_up, k_k, k_a, r_k, ln_w, ln_b):
    bsz, t, _ = p.shape
    dt = p.dtype
    prev = jnp.concatenate([shift_prev.astype(dt), p[:, :-1]], axis=1)
    xs = p + (prev - p) * mu
    r, k, v, wl, al, gl = split_cols(xs, (MIX_W, MIX_W, MIX_W, A_DECAY_RANK, A_ICL_RANK, A_GATE_RANK))
    log_w = -A_DECAY_SCALE * jax.nn.sigmoid((w0 + jnp.tanh(wl) @ w_up).astype(jnp.float32))
    a = jax.nn.sigmoid((a0 + al @ a_up).astype(jnp.float32))
    g = jax.nn.sigmoid(gl) @ g_up

    def heads(z):
        return z.reshape(bsz, t, A_HEADS, A_HD).astype(jnp.float32)

    kk = l2_normalize(heads(k * k_k))
    a_h = heads(a)
    k_h = heads(k * (1.0 + (a - 1.0) * k_a))
    r_h, v_h, decay = heads(r), heads(v), jnp.exp(heads(log_w))

    def step(s, inp):
        r_t, k_t, v_t, kk_t, a_t, w_t = inp
        s_kk = jnp.einsum('bhvk,bhk->bhv', s, kk_t)
        s = (s * w_t[:, :, None, :] - s_kk[..., None] * (kk_t * a_t)[:, :, None, :]
             + v_t[..., None] * k_t[:, :, None, :])
        return s, jnp.einsum('bhvk,bhk->bhv', s, r_t)

    def tm(z):
        return jnp.moveaxis(z, 1, 0)

    s_fin, y = lax.scan(step, s0.astype(jnp.float32),
                        (tm(r_h), tm(k_h), tm(v_h), tm(kk), tm(a_h), tm(decay)))
    y = jnp.moveaxis(y, 0, 1)
    mean = jnp.mean(y, axis=-1, keepdims=True)
    var = jnp.mean(jnp.square(y - mean), axis=-1, keepdims=True)
    y = ((y - mean) * lax.rsqrt(var + A_GN_EPS)).reshape(bsz, t, MIX_W) * ln_w + ln_b
    bonus = jnp.sum(r_h * k_h * r_k, axis=-1, keepdims=True) * v_h
    y = (y + bonus.reshape(bsz, t, MIX_W)) * g
    return y.astype(dt), p[:, -1:], s_fin


def gla_mix(p, s0, alpha_up, alpha_bias, norm_w):
    bsz, t, _ = p.shape
    dt = p.dtype
    chunk = min(CHUNK, t)
    n_c = t // chunk
    qk = B_HEADS * B_DK
    q, k, v, al, rg = split_cols(p, (qk, qk, MIX_W, B_GATE_RANK, MIX_W))
    log_a = jax.nn.log_sigmoid((al @ alpha_up + alpha_bias).astype(jnp.float32)) / B_GATE_TEMP

    def blocks(z, d):
        return z.astype(jnp.float32).reshape(bsz, n_c, chunk, B_HEADS, d).transpose(1, 0, 3, 2, 4)

    qb = blocks(q, B_DK) * B_DK ** -0.5
    kb, vb = blocks(k, B_DK), blocks(v, B_DV)
    cum = jnp.cumsum(blocks(log_a, B_DK), axis=3)
    causal = jnp.tril(jnp.ones((chunk, chunk), bool))

    def step(s, inp):
        q_c, k_c, v_c, b_c = inp
        inter = jnp.einsum('bhid,bhdv->bhiv', q_c * jnp.exp(b_c), s)
        rel = jnp.exp(jnp.where(causal[:, :, None],
                                b_c[:, :, :, None, :] - b_c[:, :, None, :, :], -jnp.inf))
        att = jnp.einsum('bhid,bhjd,bhijd->bhij', q_c, k_c, rel)
        intra = jnp.einsum('bhij,bhjv->bhiv', att, v_c)
        b_last = b_c[:, :, -1:, :]
        s = (jnp.exp(b_last[:, :, 0, :])[..., None] * s
             + jnp.einsum('bhjd,bhjv->bhdv', k_c * jnp.exp(b_last - b_c), v_c))
        return s, inter + intra

    s_fin, o = lax.scan(step, s0.astype(jnp.float32), (qb, kb, vb, cum))
    o = o.transpose(1, 0, 3, 2, 4).reshape(bsz, t, B_HEADS, B_DV)
    o = o * lax.rsqrt(jnp.mean(o * o, axis=-1, keepdims=True) + NORM_EPS)
    o = o.reshape(bsz, t, MIX_W) * norm_w * jax.nn.silu(rg.astype(jnp.float32))
    return o.astype(dt), s_fin


def band_attend(q, k, v, key_valid, n_past, rel_table):
    cq, lk = q.shape[1], k.shape[1]
    s = jnp.einsum('bihd,bjhd->bhij', q, k).astype(jnp.float32) * C_HD ** -0.5
    dist = jnp.arange(cq)[:, None] + n_past - jnp.arange(lk)[None, :]
    bias = rel_table[:, jnp.clip(dist, -C_REL_CLIP, C_REL_CLIP) + C_REL_CLIP]
    s = jnp.where(key_valid, s + bias.astype(jnp.float32), -jnp.inf)
    prob = jax.nn.softmax(s, axis=-1)
    return jnp.einsum('bhij,bjhd->bihd', prob.astype(v.dtype), v)


def band_mix_prompt(p, rel_table):
    bsz, t, _ = p.shape
    q, k, v = [z.reshape(bsz, t, C_HEADS, C_HD) for z in split_cols(p, (MIX_W, MIX_W, MIX_W))]
    band = C_BAND_CHUNKS * CHUNK
    pad = ((0, 0), (band, 0), (0, 0), (0, 0))
    kp, vp = jnp.pad(k, pad), jnp.pad(v, pad)

    def block(c):
        start = c * CHUNK
        qb = lax.dynamic_slice_in_dim(q, start, CHUNK, axis=1)
        kb = lax.dynamic_slice_in_dim(kp, start, band + CHUNK, axis=1)
        vb = lax.dynamic_slice_in_dim(vp, start, band + CHUNK, axis=1)
        valid = jnp.arange(band + CHUNK) >= band - start
        return band_attend(qb, kb, vb, valid, band, rel_table)

    o = lax.map(block, jnp.arange(t // CHUNK))
    o = o.transpose(1, 0, 2, 3, 4).reshape(bsz, t, MIX_W)
    keep = min(band, t)
    return o, k[:, t - keep:], v[:, t - keep:]


def band_mix_sample(p, cache_k, cache_v, rel_table):
    bsz, t, _ = p.shape
    q, k, v = [z.reshape(bsz, t, C_HEADS, C_HD) for z in split_cols(p, (MIX_W, MIX_W, MIX_W))]
    kb = jnp.concatenate([cache_k.astype(k.dtype), k], axis=1)
    vb = jnp.concatenate([cache_v.astype(v.dtype), v], axis=1)
    valid = jnp.ones((kb.shape[1],), bool)
    o = band_attend(q, kb, vb, valid, cache_k.shape[1], rel_table)
    return o.reshape(bsz, t, MIX_W), k, v


def deltanet_mix(p, conv_buf, s0, conv_w, a_log, dt_bias, norm_w):
    bsz, t, _ = p.shape
    dt = p.dtype
    chunk = min(CHUNK, t)
    n_c = t // chunk
    qkv_raw, beta_raw, a_raw, gate = split_cols(p, (3 * MIX_W, D_HEADS, D_HEADS, MIX_W))
    xc = jnp.concatenate([conv_buf.astype(dt), qkv_raw], axis=1)
    conv = xc[:, 0:t] * conv_w[0]
    for i in range(1, D_CONV):
        conv = conv + xc[:, i:i + t] * conv_w[i]
    qkv = jax.nn.silu(conv.astype(jnp.float32))
    q, k, v = [z.reshape(bsz, t, D_HEADS, D_HD) for z in split_cols(qkv, (MIX_W, MIX_W, MIX_W))]
    q = l2_normalize(q) * D_HD ** -0.5
    k = l2_normalize(k)
    beta = jax.nn.sigmoid(beta_raw.astype(jnp.float32))
    g = -jnp.exp(a_log.astype(jnp.float32)) * jax.nn.softplus(
        (a_raw + dt_bias).astype(jnp.float32))

    def blocks(z):
        return z.reshape(bsz, n_c, chunk, D_HEADS, z.shape[-1]).transpose(0, 3, 1, 2, 4)

    qb, kb, vb = blocks(q), blocks(k), blocks(v)
    beta_b = blocks(beta[..., None])
    cum = jnp.cumsum(blocks(g[..., None]), axis=3)
    diff = cum - jnp.swapaxes(cum, -1, -2)
    incl = jnp.tril(jnp.ones((chunk, chunk), bool))
    strict = jnp.tril(jnp.ones((chunk, chunk), bool), -1)
    decay = jnp.exp(jnp.where(incl, diff, -jnp.inf))
    kk = jnp.einsum('bhnid,bhnjd->bhnij', kb, kb)
    tri = jnp.eye(chunk, dtype=jnp.float32) + jnp.where(strict, beta_b * decay * kk, 0.0)
    u = lax.linalg.triangular_solve(tri, vb * beta_b, left_side=True, lower=True,
                                    unit_diagonal=True)
    w = lax.linalg.triangular_solve(tri, kb * (beta_b * jnp.exp(cum)), left_side=True,
                                    lower=True, unit_diagonal=True)
    att = jnp.einsum('bhnid,bhnjd->bhnij', qb, kb) * decay
    q_dec = qb * jnp.exp(cum)
    k_dec = kb * jnp.exp(cum[:, :, :, -1:] - cum)
    g_last = jnp.exp(cum[:, :, :, -1])

    def step(s, inp):
        u_c, w_c, att_c, qd_c, kd_c, gl_c = inp
        delta = u_c - jnp.einsum('bhik,bhkv->bhiv', w_c, s)
        o_c = (jnp.einsum('bhik,bhkv->bhiv', qd_c, s)
               + jnp.einsum('bhij,bhjv->bhiv', att_c, delta))
        s = gl_c[..., None] * s + jnp.einsum('bhjk,bhjv->bhkv', kd_c, delta)
        return s, o_c

    def sw(z):
        return jnp.moveaxis(z, 2, 0)

    s_fin, o = lax.scan(step, s0.astype(jnp.float32),
                        (sw(u), sw(w), sw(att), sw(q_dec), sw(k_dec), sw(g_last)))
    o = jnp.moveaxis(o, 0, 2).transpose(0, 2, 3, 1, 4).reshape(bsz, t, D_HEADS, D_HD)
    o = o * lax.rsqrt(jnp.mean(o * o, axis=-1, keepdims=True) + NORM_EPS) * norm_w
    o = o.reshape(bsz, t, MIX_W) * jax.nn.silu(gate.astype(jnp.float32))
    return o.astype(dt), xc[:, t:], s_fin


def layer_forward(x, shift_prev, s_a, s_b, band_k, band_v, conv_buf, s_d, lw):
    u = rms_norm(x, lw['norm_mix_pre'])
    p_a, p_b, p_c, p_d = split_cols(u @ lw['w_in'], (A_IN, B_IN, C_IN, D_IN))
    y_a, shift_new, s_a_new = rwkv7_mix(p_a, shift_prev, s_a, lw['a_mu'], lw['a_w0'], lw['a_w_up'],
                                        lw['a_a0'], lw['a_a_up'], lw['a_g_up'], lw['a_k_k'],
                                        lw['a_k_a'], lw['a_r_k'], lw['a_ln_w'], lw['a_ln_b'])
    y_b, s_b_new = gla_mix(p_b, s_b, lw['b_alpha_up'], lw['b_alpha_bias'], lw['b_norm'])
    if band_k is None:
        y_c, k_new, v_new = band_mix_prompt(p_c, lw['c_rel_bias'])
    else:
        y_c, k_new, v_new = band_mix_sample(p_c, band_k, band_v, lw['c_rel_bias'])
    y_d, conv_new, s_d_new = deltanet_mix(p_d, conv_buf, s_d, lw['d_conv_w'], lw['d_a_log'],
                                          lw['d_dt_bias'], lw['d_norm'])
    merged = None
    for i, y_br in enumerate((y_a, y_b, y_c, y_d)):
        term = jax.nn.sigmoid(u @ lw['w_merge_gate'][i]) * (y_br @ lw['w_branch'][i])
        merged = term if merged is None else merged + term
    h = x + rms_norm(merged @ lw['w_out'], lw['norm_mix_post'])
    z = rms_norm(h, lw['norm_ffn_pre'])
    f = jnp.square(jax.nn.relu(z @ lw['w_ffn_up'])) @ lw['w_ffn_down']
    out = h + rms_norm(f, lw['norm_ffn_post'])
    return out, (shift_new, s_a_new, s_b_new, k_new, v_new, conv_new, s_d_new)


def stack_states(states, i):
    return jnp.stack([s[i] for s in states])


def setup_inputs(seed: int = 0) -> dict:
    key = jax.random.key(seed)
    keys = iter(jax.random.split(key, 48))
    f32 = jnp.float32

    def normal(shape, scale):
        return scale * jax.random.normal(next(keys), shape, f32)

    def gain(shape, base=1.0):
        return base + 0.05 * jax.random.normal(next(keys), shape, f32)

    def uniform(shape, lo, hi):
        return jax.random.uniform(next(keys), shape, f32, lo, hi)

    band_past = min(C_BAND_CHUNKS * CHUNK, PAST_LEN)
    return {
        'x_prompt': normal((BATCH, SEQ, D_MODEL), 1.0),
        'x_sample': normal((DEC_BATCH, DEC_SEQ, D_MODEL), 1.0),
        'state_rwkv_shift': normal((DEPTH, DEC_BATCH, 1, A_IN), 1.0),
        'state_rwkv': normal((DEPTH, DEC_BATCH, A_HEADS, A_HD, A_HD), 0.5),
        'state_gla': normal((DEPTH, DEC_BATCH, B_HEADS, B_DK, B_DV), 0.5),
        'cache_band_k': normal((DEPTH, DEC_BATCH, band_past, C_HEADS, C_HD), 1.0),
        'cache_band_v': normal((DEPTH, DEC_BATCH, band_past, C_HEADS, C_HD), 1.0),
        'state_dn_conv': normal((DEPTH, DEC_BATCH, D_CONV - 1, 3 * MIX_W), 1.0),
        'state_dn': normal((DEPTH, DEC_BATCH, D_HEADS, D_HD, D_HD), 0.3),
        'norm_mix_pre': gain((DEPTH, D_MODEL)),
        'norm_mix_post': gain((DEPTH, D_MODEL)),
        'norm_ffn_pre': gain((DEPTH, D_MODEL)),
        'norm_ffn_post': gain((DEPTH, D_MODEL)),
        'w_in': normal((DEPTH, D_MODEL, IN_TOTAL), D_MODEL ** -0.5),
        'w_merge_gate': normal((DEPTH, N_BRANCH, D_MODEL, D_MODEL), D_MODEL ** -0.5),
        'w_branch': normal((DEPTH, N_BRANCH, MIX_W, D_MODEL), MIX_W ** -0.5),
        'w_out': normal((DEPTH, D_MODEL, D_MODEL), D_MODEL ** -0.5),
        'w_ffn_up': normal((DEPTH, D_MODEL, D_FF), D_MODEL ** -0.5),
        'w_ffn_down': normal((DEPTH, D_FF, D_MODEL), D_FF ** -0.5),
        'a_mu': uniform((DEPTH, A_IN), 0.0, 1.0),
        'a_w0': normal((DEPTH, MIX_W), 0.5),
        'a_w_up': normal((DEPTH, A_DECAY_RANK, MIX_W), A_DECAY_RANK ** -0.5),
        'a_a0': normal((DEPTH, MIX_W), 0.3),
        'a_a_up': normal((DEPTH, A_ICL_RANK, MIX_W), A_ICL_RANK ** -0.5),
        'a_g_up': normal((DEPTH, A_GATE_RANK, MIX_W), A_GATE_RANK ** -0.5),
        'a_k_k': gain((DEPTH, MIX_W), 0.85),
        'a_k_a': gain((DEPTH, MIX_W)),
        'a_r_k': normal((DEPTH, A_HEADS, A_HD), 0.3),
        'a_ln_w': gain((DEPTH, MIX_W)),
        'a_ln_b': normal((DEPTH, MIX_W), 0.02),
        'b_alpha_up': normal((DEPTH, B_GATE_RANK, B_HEADS * B_DK), B_GATE_RANK ** -0.5),
        'b_alpha_bias': normal((DEPTH, B_HEADS * B_DK), 0.5),
        'b_norm': gain((DEPTH, MIX_W)),
        'c_rel_bias': normal((DEPTH, C_HEADS, 2 * C_REL_CLIP + 1), 0.5),
        'd_conv_w': normal((DEPTH, D_CONV, 3 * MIX_W), D_CONV ** -0.5),
        'd_a_log': jnp.log(uniform((DEPTH, D_HEADS), 1.0, 16.0)),
        'd_dt_bias': uniform((DEPTH, D_HEADS), -6.0, -2.0),
        'd_norm': gain((DEPTH, D_HD)),
    }


def reference(x_prompt, x_sample, state_rwkv_shift, state_rwkv, state_gla, cache_band_k,
              cache_band_v, state_dn_conv, state_dn, norm_mix_pre, norm_mix_post, norm_ffn_pre,
              norm_ffn_post, w_in, w_merge_gate, w_branch, w_out, w_ffn_up, w_ffn_down, a_mu,
              a_w0, a_w_up, a_a0, a_a_up, a_g_up, a_k_k, a_k_a, a_r_k, a_ln_w, a_ln_b,
              b_alpha_up, b_alpha_bias, b_norm, c_rel_bias, d_conv_w, d_a_log, d_dt_bias, d_norm):
    bsz = x_prompt.shape[0]
    f32 = jnp.float32
    y_p, y_s = x_prompt, x_sample
    new_p, new_s = [], []
    for l in range(DEPTH):
        lw = {
            'norm_mix_pre': norm_mix_pre[l], 'norm_mix_post': norm_mix_post[l],
            'norm_ffn_pre': norm_ffn_pre[l], 'norm_ffn_post': norm_ffn_post[l],
            'w_in': w_in[l], 'w_merge_gate': w_merge_gate[l], 'w_branch': w_branch[l],
            'w_out': w_out[l], 'w_ffn_up': w_ffn_up[l], 'w_ffn_down': w_ffn_down[l],
            'a_mu': a_mu[l], 'a_w0': a_w0[l], 'a_w_up': a_w_up[l], 'a_a0': a_a0[l],
            'a_a_up': a_a_up[l], 'a_g_up': a_g_up[l], 'a_k_k': a_k_k[l], 'a_k_a': a_k_a[l],
            'a_r_k': a_r_k[l], 'a_ln_w': a_ln_w[l], 'a_ln_b': a_ln_b[l],
            'b_alpha_up': b_alpha_up[l], 'b_alpha_bias': b_alpha_bias[l], 'b_norm': b_norm[l],
            'c_rel_bias': c_rel_bias[l], 'd_conv_w': d_conv_w[l], 'd_a_log': d_a_log[l],
            'd_dt_bias': d_dt_bias[l], 'd_norm': d_norm[l],
        }
        y_p, st_p = layer_forward(
            y_p,
            jnp.zeros((bsz, 1, A_IN), y_p.dtype),
            jnp.zeros((bsz, A_HEADS, A_HD, A_HD), f32),
            jnp.zeros((bsz, B_HEADS, B_DK, B_DV), f32),
            None, None,
            jnp.zeros((bsz, D_CONV - 1, 3 * MIX_W), y_p.dtype),
            jnp.zeros((bsz, D_HEADS, D_HD, D_HD), f32),
            lw)
        y_s, st_s = layer_forward(y_s, state_rwkv_shift[l], state_rwkv[l], state_gla[l],
                                  cache_band_k[l], cache_band_v[l], state_dn_conv[l],
                                  state_dn[l], lw)
        new_p.append(st_p)
        new_s.append(st_s)
    return (y_p, y_s,
            stack_states(new_p, 0), stack_states(new_p, 1), stack_states(new_p, 2),
            stack_states(new_p, 3), stack_states(new_p, 4), stack_states(new_p, 5),
            stack_states(new_p, 6),
            stack_states(new_s, 0), stack_states(new_s, 1), stack_states(new_s, 2),
            stack_states(new_s, 3), stack_states(new_s, 4), stack_states(new_s, 5),
            stack_states(new_s, 6))
```

```python
import numpy as np
from contextlib import ExitStack, contextmanager
import concourse.bass as bass
import concourse.mybir as mybir
from concourse.bass_utils import run_bass_kernel_spmd

F32 = mybir.dt.float32
F32R = mybir.dt.float32r
AF = mybir.ActivationFunctionType
ALU = mybir.AluOpType
AX = mybir.AxisListType

ENG = ["pe", "act", "dve", "pool", "sp"]
NDMA = 40


class V:
    __slots__ = ("key", "ap")

    def __init__(self, key, ap):
        self.key = key
        self.ap = ap

    def __getitem__(self, idx):
        return V(self.key, self.ap[idx])

    def bitcast(self, dt):
        return V(self.key, self.ap.bitcast(dt))

    def rearrange(self, pat, **kw):
        return V(self.key, self.ap.rearrange(pat, **kw))

    def bcast(self, shape):
        return V(self.key, self.ap.broadcast_to(shape))

    def pbcast(self, n):
        return V(self.key, self.ap.partition_broadcast(n))

    @property
    def shape(self):
        return self.ap.shape


def _ap(x):
    return x.ap if isinstance(x, V) else x


class Prog:
    def __init__(self, nc, es):
        self.nc = nc
        self.sem = {e: es.enter_context(nc.semaphore("sem_" + e)) for e in ENG}
        self.dsem = [es.enter_context(nc.semaphore("dsem%d" % i)) for i in range(NDMA)]
        self.cnt = {e: 0 for e in ENG}
        self.known = {e: {f: 0 for f in ENG} for e in ENG}
        self.dval = [0] * NDMA
        self.dknown = {e: [0] * NDMA for e in ENG}
        self.dnext = 0
        self.ops = None
        self.lastw = {}
        self.readers = {}
        self.n_instr = 0
        self.uid = 0

    def eng_obj(self, e):
        nc = self.nc
        return {"pe": nc.tensor, "act": nc.scalar, "dve": nc.vector, "pool": nc.gpsimd, "sp": nc.sync}[e]

    def _deps(self, reads, writes):
        toks = []
        for r in reads:
            t = self.lastw.get(r)
            if t is not None:
                toks.append(t)
        for w in writes:
            t = self.lastw.get(w)
            if t is not None:
                toks.append(t)
            toks.extend(self.readers.get(w, ()))
        return toks

    def _waits(self, e, toks):
        waits = []
        need_e = {}
        need_d = {}
        for t in toks:
            if t[0] == "eng":
                _, f, c = t
                if f == e and e == "pe":
                    continue
                if self.known[e][f] < c:
                    need_e[f] = max(need_e.get(f, 0), c)
            else:
                _, s, v = t
                if self.dknown[e][s] < v:
                    need_d[s] = max(need_d.get(s, 0), v)
        for f, c in need_e.items():
            self.known[e][f] = c
            waits.append((self.sem[f], c))
        for s, v in need_d.items():
            self.dknown[e][s] = v
            waits.append((self.dsem[s], v))
        return waits

    def _commit(self, tok, reads, writes):
        for w in writes:
            self.lastw[w] = tok
            self.readers[w] = []
        for r in reads:
            if r in writes:
                continue
            self.readers.setdefault(r, []).append(tok)

    def op(self, e, fn, reads=(), writes=()):
        reads = [r.key if isinstance(r, V) else r for r in reads]
        writes = [w.key if isinstance(w, V) else w for w in writes]
        for r in reads:
            if r.startswith("psbank") and r not in writes:
                writes.append(r)
        waits = self._waits(e, self._deps(reads, writes))
        self.cnt[e] += 1
        tok = ("eng", e, self.cnt[e])
        self.ops[e].append((waits, fn, (self.sem[e], 1)))
        self._commit(tok, reads, writes)
        self.n_instr += 1 + len(waits)

    def dma(self, out, in_, q="sp"):
        reads = [in_.key]
        writes = [out.key]
        s = self.dnext
        self.dnext = (self.dnext + 1) % NDMA
        toks = self._deps(reads, writes)
        if self.dval[s] > 0:
            toks.append(("dma", s, self.dval[s]))
        waits = self._waits(q, toks)
        self.dval[s] += 16
        tok = ("dma", s, self.dval[s])
        oa, ia = out.ap, in_.ap
        self.ops[q].append((waits, lambda eng: eng.dma_start(out=oa, in_=ia), (self.dsem[s], 16)))
        self._commit(tok, reads, writes)
        self.n_instr += 1 + len(waits)

    @contextmanager
    def stage(self, name):
        self.ops = {e: [] for e in ENG}
        self.lastw = {}
        self.readers = {}
        pre = {e: [] for e in ENG}
        for e in ENG:
            for f in ENG:
                if f != e and self.known[e][f] < self.cnt[f]:
                    pre[e].append((self.sem[f], self.cnt[f]))
                    self.known[e][f] = self.cnt[f]
            for s in range(NDMA):
                if self.dknown[e][s] < self.dval[s]:
                    pre[e].append((self.dsem[s], self.dval[s]))
                    self.dknown[e][s] = self.dval[s]
            self.n_instr += len(pre[e])
        yield
        fin = []
        for s in range(NDMA):
            if self.dknown["sp"][s] < self.dval[s]:
                fin.append((self.dsem[s], self.dval[s]))
        self.dknown["sp"] = list(self.dval)
        ops = self.ops
        nc = self.nc
        with nc.allow_non_contiguous_dma(reason="small strided param/state loads"), nc.Block(name, no_gpsimd_drain=True) as block:
            def mk(e):
                def body(eng):
                    for (s, v) in pre[e]:
                        eng.wait_ge(s, v)
                    for waits, fn, (sem, inc) in ops[e]:
                        for (s, v) in waits:
                            eng.wait_ge(s, v)
                        fn(eng).then_inc(sem, inc)
                    if e == "sp":
                        for (s, v) in fin:
                            eng.wait_ge(s, v)
                return body
            block.tensor(mk("pe"))
            block.scalar(mk("act"))
            block.vector(mk("dve"))
            block.gpsimd(mk("pool"))
            block.sync(mk("sp"))
        self.ops = None

    def sb(self, es, shape, dt=F32, name=None):
        self.uid += 1
        name = name or "t"
        t = es.enter_context(self.nc.sbuf_tensor("%s_%d" % (name, self.uid), list(shape), dt))
        return V("%s_%d" % (name, self.uid), t[:] if hasattr(t, "__getitem__") else t.ap())

    def psum(self, es, ncols=4096):
        self.uid += 1
        t = es.enter_context(self.nc.psum_tensor("ps_%d" % self.uid, [128, ncols], F32))
        return t

    def mm(self, out, lhsT, rhs, start=True, stop=True):
        o, l, r = out.ap, lhsT.ap, rhs.ap
        self.op("pe", lambda e: e.matmul(o, lhsT=l, rhs=r, start=start, stop=stop),
                reads=[lhsT, rhs], writes=[out])

    def transpose(self, out, in_, ident):
        o, i, d = out.ap, in_.ap, ident.ap
        self.op("pe", lambda e: e.transpose(o, i, d), reads=[in_, ident], writes=[out])

    def act(self, out, in_, func, bias=None, scale=None, accum=None, eng="act"):
        o, i = out.ap, in_.ap
        kw = {}
        reads = [in_]
        if bias is not None:
            kw["bias"] = _ap(bias)
            if isinstance(bias, V):
                reads.append(bias)
        if scale is not None:
            kw["scale"] = _ap(scale)
            if isinstance(scale, V):
                reads.append(scale)
        writes = [out]
        if accum is not None:
            kw["accum_out"] = accum.ap
            writes.append(accum)
        self.op("act", lambda e: e.activation(out=o, in_=i, func=func, **kw), reads=reads, writes=writes)

    def tt(self, out, a, b, op, eng="dve"):
        o, x, y = out.ap, a.ap, b.ap
        self.op(eng, lambda e: e.tensor_tensor(out=o, in0=x, in1=y, op=op), reads=[a, b], writes=[out])

    def ts(self, out, a, s1, op0, s2=None, op1=None, eng="dve", accum=None):
        o, x = out.ap, a.ap
        reads = [a]
        for s in (s1, s2):
            if isinstance(s, V):
                reads.append(s)
        writes = [out]
        kw = {}
        if op1 is not None:
            kw["op1"] = op1
        if accum is not None:
            kw["accum_out"] = accum.ap
            writes.append(accum)
        a1, a2 = _ap(s1), _ap(s2)
        self.op(eng, lambda e: e.tensor_scalar(out=o, in0=x, scalar1=a1, scalar2=a2, op0=op0, **kw),
                reads=reads, writes=writes)

    def stt(self, out, in0, scalar, in1, op0, op1):
        o, x, y = out.ap, in0.ap, in1.ap
        reads = [in0, in1]
        if isinstance(scalar, V):
            reads.append(scalar)
        sc = _ap(scalar)
        self.op("dve", lambda e: e.scalar_tensor_tensor(out=o, in0=x, scalar=sc, in1=y, op0=op0, op1=op1),
                reads=reads, writes=[out])

    def copy(self, out, in_, eng="dve"):
        o, i = out.ap, in_.ap
        if eng == "act":
            self.op("act", lambda e: e.copy(out=o, in_=i), reads=[in_], writes=[out])
        else:
            self.op(eng, lambda e: e.tensor_copy(out=o, in_=i), reads=[in_], writes=[out])

    def memset(self, out, val, eng="pool"):
        o = out.ap
        self.op(eng, lambda e: e.memset(o, val), reads=[], writes=[out])

    def recip(self, out, in_):
        o, i = out.ap, in_.ap
        self.op("dve", lambda e: e.reciprocal(out=o, in_=i), reads=[in_], writes=[out])

    def scan(self, out, d0, d1, init, op0, op1):
        o, a, b = out.ap, d0.ap, d1.ap
        self.op("dve", lambda e: e.tensor_tensor_scan(out=o, data0=a, data1=b, initial=init, op0=op0, op1=op1),
                reads=[d0, d1], writes=[out])

    def affsel(self, out, in_, pattern, cmp, fill, base, cm):
        o, i = out.ap, in_.ap
        self.op("pool", lambda e: e.affine_select(out=o, in_=i, pattern=pattern, compare_op=cmp, fill=fill,
                                                  base=base, channel_multiplier=cm), reads=[in_], writes=[out])

    def pv(self, pst, name, c0, n, parts=128, p0=0):
        assert (c0 // 512) == ((c0 + n - 1) // 512), (name, c0, n)
        return V("psbank%d" % (c0 // 512), pst[p0:p0 + parts, c0:c0 + n])


D_MODEL = 2048
D_FF = 8192
IN_TOTAL = 6936
EPS = 1e-6


def tok_blocks(T, bs=128):
    return [(t0, min(bs, T - t0)) for t0 in range(0, T, bs)]


class Consts:
    pass


def build_consts(P, es):
    c = Consts()
    c.ident_r = P.sb(es, [128, 128], F32R, "identr")
    c.ident_f = P.sb(es, [128, 128], F32, "identf")
    c.ones_f = P.sb(es, [128, 128], F32, "onesf")
    c.zeros_f = P.sb(es, [128, 128], F32, "zerosf")
    c.mU_incl = P.sb(es, [64, 64], F32, "mUi")
    c.mU_strict = P.sb(es, [64, 64], F32, "mUs")
    c.mL_strict = P.sb(es, [64, 64], F32, "mLs")
    c.nL_strict = P.sb(es, [64, 64], F32, "nLs")
    c.nU_incl = P.sb(es, [64, 64], F32, "nUi")
    c.eps_col = P.sb(es, [128, 1], F32, "epsc")
    with P.stage("init"):
        P.memset(c.zeros_f, 0.0)
        P.memset(c.ones_f, 1.0)
        P.memset(c.eps_col, EPS)
        P.affsel(c.ident_f, c.zeros_f, [[-1, 128]], ALU.not_equal, 1.0, 0, 1)
        P.affsel(c.ident_r, c.zeros_f, [[-1, 128]], ALU.not_equal, 1.0, 0, 1)
        P.affsel(c.mU_incl, c.ones_f[0:64, 0:64], [[1, 64]], ALU.is_ge, 0.0, 0, -1)
        P.affsel(c.mU_strict, c.ones_f[0:64, 0:64], [[1, 64]], ALU.is_ge, 0.0, -1, -1)
        P.affsel(c.nU_incl, c.zeros_f[0:64, 0:64], [[1, 64]], ALU.is_ge, -1e30, 0, -1)
        P.affsel(c.mL_strict, c.ones_f[0:64, 0:64], [[-1, 64]], ALU.is_ge, 0.0, -1, 1)
        P.affsel(c.nL_strict, c.zeros_f[0:64, 0:64], [[-1, 64]], ALU.is_ge, -1e30, -1, 1)
    return c


def stage_norm(P, C, name, src, gain, T, resid=None, out_tok=None, outT=None, gain2=None):
    two = gain2 is not None
    with ExitStack() as es:
        gbc = P.sb(es, [128, D_MODEL], F32, "gbc")
        gbc2 = P.sb(es, [128, D_MODEL], F32, "gbc2") if two else None
        xt = [P.sb(es, [128, D_MODEL], F32, "xt") for _ in range(2)]
        rt = [P.sb(es, [128, D_MODEL], F32, "rt") for _ in range(2)] if resid is not None else None
        junk = P.sb(es, [128, D_MODEL], F32, "junk")
        first_r = (outT is not None) and not two and resid is None
        ut = [P.sb(es, [128, D_MODEL], F32R if first_r else F32, "ut") for _ in range(2)]
        vt = [P.sb(es, [128, D_MODEL], F32, "vt") for _ in range(2)] if (resid is not None) else None
        zt = [P.sb(es, [128, D_MODEL], F32R, "zt") for _ in range(2)] if two else None
        oT = [P.sb(es, [128, 16, 128], F32, "oT") for _ in range(2)] if outT is not None else None
        ss = [P.sb(es, [128, 1], F32, "ss") for _ in range(4)]
        sd = [P.sb(es, [128, 1], F32, "sd") for _ in range(4)]
        rs = [P.sb(es, [128, 1], F32, "rs") for _ in range(4)]
        pst = P.psum(es)
        with P.stage(name):
            P.dma(gbc, gain.pbcast(128))
            if two:
                P.dma(gbc2, gain2.pbcast(128), q="act")
            for bi, (t0, nt) in enumerate(tok_blocks(T)):
                b = bi % 2
                P.dma(xt[b][:nt], src[t0:t0 + nt, :])
                if resid is not None:
                    P.dma(rt[b][:nt], resid[t0:t0 + nt, :], q="act")
                P.act(junk[:nt], xt[b][:nt], AF.Square, accum=ss[b][:nt])
                P.act(sd[b][:nt], ss[b][:nt], AF.Sqrt, bias=C.eps_col[:nt], scale=1.0 / D_MODEL)
                P.recip(rs[b][:nt], sd[b][:nt])
                P.stt(ut[b][:nt], xt[b][:nt], rs[b][:nt, 0:1], gbc[:nt], ALU.mult, ALU.mult)
                res = ut[b]
                if resid is not None:
                    P.tt(vt[b][:nt], ut[b][:nt], rt[b][:nt], ALU.add, eng="pool")
                    res = vt[b]
                if out_tok is not None:
                    P.dma(out_tok[t0:t0 + nt, :], res[:nt], q="act")
                if two:
                    P.act(junk[:nt], res[:nt], AF.Square, accum=ss[2 + b][:nt])
                    P.act(sd[2 + b][:nt], ss[2 + b][:nt], AF.Sqrt, bias=C.eps_col[:nt], scale=1.0 / D_MODEL)
                    P.recip(rs[2 + b][:nt], sd[2 + b][:nt])
                    P.stt(zt[b][:nt], res[:nt], rs[2 + b][:nt, 0:1], gbc2[:nt], ALU.mult, ALU.mult)
                    res = zt[b]
                if outT is not None:
                    for c in range(16):
                        pv = P.pv(pst, "pb", (b * 4 + c // 4) * 512 + (c % 4) * 128, nt)
                        P.transpose(pv.bitcast(F32R), res[:nt, c * 128:(c + 1) * 128], C.ident_r[:nt, :nt])
                    for g in range(4):
                        pv = P.pv(pst, "pb", (b * 4 + g) * 512, 512)
                        src_v = pv.rearrange("p (c n) -> p c n", c=4)[:, :, :nt]
                        P.copy(oT[b][:, g * 4:(g + 1) * 4, :nt], src_v, eng=("act" if g % 2 else "dve"))
                    P.dma(outT.rearrange("(c p) t -> p c t", p=128)[:, :, t0:t0 + nt], oT[b][:, :, :nt])


def stage_gemm_A(P, C, name, xT, K, T, groups, epi="copy"):
    KC = K // 128
    TT = [(t0, min(512, T - t0)) for t0 in range(0, T, 512)]
    with ExitStack() as es:
        xr = P.sb(es, [128, KC, T], F32R, "xr")
        wt = [P.sb(es, [128, KC, 256], F32R, "wt") for _ in range(2)]
        ot = [P.sb(es, [128, T], F32, "ot") for _ in range(2)]
        tmp = P.sb(es, [128, 512], F32, "tmp") if epi == "relu2" else None
        pst = P.psum(es)
        with P.stage(name):
            xTv = xT.bitcast(F32R).rearrange("(c p) t -> p c t", p=128)
            for kc in range(KC):
                P.dma(xr[:, kc, :], xTv[:, kc, :], q=("sp" if kc % 2 else "act"))
            bank = 0
            oi = 0
            for gi, (w_ap, chunks) in enumerate(groups):
                ncols = w_ap.shape[1]
                wb = wt[gi % 2]
                P.dma(wb[:, :, :ncols], w_ap.bitcast(F32R).rearrange("(c p) n -> p c n", p=128))
                for (c0, m, orows) in chunks:
                    ob = ot[oi % 2]
                    oi += 1
                    for ti, (t0, n) in enumerate(TT):
                        pv = P.pv(pst, "bank%d" % bank, bank * 512, n, parts=m)
                        bank = (bank + 1) % 8
                        for kc in range(KC):
                            P.mm(pv, wb[:, kc, c0:c0 + m], xr[:, kc, t0:t0 + n], start=(kc == 0), stop=(kc == KC - 1))
                        dst = ob[:m, t0:t0 + n]
                        if epi == "copy":
                            if ti % 2:
                                P.act(dst, pv, AF.Copy)
                            else:
                                P.copy(dst, pv)
                        elif epi == "sigmoid":
                            P.act(dst, pv, AF.Sigmoid)
                        elif epi == "relu2":
                            P.act(tmp[:m, :n], pv, AF.Relu)
                            P.tt(dst, tmp[:m, :n], tmp[:m, :n], ALU.mult)
                    P.dma(orows, ob[:m, :], q="act")


def stage_gemm_B(P, C, name, xT, K, T, W, N, NT, out_tok):
    KC = K // 128
    with ExitStack() as es:
        wr = P.sb(es, [128, KC, NT], F32R, "wr")
        xb = [P.sb(es, [128, KC, 128], F32R, "xb") for _ in range(2)]
        ot = [P.sb(es, [128, NT], F32, "ot") for _ in range(2)]
        pst = P.psum(es)
        with P.stage(name):
            xTv = xT.bitcast(F32R).rearrange("(c p) t -> p c t", p=128)
            Wv = W.bitcast(F32R).rearrange("(c p) n -> p c n", p=128)
            bank = 0
            it = 0
            for n0 in range(0, N, NT):
                for k0 in range(0, KC, 8):
                    P.dma(wr[:, k0:k0 + 8, :], Wv[:, k0:k0 + 8, n0:n0 + NT], q=("sp" if (k0 // 8) % 2 else "act"))
                for (t0, nt) in tok_blocks(T):
                    b = it % 2
                    it += 1
                    P.dma(xb[b][:, :, :nt], xTv[:, :, t0:t0 + nt])
                    for j in range(NT // 512):
                        pv = P.pv(pst, "bank%d" % bank, bank * 512, 512, parts=nt)
                        bank = (bank + 1) % 8
                        for kc in range(KC):
                            P.mm(pv, xb[b][:, kc, :nt], wr[:, kc, j * 512:(j + 1) * 512], start=(kc == 0), stop=(kc == KC - 1))
                        if j % 2:
                            P.act(ot[b][:nt, j * 512:(j + 1) * 512], pv, AF.Copy)
                        else:
                            P.copy(ot[b][:nt, j * 512:(j + 1) * 512], pv)
                    P.dma(out_tok[t0:t0 + nt, n0:n0 + NT], ot[b][:nt, :], q="act")


def _v_unsq(self, d):
    return V(self.key, self.ap.unsqueeze(d))


V.unsq = _v_unsq


def bc3(par, n):
    p, h = par.shape
    return par.unsq(2).bcast([p, h, n])


def tri_inverse_gen(P, C, pst, nm, X, XT, tiles):
    zt = tiles["z"][0]
    P.tt(zt, XT, C.ident_f[0:64, 0:64], ALU.add)
    cur = 0
    x, xt = X, XT
    for k in range(5):
        pp = P.pv(pst, nm + "_sq", tiles["pcol"], 128, parts=64)
        P.mm(pp[:, 0:64], xt, x)
        P.mm(pp[:, 64:128], x, xt)
        yield
        pair = tiles["pair"][k % 2]
        P.copy(pair, pp, eng="act")
        x, xt = pair[:, 0:64], pair[:, 64:128]
        yield
        pz = P.pv(pst, nm + "_z", tiles["pcol"] + 128, 64, parts=64)
        P.mm(pz, x, zt)
        yield
        znew = tiles["z"][1 - cur]
        P.tt(znew, pz, zt, ALU.add)
        zt = znew
        cur = 1 - cur
        yield
    return zt


def lockstep(gens):
    gens = list(gens)
    while gens:
        alive = []
        for g in gens:
            try:
                next(g)
                alive.append(g)
            except StopIteration:
                pass
        gens = alive


def stage_rwkv(P, C, name, l, pT, ymixT, seqs, prm, st_in, st_out):
    NT = 128
    with ExitStack() as es:
        def t3(nm, h=8, n=NT):
            return P.sb(es, [64, h, n], F32, nm)
        praw = P.sb(es, [64, 24, NT + 1], F32, "praw")
        pw = P.sb(es, [64, NT + 1], F32, "pw")
        pa = P.sb(es, [64, NT + 1], F32, "pa")
        pg = P.sb(es, [128, NT + 1], F32, "pg")
        xs = P.sb(es, [64, 24, NT], F32, "xs")
        dtl = P.sb(es, [64, 24, NT], F32, "dtl")
        xw = P.sb(es, [64, NT], F32, "xw")
        xa = P.sb(es, [64, NT], F32, "xa")
        xg = P.sb(es, [128, NT], F32, "xg")
        tmpg = P.sb(es, [128, NT], F32, "tmpg")
        a_t, g_t, lw, cum = t3("a"), t3("g"), t3("lw"), t3("cum")
        G, Gm1, Ginv, Edec = t3("G"), t3("Gm1"), t3("Ginv"), t3("Edec")
        kk, kh, b_t, bon = t3("kk"), t3("kh"), t3("b"), t3("bon")
        KKdec, Rdec, Kinv, Binv, Kdec, nBdec = t3("KKdec"), t3("Rdec"), t3("Kinv"), t3("Binv"), t3("Kdec"), t3("nBdec")
        yraw, tmp1, tmp2 = t3("yraw"), t3("tmp1"), t3("tmp2")
        mask0 = P.sb(es, [64, 8 * NT], F32, "mask0")
        mu_rkv = P.sb(es, [64, 24], F32, "mu_rkv")
        mu_w = P.sb(es, [64, 1], F32, "mu_w")
        mu_a = P.sb(es, [64, 1], F32, "mu_a")
        mu_g = P.sb(es, [128, 1], F32, "mu_g")
        pr = {k: P.sb(es, [64, 8], F32, k) for k in ["w0", "a0", "k_k", "k_a", "r_k", "ln_w", "ln_b", "omka"]}
        w_up = P.sb(es, [64, 512], F32, "w_up")
        a_up = P.sb(es, [64, 512], F32, "a_up")
        g_up = P.sb(es, [128, 512], F32, "g_up")
        M5 = P.sb(es, [64, 320], F32, "M5")
        gneps = P.sb(es, [64, 1], F32, "gneps")
        H = [[P.sb(es, [64, 64], F32, "H%d" % h) for _ in range(2)] for h in range(8)]
        S5 = [P.sb(es, [64, 320], F32, "S5") for _ in range(8)]
        tk = [P.sb(es, [64, 192], F32, "tk") for _ in range(8)]
        W1s = [P.sb(es, [64, 64], F32, "W1s") for _ in range(8)]
        Us = [P.sb(es, [64, 64], F32, "Us") for _ in range(8)]
        inv_tiles = [{"pair": [P.sb(es, [64, 128], F32, "pair") for _ in range(2)],
                      "z": [P.sb(es, [64, 64], F32, "z") for _ in range(2)], "pcol": 0} for _ in range(8)]
        stg = P.sb(es, [64, 8, 64], F32, "stg")
        pst = P.psum(es)
        with P.stage(name):
            P.dma(mu_rkv, prm["a_mu"][0:1536].rearrange("(s d) -> d s", d=64))
            P.dma(mu_w, prm["a_mu"][1536:1600].rearrange("(d o) -> d o", o=1))
            P.dma(mu_a, prm["a_mu"][1600:1664].rearrange("(d o) -> d o", o=1))
            P.dma(mu_g, prm["a_mu"][1664:1792].rearrange("(d o) -> d o", o=1))
            for k_ in ["w0", "a0", "k_k", "k_a", "ln_w", "ln_b"]:
                P.dma(pr[k_], prm["a_" + k_].rearrange("(h d) -> d h", d=64), q="act")
            P.dma(pr["r_k"], prm["a_r_k"].rearrange("h d -> d h"), q="act")
            P.dma(w_up, prm["a_w_up"])
            P.dma(a_up, prm["a_a_up"])
            P.dma(g_up, prm["a_g_up"])
            P.ts(pr["omka"], pr["k_a"], -1.0, ALU.mult, 1.0, ALU.add)
            P.memset(gneps, 64e-5)
            P.memset(mask0, 1.0)
            P.memset(mask0.rearrange("p (c k) -> p c k", k=64)[:, :, 0:1], 0.0)
            P.copy(M5[:, 0:64], C.mU_strict, eng="pool")
            P.copy(M5[:, 64:128], C.mU_incl, eng="pool")
            P.ts(M5[:, 128:192], C.mU_strict, -1.0, ALU.mult, eng="pool")
            P.ts(M5[:, 192:256], C.mU_incl, -1.0, ALU.mult, eng="pool")
            P.ts(M5[:, 256:320], C.mL_strict, -1.0, ALU.mult, eng="pool")
            unit = 0
            for si, (ts0, tl, is_s) in enumerate(seqs):
                hcur = [0] * 8
                if is_s:
                    P.dma(stg, st_in[1].rearrange("h v k -> v h k"))
                    for h in range(8):
                        pp = P.pv(pst, "ptr", 7 * 512, 64, parts=64)
                        P.mm(pp, stg[:, h, :], C.ident_f[0:64, 0:64])
                        P.copy(H[h][0], pp)
                else:
                    for h in range(8):
                        P.memset(H[h][0], 0.0)
                for t0 in range(ts0, ts0 + tl, NT):
                    n = min(NT, ts0 + tl - t0)
                    nch = n // 64
                    first = (t0 == ts0)
                    lo = 1 if first else 0
                    P.dma(praw[:, :, lo:n + 1], pT[0:1536, t0 - 1 + lo:t0 + n].rearrange("(s d) t -> d s t", d=64))
                    P.dma(pw[:, lo:n + 1], pT[1536:1600, t0 - 1 + lo:t0 + n], q="act")
                    P.dma(pa[:, lo:n + 1], pT[1600:1664, t0 - 1 + lo:t0 + n], q="act")
                    P.dma(pg[:, lo:n + 1], pT[1664:1792, t0 - 1 + lo:t0 + n], q="act")
                    if first:
                        if is_s:
                            sh = st_in[0]
                            P.dma(praw[:, :, 0:1], sh[0:1536].rearrange("(s d o) -> d s o", d=64, o=1))
                            P.dma(pw[:, 0:1], sh[1536:1600].rearrange("(d o) -> d o", o=1))
                            P.dma(pa[:, 0:1], sh[1600:1664].rearrange("(d o) -> d o", o=1))
                            P.dma(pg[:, 0:1], sh[1664:1792].rearrange("(d o) -> d o", o=1))
                        else:
                            P.memset(praw[:, :, 0:1], 0.0)
                            P.memset(pw[:, 0:1], 0.0)
                            P.memset(pa[:, 0:1], 0.0)
                            P.memset(pg[:, 0:1], 0.0)
                    P.tt(dtl[:, :, :n], praw[:, :, 0:n], praw[:, :, 1:n + 1], ALU.subtract)
                    P.tt(dtl[:, :, :n], dtl[:, :, :n], bc3(mu_rkv, n), ALU.mult)
                    P.tt(xs[:, :, :n], dtl[:, :, :n], praw[:, :, 1:n + 1], ALU.add)
                    for (xx, pp_, mu_, np_) in ((xw, pw, mu_w, 64), (xa, pa, mu_a, 64), (xg, pg, mu_g, 128)):
                        P.tt(tmpg[:np_, :n], pp_[:, 0:n], pp_[:, 1:n + 1], ALU.subtract)
                        P.stt(xx[:, :n], tmpg[:np_, :n], mu_[:, 0:1], pp_[:, 1:n + 1], ALU.mult, ALU.add)
                    xr, xk, xv = xs[:, 0:8, :n], xs[:, 8:16, :n], xs[:, 16:24, :n]
                    P.act(xw[:, :n], xw[:, :n], AF.Tanh)
                    P.act(xg[:, :n], xg[:, :n], AF.Sigmoid)
                    for h in range(8):
                        p1 = P.pv(pst, "pw%d" % (h % 2), (h % 2) * 256, n, parts=64)
                        P.mm(p1, w_up[:, h * 64:(h + 1) * 64], xw[:, :n])
                        P.act(lw[:, h, :n], p1, AF.Sigmoid, bias=pr["w0"][:, h:h + 1])
                        p2 = P.pv(pst, "pa%d" % (h % 2), 512 + (h % 2) * 256, n, parts=64)
                        P.mm(p2, a_up[:, h * 64:(h + 1) * 64], xa[:, :n])
                        P.act(a_t[:, h, :n], p2, AF.Sigmoid, bias=pr["a0"][:, h:h + 1])
                        p3 = P.pv(pst, "pg%d" % (h % 2), 1024 + (h % 2) * 256, n, parts=64)
                        P.mm(p3, g_up[:, h * 64:(h + 1) * 64], xg[:, :n])
                        P.copy(g_t[:, h, :n], p3)
                    P.ts(lw[:, :, :n], lw[:, :, :n], -0.6065306597, ALU.mult)
                    if n == NT:
                        P.scan(cum.rearrange("p h n -> p (h n)"), mask0, lw.rearrange("p h n -> p (h n)"), 0.0, ALU.mult, ALU.add)
                    else:
                        for h in range(8):
                            P.scan(cum[:, h, :n], mask0[:, :n], lw[:, h, :n], 0.0, ALU.mult, ALU.add)
                    P.act(G[:, :, :n], cum[:, :, :n], AF.Exp)
                    P.act(Ginv[:, :, :n], cum[:, :, :n], AF.Exp, scale=-1.0)
                    P.tt(tmp1[:, :, :n], cum[:, :, :n], lw[:, :, :n], ALU.subtract)
                    P.act(Gm1[:, :, :n], tmp1[:, :, :n], AF.Exp)
                    for h in range(8):
                        cv = cum[:, h, :n].rearrange("p (c k) -> p c k", k=64)
                        P.tt(tmp1[:, h, :n].rearrange("p (c k) -> p c k", k=64), cv[:, :, 63:64].bcast([64, nch, 64]), cv, ALU.subtract)
                    P.act(Edec[:, :, :n], tmp1[:, :, :n], AF.Exp)
                    P.tt(kk[:, :, :n], xk, bc3(pr["k_k"], n), ALU.mult)
                    P.tt(tmp1[:, :, :n], kk[:, :, :n], kk[:, :, :n], ALU.mult)
                    for h in range(8):
                        p1 = P.pv(pst, "pw%d" % (h % 2), (h % 2) * 256, n, parts=64)
                        P.mm(p1, C.ones_f[0:64, 0:64], tmp1[:, h, :n])
                        P.act(tmp2[:, h, :n], p1, AF.Sqrt, bias=C.eps_col[0:64])
                    P.recip(tmp2[:, :, :n], tmp2[:, :, :n])
                    P.tt(kk[:, :, :n], kk[:, :, :n], tmp2[:, :, :n], ALU.mult)
                    P.tt(tmp1[:, :, :n], a_t[:, :, :n], bc3(pr["k_a"], n), ALU.mult)
                    P.tt(tmp1[:, :, :n], tmp1[:, :, :n], bc3(pr["omka"], n), ALU.add)
                    P.tt(kh[:, :, :n], xk, tmp1[:, :, :n], ALU.mult)
                    P.tt(b_t[:, :, :n], kk[:, :, :n], a_t[:, :, :n], ALU.mult)
                    P.tt(tmp1[:, :, :n], xr, kh[:, :, :n], ALU.mult)
                    P.tt(tmp1[:, :, :n], tmp1[:, :, :n], bc3(pr["r_k"], n), ALU.mult)
                    for h in range(8):
                        p1 = P.pv(pst, "pw%d" % (h % 2), (h % 2) * 256, n, parts=64)
                        P.mm(p1, C.ones_f[0:64, 0:64], tmp1[:, h, :n])
                        P.tt(bon[:, h, :n], p1, xs[:, 16 + h, :n], ALU.mult)
                    P.tt(KKdec[:, :, :n], kk[:, :, :n], Gm1[:, :, :n], ALU.mult)
                    P.tt(Rdec[:, :, :n], xr, G[:, :, :n], ALU.mult)
                    P.tt(Kinv[:, :, :n], kh[:, :, :n], Ginv[:, :, :n], ALU.mult)
                    P.tt(Binv[:, :, :n], b_t[:, :, :n], Ginv[:, :, :n], ALU.mult)
                    P.tt(Kdec[:, :, :n], kh[:, :, :n], Edec[:, :, :n], ALU.mult)
                    P.stt(nBdec[:, :, :n], b_t[:, :, :n], -1.0, Edec[:, :, :n], ALU.mult, ALU.mult)
                    def unit_gen(h, c):
                        cs = slice(c * 64, (c + 1) * 64)
                        import os as _os2
                        _L = int(_os2.environ.get('LAYOUT', '0'))
                        base = h * 512 if _L == 0 else 1536 + (h % 2) * 1024
                        b2 = base if _L == 0 else base + 512
                        ptk = P.pv(pst, "ptk", base, 192, parts=64)
                        P.mm(ptk[:, 0:64], xs[:, 16 + h, cs], C.ident_f[0:64, 0:64])
                        P.mm(ptk[:, 64:128], Kdec[:, h, cs], C.ident_f[0:64, 0:64])
                        P.mm(ptk[:, 128:192], nBdec[:, h, cs], C.ident_f[0:64, 0:64])
                        p5 = P.pv(pst, "p5", base + 192, 320, parts=64)
                        P.mm(p5[:, 0:64], Kinv[:, h, cs], KKdec[:, h, cs])
                        P.mm(p5[:, 64:128], Kinv[:, h, cs], Rdec[:, h, cs])
                        P.mm(p5[:, 128:192], Binv[:, h, cs], KKdec[:, h, cs])
                        P.mm(p5[:, 192:256], Binv[:, h, cs], Rdec[:, h, cs])
                        P.mm(p5[:, 256:320], KKdec[:, h, cs], Binv[:, h, cs])
                        yield
                        P.copy(tk[h], ptk, eng="act")
                        Vt, Kdt, nBdt = tk[h][:, 0:64], tk[h][:, 64:128], tk[h][:, 128:192]
                        s5 = S5[h]
                        P.tt(s5, p5, M5, ALU.mult)
                        AT, PT, XT, nQT, X = s5[:, 0:64], s5[:, 64:128], s5[:, 128:192], s5[:, 192:256], s5[:, 256:320]
                        yield
                        it = inv_tiles[h]
                        it["pcol"] = b2
                        ZT = yield from tri_inverse_gen(P, C, pst, "inv", X, XT, it)
                        Hh = H[h][hcur[h]]
                        pW = P.pv(pst, "pW", b2 + 192, 64, parts=64)
                        P.mm(pW, KKdec[:, h, cs], Hh, start=True, stop=False)
                        P.mm(pW, AT, Vt, start=False, stop=True)
                        yield
                        P.copy(W1s[h], pW, eng="act")
                        yield
                        pU = P.pv(pst, "pU", b2 + 256, 64, parts=64)
                        P.mm(pU, ZT, W1s[h])
                        yield
                        P.copy(Us[h], pU, eng="act")
                        yield
                        pY = P.pv(pst, "pY", b2 + 320, 64, parts=64)
                        P.mm(pY, Hh, Rdec[:, h, cs], start=True, stop=False)
                        P.mm(pY, Vt, PT, start=False, stop=False)
                        P.mm(pY, Us[h], nQT, start=False, stop=True)
                        pH = P.pv(pst, "pH", b2 + 384, 64, parts=64)
                        P.mm(pH, Kdt, Vt, start=True, stop=False)
                        P.mm(pH, nBdt, Us[h], start=False, stop=True)
                        yield
                        P.copy(yraw[:, h, cs], pY, eng="act")
                        Hn = H[h][1 - hcur[h]]
                        gc = G[:, h, c * 64 + 63:c * 64 + 64]
                        P.stt(Hn, Hh, gc, pH, ALU.mult, ALU.add)
                        hcur[h] = 1 - hcur[h]

                    import os as _os
                    _G = int(_os.environ.get("LOCKG", "8"))
                    for c in range(nch):
                        for h0 in range(0, 8, _G):
                            lockstep([unit_gen(h, c) for h in range(h0, h0 + _G)])
                    for h in range(8):
                        p1 = P.pv(pst, "pw%d" % (h % 2), (h % 2) * 256, n, parts=64)
                        P.mm(p1, C.ones_f[0:64, 0:64], yraw[:, h, :n])
                        P.stt(tmp1[:, h, :n], p1, -1.0 / 64, yraw[:, h, :n], ALU.mult, ALU.add)
                    P.tt(tmp2[:, :, :n], tmp1[:, :, :n], tmp1[:, :, :n], ALU.mult)
                    for h in range(8):
                        p1 = P.pv(pst, "pw%d" % (h % 2), (h % 2) * 256, n, parts=64)
                        P.mm(p1, C.ones_f[0:64, 0:64], tmp2[:, h, :n])
                        P.act(dtl[:, h, :n], p1, AF.Sqrt, bias=gneps, scale=1.0 / 64)
                    P.recip(dtl[:, 0:8, :n], dtl[:, 0:8, :n])
                    P.tt(tmp1[:, :, :n], tmp1[:, :, :n], dtl[:, 0:8, :n], ALU.mult)
                    P.tt(tmp1[:, :, :n], tmp1[:, :, :n], bc3(pr["ln_w"], n), ALU.mult)
                    P.tt(tmp1[:, :, :n], tmp1[:, :, :n], bc3(pr["ln_b"], n), ALU.add)
                    P.tt(tmp1[:, :, :n], tmp1[:, :, :n], bon[:, :, :n], ALU.add)
                    P.tt(tmp2[:, :, :n], tmp1[:, :, :n], g_t[:, :, :n], ALU.mult)
                    P.dma(ymixT[0:512, t0:t0 + n].rearrange("(h d) t -> d h t", d=64), tmp2[:, :, :n])
                sh_out, s_out = st_out[si]
                tl_last = ts0 + tl - 1
                P.dma(sh_out.rearrange("(f o) -> f o", o=1), pT[0:1792, tl_last:tl_last + 1])
                for h in range(8):
                    pp = P.pv(pst, "ptr", 7 * 512, 64, parts=64)
                    P.mm(pp, H[h][hcur[h]], C.ident_f[0:64, 0:64])
                    P.copy(stg[:, h, :], pp)
                P.dma(s_out.rearrange("h v k -> v h k"), stg)


def stage_gla(P, C, name, pT, ymixT, seqs, prm, st_in, st_out):
    NT = 256
    B0 = 1792
    with ExitStack() as es:
        q = P.sb(es, [64, 4, NT], F32, "q")
        k = P.sb(es, [64, 4, NT], F32, "k")
        v = P.sb(es, [128, 4, NT], F32, "v")
        al = P.sb(es, [16, NT], F32, "al")
        rg = P.sb(es, [128, 4, NT], F32, "rg")
        la = P.sb(es, [64, 4, NT], F32, "la")
        cum = P.sb(es, [64, 4, NT], F32, "cum")
        Eb = P.sb(es, [64, 4, NT], F32, "Eb")
        qe = P.sb(es, [64, 4, NT], F32, "qe")
        ke = P.sb(es, [64, 4, NT], F32, "ke")
        kd = P.sb(es, [64, 4, NT], F32, "kd")
        t64 = P.sb(es, [64, 4, NT], F32, "t64")
        oraw = P.sb(es, [128, 4, NT], F32, "oraw")
        t128 = P.sb(es, [128, 4, NT], F32, "t128")
        u128 = P.sb(es, [128, 4, NT], F32, "u128")
        mask0 = P.sb(es, [64, 4 * NT], F32, "mask0")
        aup = P.sb(es, [16, 256], F32, "aup")
        nbias = P.sb(es, [64, 4], F32, "nbias")
        bnorm = P.sb(es, [128, 4], F32, "bnorm")
        S = [[P.sb(es, [64, 128], F32, "S%d" % h) for _ in range(2)] for h in range(4)]
        attT = [P.sb(es, [64, 64], F32, "attT") for _ in range(4)]
        tkb = [P.sb(es, [64, 192], F32, "tkb") for _ in range(4)]
        sstage = P.sb(es, [64, 4, 128], F32, "sstage")
        pst = P.psum(es)
        with P.stage(name):
            P.dma(aup, prm["b_alpha_up"])
            P.dma(nbias, prm["b_alpha_bias"].rearrange("(h d) -> d h", d=64))
            P.dma(bnorm, prm["b_norm"].rearrange("(h v) -> v h", v=128))
            P.ts(nbias, nbias, -1.0, ALU.mult)
            P.memset(mask0, 1.0)
            P.memset(mask0.rearrange("p (c k) -> p c k", k=64)[:, :, 0:1], 0.0)
            unit = 0
            for si, (ts0, tl, is_s) in enumerate(seqs):
                scur = [0] * 4
                if is_s:
                    P.dma(sstage, st_in.rearrange("h d v -> d h v"))
                    for h in range(4):
                        P.copy(S[h][0], sstage[:, h, :])
                else:
                    for h in range(4):
                        P.memset(S[h][0], 0.0)
                for t0 in range(ts0, ts0 + tl, NT):
                    n = min(NT, ts0 + tl - t0)
                    nch = n // 64
                    tsl = slice(t0, t0 + n)
                    P.dma(q[:, :, :n], pT[B0:B0 + 256, tsl].rearrange("(h d) t -> d h t", d=64))
                    P.dma(k[:, :, :n], pT[B0 + 256:B0 + 512, tsl].rearrange("(h d) t -> d h t", d=64))
                    P.dma(v[:, :, :n], pT[B0 + 512:B0 + 1024, tsl].rearrange("(h d) t -> d h t", d=128), q="act")
                    P.dma(al[:, :n], pT[B0 + 1024:B0 + 1040, tsl], q="act")
                    P.dma(rg[:, :, :n], pT[B0 + 1040:B0 + 1552, tsl].rearrange("(h d) t -> d h t", d=128), q="act")
                    for h in range(4):
                        p1 = P.pv(pst, "x", (h % 2) * 512, n, parts=64)
                        P.mm(p1, aup[:, h * 64:(h + 1) * 64], al[:, :n])
                        P.act(t64[:, h, :n], p1, AF.Exp, bias=nbias[:, h:h + 1], scale=-1.0)
                    P.act(t64[:, :, :n], t64[:, :, :n], AF.Ln, bias=1.0)
                    P.ts(la[:, :, :n], t64[:, :, :n], -1.0 / 16.0, ALU.mult)
                    if n == NT:
                        P.scan(cum.rearrange("p h n -> p (h n)"), mask0, la.rearrange("p h n -> p (h n)"), 0.0, ALU.mult, ALU.add)
                    else:
                        for h in range(4):
                            P.scan(cum[:, h, :n], mask0[:, :n], la[:, h, :n], 0.0, ALU.mult, ALU.add)
                    P.act(Eb[:, :, :n], cum[:, :, :n], AF.Exp)
                    P.stt(qe[:, :, :n], q[:, :, :n], 0.125, Eb[:, :, :n], ALU.mult, ALU.mult)
                    P.act(t64[:, :, :n], cum[:, :, :n], AF.Exp, scale=-1.0)
                    P.tt(ke[:, :, :n], k[:, :, :n], t64[:, :, :n], ALU.mult)
                    for h in range(4):
                        cv = cum[:, h, :n].rearrange("p (c k) -> p c k", k=64)
                        P.tt(t64[:, h, :n].rearrange("p (c k) -> p c k", k=64), cv[:, :, 63:64].bcast([64, nch, 64]), cv, ALU.subtract)
                    P.act(t64[:, :, :n], t64[:, :, :n], AF.Exp)
                    P.tt(kd[:, :, :n], k[:, :, :n], t64[:, :, :n], ALU.mult)
                    def unit_gen(h, c):
                        cs = slice(c * 64, (c + 1) * 64)
                        base = (2 + h) * 512
                        pA = P.pv(pst, "pA", base, 64, parts=64)
                        P.mm(pA, ke[:, h, cs], qe[:, h, cs])
                        ptk = P.pv(pst, "ptk", base + 64, 192, parts=64)
                        P.mm(ptk[:, 0:128], v[:, h, cs], C.ident_f)
                        P.mm(ptk[:, 128:192], kd[:, h, cs], C.ident_f[0:64, 0:64])
                        yield
                        P.tt(attT[h], pA, C.mU_incl, ALU.mult)
                        yield
                        P.copy(tkb[h], ptk, eng="act")
                        yield
                        Sh = S[h][scur[h]]
                        pO = P.pv(pst, "pO", base + 256, 64, parts=128)
                        P.mm(pO, Sh, qe[:, h, cs], start=True, stop=False)
                        P.mm(pO, tkb[h][:, 0:128], attT[h], start=False, stop=True)
                        pS = P.pv(pst, "pS", base + 320, 128, parts=64)
                        P.mm(pS, tkb[h][:, 128:192], tkb[h][:, 0:128])
                        yield
                        P.copy(oraw[:, h, cs], pO, eng="act")
                        yield
                        Sn = S[h][1 - scur[h]]
                        P.stt(Sn, Sh, Eb[:, h, c * 64 + 63:c * 64 + 64], pS, ALU.mult, ALU.add)
                        scur[h] = 1 - scur[h]

                    for c in range(nch):
                        lockstep([unit_gen(h, c) for h in range(4)])
                    P.tt(t128[:, :, :n], oraw[:, :, :n], oraw[:, :, :n], ALU.mult)
                    for h in range(4):
                        p1 = P.pv(pst, "x", (h % 2) * 512, n, parts=128)
                        P.mm(p1, C.ones_f, t128[:, h, :n])
                        P.act(u128[:, h, :n], p1, AF.Sqrt, bias=C.eps_col, scale=1.0 / 128)
                    P.recip(u128[:, :, :n], u128[:, :, :n])
                    P.tt(t128[:, :, :n], oraw[:, :, :n], u128[:, :, :n], ALU.mult)
                    P.tt(t128[:, :, :n], t128[:, :, :n], bc3(bnorm, n), ALU.mult)
                    P.act(u128[:, :, :n], rg[:, :, :n], AF.Sigmoid)
                    P.tt(u128[:, :, :n], u128[:, :, :n], rg[:, :, :n], ALU.mult)
                    P.tt(t128[:, :, :n], t128[:, :, :n], u128[:, :, :n], ALU.mult)
                    P.dma(ymixT[512:1024, tsl].rearrange("(h d) t -> d h t", d=128), t128[:, :, :n])
                for h in range(4):
                    P.copy(sstage[:, h, :], S[h][scur[h]])
                P.dma(st_out[si].rearrange("h d v -> d h v"), sstage)


def stage_dn(P, C, name, pT, ymixT, seqs, prm, st_in, st_out):
    NT = 128
    B0 = 4880
    with ExitStack() as es:
        xc = P.sb(es, [128, 12, NT + 3], F32, "xc")
        acc = P.sb(es, [128, 12, NT], F32, "acc")
        tm = P.sb(es, [128, 12, NT], F32, "tm")
        gate = P.sb(es, [128, 4, NT], F32, "gate")
        bet = P.sb(es, [4, NT], F32, "bet")
        araw = P.sb(es, [4, NT], F32, "araw")
        gg = P.sb(es, [4, NT], F32, "gg")
        cum4 = P.sb(es, [4, NT], F32, "cum4")
        ncum4 = P.sb(es, [4, NT], F32, "ncum4")
        r4 = {k_: P.sb(es, [4, NT], F32, k_) for k_ in ["ecum", "bec", "edec"]}
        bc = {k_: P.sb(es, [128, 4, NT], F32, "bc_" + k_) for k_ in ["bet", "ecum", "bec", "edec"]}
        KbT = P.sb(es, [128, 4, NT], F32, "KbT")
        KbeT = P.sb(es, [128, 4, NT], F32, "KbeT")
        VbT = P.sb(es, [128, 4, NT], F32, "VbT")
        qdT = P.sb(es, [128, 4, NT], F32, "qdT")
        kdT = P.sb(es, [128, 4, NT], F32, "kdT")
        oraw = P.sb(es, [128, 4, NT], F32, "oraw")
        mask0 = P.sb(es, [4, NT], F32, "mask0")
        cw = P.sb(es, [128, 12, 4], F32, "cw")
        alog = P.sb(es, [4, 1], F32, "alog")
        dtb = P.sb(es, [4, 1], F32, "dtb")
        nea = P.sb(es, [4, 1], F32, "nea")
        dnorm = P.sb(es, [128, 1], F32, "dnorm")
        sel = P.sb(es, [4, 4, 128], F32, "sel")
        S = [[P.sb(es, [128, 128], F32, "S%d" % h) for _ in range(2)] for h in range(4)]
        dec = [{k_: P.sb(es, [64, 64], F32, k_) for k_ in ["mL", "mU", "decL", "decUi", "decUs", "X", "XT", "attT"]} for _ in range(4)]
        tk = [P.sb(es, [64, 384], F32, "tk") for _ in range(4)]
        nwT = [P.sb(es, [128, 64], F32, "nwT") for _ in range(4)]
        dl = [P.sb(es, [64, 128], F32, "dl") for _ in range(4)]
        inv_tiles = [{"pair": [P.sb(es, [64, 128], F32, "pair") for _ in range(2)],
                      "z": [P.sb(es, [64, 64], F32, "z") for _ in range(2)], "pcol": 0} for _ in range(4)]
        sstage = P.sb(es, [128, 4, 128], F32, "sstage")
        pst = P.psum(es)
        with P.stage(name):
            for i in range(4):
                P.dma(cw[:, :, i:i + 1], prm["d_conv_w"][i, :].rearrange("(s p o) -> p s o", p=128, o=1))
            P.dma(alog, prm["d_a_log"].rearrange("(h o) -> h o", o=1))
            P.dma(dtb, prm["d_dt_bias"].rearrange("(h o) -> h o", o=1))
            P.dma(dnorm, prm["d_norm"].rearrange("(d o) -> d o", o=1))
            P.act(nea, alog, AF.Exp)
            P.ts(nea, nea, -1.0, ALU.mult)
            P.memset(mask0, 1.0)
            P.memset(mask0.rearrange("p (c k) -> p c k", k=64)[:, :, 0:1], 0.0)
            P.affsel(sel, C.ones_f[0:4, :].unsq(1).bcast([4, 4, 128]), [[-1, 4], [0, 128]], ALU.is_equal, 0.0, 0, 1)
            unit = 0
            for si, (ts0, tl, is_s) in enumerate(seqs):
                scur = [0] * 4
                if is_s:
                    P.dma(sstage, st_in[1].rearrange("h k v -> k h v"))
                    for h in range(4):
                        P.copy(S[h][0], sstage[:, h, :])
                else:
                    for h in range(4):
                        P.memset(S[h][0], 0.0)
                for t0 in range(ts0, ts0 + tl, NT):
                    n = min(NT, ts0 + tl - t0)
                    nch = n // 64
                    tsl = slice(t0, t0 + n)
                    first = (t0 == ts0)
                    lo = 3 if first else 0
                    P.dma(xc[:, :, lo:n + 3], pT[B0:B0 + 1536, t0 - 3 + lo:t0 + n].rearrange("(s p) t -> p s t", p=128))
                    if first:
                        if is_s:
                            for r_ in range(3):
                                P.dma(xc[:, :, r_:r_ + 1], st_in[0][r_, :].rearrange("(s p o) -> p s o", p=128, o=1))
                        else:
                            P.memset(xc[:, :, 0:3], 0.0)
                    P.dma(bet[:, :n], pT[B0 + 1536:B0 + 1540, tsl], q="act")
                    P.dma(araw[:, :n], pT[B0 + 1540:B0 + 1544, tsl], q="act")
                    P.dma(gate[:, :, :n], pT[B0 + 1544:B0 + 2056, tsl].rearrange("(s p) t -> p s t", p=128), q="act")
                    P.tt(acc[:, :, :n], xc[:, :, 0:n], cw[:, :, 0:1].bcast([128, 12, n]), ALU.mult)
                    for i in range(1, 4):
                        P.tt(tm[:, :, :n], xc[:, :, i:i + n], cw[:, :, i:i + 1].bcast([128, 12, n]), ALU.mult)
                        P.tt(acc[:, :, :n], acc[:, :, :n], tm[:, :, :n], ALU.add)
                    P.act(tm[:, :, :n], acc[:, :, :n], AF.Sigmoid)
                    P.tt(acc[:, :, :n], acc[:, :, :n], tm[:, :, :n], ALU.mult)
                    P.tt(tm[:, 0:8, :n], acc[:, 0:8, :n], acc[:, 0:8, :n], ALU.mult)
                    for s_ in range(8):
                        p1 = P.pv(pst, "x", (s_ % 2) * 512, n)
                        P.mm(p1, C.ones_f, tm[:, s_, :n])
                        P.act(tm[:, s_, :n], p1, AF.Sqrt, bias=C.eps_col)
                    P.recip(tm[:, 0:8, :n], tm[:, 0:8, :n])
                    P.stt(acc[:, 0:4, :n], acc[:, 0:4, :n], 128 ** -0.5, tm[:, 0:4, :n], ALU.mult, ALU.mult)
                    P.tt(acc[:, 4:8, :n], acc[:, 4:8, :n], tm[:, 4:8, :n], ALU.mult)
                    P.act(bet[:, :n], bet[:, :n], AF.Sigmoid)
                    P.act(gg[:, :n], araw[:, :n], AF.Exp, bias=dtb[:, 0:1])
                    P.act(gg[:, :n], gg[:, :n], AF.Ln, bias=1.0)
                    P.ts(gg[:, :n], gg[:, :n], nea[:, 0:1], ALU.mult)
                    P.scan(cum4[:, :n], mask0[:, :n], gg[:, :n], 0.0, ALU.mult, ALU.add)
                    P.ts(ncum4[:, :n], cum4[:, :n], -1.0, ALU.mult)
                    P.act(r4["ecum"][:, :n], cum4[:, :n], AF.Exp)
                    P.tt(r4["bec"][:, :n], r4["ecum"][:, :n], bet[:, :n], ALU.mult)
                    cv = cum4[:, :n].rearrange("p (c k) -> p c k", k=64)
                    P.tt(r4["edec"][:, :n].rearrange("p (c k) -> p c k", k=64), cv[:, :, 63:64].bcast([4, nch, 64]), cv, ALU.subtract)
                    P.act(r4["edec"][:, :n], r4["edec"][:, :n], AF.Exp)
                    srcs = {"bet": bet, "ecum": r4["ecum"], "bec": r4["bec"], "edec": r4["edec"]}
                    bi = 0
                    for k_ in ["bet", "ecum", "bec", "edec"]:
                        for h in range(4):
                            p1 = P.pv(pst, "x", (bi % 2) * 512, n)
                            bi += 1
                            P.mm(p1, sel[:, h, :], srcs[k_][:, :n])
                            P.copy(bc[k_][:, h, :n], p1, eng=("act" if bi % 2 else "dve"))
                    Qn, Kn, Vn = acc[:, 0:4, :n], acc[:, 4:8, :n], acc[:, 8:12, :n]
                    P.tt(KbT[:, :, :n], Kn, bc["bet"][:, :, :n], ALU.mult)
                    P.tt(KbeT[:, :, :n], Kn, bc["bec"][:, :, :n], ALU.mult)
                    P.tt(VbT[:, :, :n], Vn, bc["bet"][:, :, :n], ALU.mult)
                    P.tt(qdT[:, :, :n], Qn, bc["ecum"][:, :, :n], ALU.mult)
                    P.tt(kdT[:, :, :n], Kn, bc["edec"][:, :, :n], ALU.mult)
                    def unit_gen(h, c):
                        cs = slice(c * 64, (c + 1) * 64)
                        base = h * 1024
                        d = dec[h]
                        pD = P.pv(pst, "pD", base, 64, parts=64)
                        P.mm(pD, cum4[:, cs], sel[:, h, 0:64], start=True, stop=False)
                        P.mm(pD, sel[:, h, 0:64], ncum4[:, cs], start=False, stop=True)
                        pN = P.pv(pst, "pN", base + 64, 192, parts=64)
                        P.mm(pN[:, 0:64], KbT[:, h, cs], acc[:, 4 + h, cs])
                        P.mm(pN[:, 64:128], acc[:, 4 + h, cs], KbT[:, h, cs])
                        P.mm(pN[:, 128:192], acc[:, 4 + h, cs], acc[:, h, cs])
                        ptk = P.pv(pst, "ptk", base + 512, 384, parts=64)
                        P.mm(ptk[:, 0:128], VbT[:, h, cs], C.ident_f)
                        P.mm(ptk[:, 128:256], KbeT[:, h, cs], C.ident_f)
                        P.mm(ptk[:, 256:384], kdT[:, h, cs], C.ident_f)
                        yield
                        P.tt(d["mL"], pD, C.nL_strict, ALU.add)
                        P.stt(d["mU"], pD, -1.0, C.nU_incl, ALU.mult, ALU.add)
                        P.copy(tk[h], ptk, eng="act")
                        Vb, Kbe, kdec = tk[h][:, 0:128], tk[h][:, 128:256], tk[h][:, 256:384]
                        yield
                        P.act(d["decL"], d["mL"], AF.Exp)
                        P.act(d["decUi"], d["mU"], AF.Exp)
                        yield
                        P.tt(d["decUs"], d["decUi"], C.mU_strict, ALU.mult, eng="pool")
                        P.stt(d["X"], pN[:, 0:64], -1.0, d["decL"], ALU.mult, ALU.mult)
                        P.tt(d["attT"], pN[:, 128:192], d["decUi"], ALU.mult)
                        yield
                        P.stt(d["XT"], pN[:, 64:128], -1.0, d["decUs"], ALU.mult, ALU.mult)
                        yield
                        it = inv_tiles[h]
                        it["pcol"] = base
                        ZT = yield from tri_inverse_gen(P, C, pst, "inv", d["X"], d["XT"], it)
                        pw = P.pv(pst, "pw", base + 192, 64, parts=128)
                        P.mm(pw, Kbe, ZT)
                        yield
                        P.ts(nwT[h], pw, -1.0, ALU.mult)
                        yield
                        Sh = S[h][scur[h]]
                        pdl = P.pv(pst, "pdl", base + 256, 128, parts=64)
                        P.mm(pdl, ZT, Vb, start=True, stop=False)
                        P.mm(pdl, nwT[h], Sh, start=False, stop=True)
                        yield
                        P.copy(dl[h], pdl, eng="act")
                        yield
                        pO = P.pv(pst, "pO", base + 384, 64, parts=128)
                        P.mm(pO, Sh, qdT[:, h, cs], start=True, stop=False)
                        P.mm(pO, dl[h], d["attT"], start=False, stop=True)
                        pS = P.pv(pst, "pS", base + 512 + 384, 128, parts=128)
                        P.mm(pS, kdec, dl[h])
                        yield
                        P.copy(oraw[:, h, cs], pO, eng="act")
                        Sn = S[h][1 - scur[h]]
                        P.stt(Sn, Sh, bc["ecum"][:, h, c * 64 + 63:c * 64 + 64], pS, ALU.mult, ALU.add)
                        scur[h] = 1 - scur[h]

                    for c in range(nch):
                        lockstep([unit_gen(h, c) for h in range(4)])
                    P.tt(tm[:, 0:4, :n], oraw[:, :, :n], oraw[:, :, :n], ALU.mult)
                    for h in range(4):
                        p1 = P.pv(pst, "x", (h % 2) * 512, n)
                        P.mm(p1, C.ones_f, tm[:, h, :n])
                        P.act(tm[:, 4 + h, :n], p1, AF.Sqrt, bias=C.eps_col, scale=1.0 / 128)
                    P.recip(tm[:, 4:8, :n], tm[:, 4:8, :n])
                    P.tt(tm[:, 0:4, :n], oraw[:, :, :n], tm[:, 4:8, :n], ALU.mult)
                    P.act(tm[:, 4:8, :n], gate[:, :, :n], AF.Sigmoid)
                    P.tt(tm[:, 4:8, :n], tm[:, 4:8, :n], gate[:, :, :n], ALU.mult)
                    P.stt(tm[:, 8:12, :n], tm[:, 0:4, :n], dnorm[:, 0:1], tm[:, 4:8, :n], ALU.mult, ALU.mult)
                    P.dma(ymixT[1536:2048, tsl].rearrange("(h d) t -> d h t", d=128), tm[:, 8:12, :n])
                cv_out, s_out = st_out[si]
                tlast = ts0 + tl - 1
                for r_ in range(3):
                    tt_ = tlast - 2 + r_
                    P.dma(cv_out[r_, :].rearrange("(f o) -> f o", o=1), pT[B0:B0 + 1536, tt_:tt_ + 1])
                for h in range(4):
                    P.copy(sstage[:, h, :], S[h][scur[h]])
                P.dma(s_out.rearrange("h k v -> k h v"), sstage)


def band_bias_index():
    a = np.arange(2)[:, None, None, None, None]
    jj = np.arange(64)[None, :, None, None, None]
    r = np.arange(5)[None, None, :, None, None]
    u = np.arange(2)[None, None, None, :, None]
    i = np.arange(64)[None, None, None, None, :]
    delta = u + 8 - 2 * r - a
    m = 128 + 64 * delta + i - jj
    idx = np.clip(m, 0, 256)
    return idx.reshape(128, 5, 128)


def stage_band(P, C, name, pT, ymixT, TP, TS, biasT, cache_k, cache_v, outs):
    CB = 3344
    NBP = TP // 128
    LK = max(TP, 640)
    with ExitStack() as es:
        kTh = P.sb(es, [64, 8, LK], F32, "kTh")
        Vtok = P.sb(es, [128, max(NBP, 5), 512], F32, "Vtok")
        qblk = [P.sb(es, [64, 8, 128], F32, "qblk") for _ in range(2)]
        bias = P.sb(es, [128, 8, 640], F32, "bias")
        E = [P.sb(es, [128, 5, 128], F32, "E") for _ in range(2)]
        yout = [P.sb(es, [128, 4, 128], F32, "yout") for _ in range(2)]
        rec = [P.sb(es, [128, 128], F32, "rec") for _ in range(2)]
        fblk = [P.sb(es, [128, 4, 128], F32, "fblk") for _ in range(2)]
        Ktok = [P.sb(es, [128, 512], F32, "Ktok") for _ in range(2)]
        kc = P.sb(es, [128, 4, 512], F32, "kc")
        pst = P.psum(es)
        bias4 = bias.rearrange("p h (r c) -> p h r c", r=5)
        with P.stage(name):
            P.dma(bias, biasT.rearrange("h p c -> p h c"))
            P.memset(bias4[64:128, :, 4, 0:64], -1e30)
            P.memset(bias4[0:64, :, 0, 64:128], -1e30)
            cnt = [0]

            def tok_major(dst, src_rows, t0, nt):
                b = cnt[0] % 2
                cnt[0] += 1
                P.dma(fblk[b][:, :, :nt], pT[src_rows:src_rows + 512, t0:t0 + nt].rearrange("(s p) t -> p s t", p=128), q="act")
                pp = P.pv(pst, "ptm", b * 512, 512, parts=nt)
                for s_ in range(4):
                    P.mm(pp[:, s_ * 128:(s_ + 1) * 128], fblk[b][:, s_, :nt], C.ident_f)
                P.copy(dst, pp, eng=("act" if b else "dve"))

            def qblock(qb, kb0, nq, nbk, tcol):
                yo = yout[cnt[0] % 2]
                cnt[0] += 1
                r_lo = max(0, -kb0)
                rs = [r for r in range(r_lo, 5) if kb0 + r < nbk]
                for h in range(8):
                    par = h % 2
                    m, hl = h // 2, h % 2
                    e = E[par]
                    groups = [[r for r in rs if r < 4], [r for r in rs if r == 4]]
                    for gi, grp in enumerate(groups):
                        if not grp:
                            continue
                        bank = par * 2 + gi
                        for r in grp:
                            kb = kb0 + r
                            pv = P.pv(pst, "pS", bank * 512 + (r % 4) * 128, nq)
                            P.mm(pv, kTh[:, h, kb * 128:(kb + 1) * 128], qb[:, h, :nq])
                        r0, r1 = grp[0], grp[-1] + 1
                        psv = P.pv(pst, "pS", bank * 512 + (r0 % 4) * 128, (r1 - r0) * 128).rearrange("p (r c) -> p r c", c=128)[:, :, :nq]
                        P.stt(e[:, r0:r1, :nq], psv, 0.125, bias4[:, h, r0:r1, :nq], ALU.mult, ALU.add)
                        P.act(e[:, r0:r1, :nq], e[:, r0:r1, :nq], AF.Exp)
                    pO = P.pv(pst, "pO", (4 + par) * 512, nq)
                    for ri, r in enumerate(rs):
                        P.mm(pO, Vtok[:, kb0 + r, m * 128:(m + 1) * 128], e[:, r, :nq], start=(ri == 0), stop=(ri == len(rs) - 1))
                    pDn = P.pv(pst, "pDn", (6 + par) * 512, nq)
                    for ri, r in enumerate(rs):
                        P.mm(pDn, C.ones_f, e[:, r, :nq], start=(ri == 0), stop=(ri == len(rs) - 1))
                    sl = slice(hl * 64, (hl + 1) * 64)
                    P.recip(rec[par][sl, :nq], pDn[sl, :nq])
                    P.tt(yo[sl, m, :nq], pO[sl, :nq], rec[par][sl, :nq], ALU.mult)
                P.dma(ymixT[1024:1536, tcol:tcol + nq].rearrange("(m p) t -> p m t", p=128), yo[:, :, :nq])

            P.dma(kTh[:, :, 0:TP], pT[CB + 512:CB + 1024, 0:TP].rearrange("(h d) t -> d h t", d=64))
            for tb in range(NBP):
                tok_major(Vtok[:, tb, :], CB + 1024, tb * 128, 128)
            keep = min(512, TP)
            nkb = keep // 128
            P.dma(outs["p_bv"].rearrange("(b p) f -> p b f", p=128), Vtok[:, NBP - nkb:NBP, :])
            for bi_ in range(nkb):
                tb = NBP - nkb + bi_
                kt = Ktok[bi_ % 2]
                tok_major(kt, CB + 512, tb * 128, 128)
                P.dma(outs["p_bk"][bi_ * 128:(bi_ + 1) * 128, :], kt)
            for n in range(NBP):
                qb = qblk[n % 2]
                P.dma(qb, pT[CB:CB + 512, n * 128:(n + 1) * 128].rearrange("(h d) t -> d h t", d=64))
                qblock(qb, n - 4, 128, NBP, n * 128)
            P.dma(Vtok[:, 0:4, :], cache_v.rearrange("(b p) f -> p b f", p=128))
            P.memset(Vtok[:, 4, :], 0.0)
            tok_major(Vtok[0:TS, 4, :], CB + 1024, TP, TS)
            P.dma(outs["s_bv"], Vtok[0:TS, 4, :])
            kt = Ktok[0]
            tok_major(kt[0:TS, :], CB + 512, TP, TS)
            P.dma(outs["s_bk"], kt[0:TS, :])
            P.dma(kc, cache_k.rearrange("(b p) f -> p b f", p=128))
            for b_ in range(4):
                for hg in range(2):
                    pp = P.pv(pst, "ptm", ((b_ * 2 + hg) % 2) * 512, 512, parts=64)
                    for hh in range(4):
                        h = hg * 4 + hh
                        P.mm(pp[:, hh * 128:(hh + 1) * 128], kc[:, b_, h * 64:(h + 1) * 64], C.ident_f)
                    P.copy(kTh[:, hg * 4:(hg + 1) * 4, b_ * 128:(b_ + 1) * 128], pp.rearrange("p (h t) -> p h t", h=4), eng=("act" if hg else "dve"))
            P.memset(kTh[:, :, 512 + TS:640], 0.0)
            P.dma(kTh[:, :, 512:512 + TS], pT[CB + 512:CB + 1024, TP:TP + TS].rearrange("(h d) t -> d h t", d=64))
            qb = qblk[0]
            P.dma(qb[:, :, 0:TS], pT[CB:CB + 512, TP:TP + TS].rearrange("(h d) t -> d h t", d=64))
            qblock(qb, 0, TS, 5, TP)


def stage_merge(P, C, name, uT, ymixT, T, wg, wb, mT):
    halves = []
    th = ((T // 2 + 31) // 32) * 32
    t0 = 0
    while t0 < T:
        halves.append((t0, min(th, T - t0)))
        t0 += th
    TH = max(n for _, n in halves)
    with ExitStack() as es:
        ur = P.sb(es, [128, 16, TH], F32R, "ur")
        yr = P.sb(es, [128, 16, TH], F32R, "yr")
        wgt = [P.sb(es, [128, 16, 256], F32R, "wgt") for _ in range(2)]
        wbt = [P.sb(es, [128, 4, 256], F32R, "wbt") for _ in range(2)]
        acc = [P.sb(es, [128, 2, TH], F32, "acc") for _ in range(2)]
        sg = [P.sb(es, [128, 512], F32, "sg") for _ in range(2)]
        pr = [P.sb(es, [128, 512], F32, "pr") for _ in range(2)]
        pst = P.psum(es)
        with P.stage(name):
            uTv = uT.bitcast(F32R).rearrange("(c p) t -> p c t", p=128)
            yTv = ymixT.bitcast(F32R).rearrange("(c p) t -> p c t", p=128)
            it = 0
            bank = 0
            for hi, (h0, hn) in enumerate(halves):
                for kc in range(16):
                    P.dma(ur[:, kc, :hn], uTv[:, kc, h0:h0 + hn], q=("sp" if kc % 2 else "act"))
                    P.dma(yr[:, kc, :hn], yTv[:, kc, h0:h0 + hn], q=("act" if kc % 2 else "sp"))
                TT = [(t0, min(512, hn - t0)) for t0 in range(0, hn, 512)]
                for jp in range(8):
                    ac = acc[jp % 2]
                    for b in range(4):
                        wi = it % 2
                        it += 1
                        P.dma(wgt[wi], wg[b, :, jp * 256:(jp + 1) * 256].bitcast(F32R).rearrange("(c p) n -> p c n", p=128))
                        P.dma(wbt[wi], wb[b, :, jp * 256:(jp + 1) * 256].bitcast(F32R).rearrange("(c p) n -> p c n", p=128), q="act")
                        for jj in range(2):
                            for ti, (t0, n) in enumerate(TT):
                                pG = P.pv(pst, "pG", bank * 512, n)
                                bank = (bank + 1) % 8
                                for kc in range(16):
                                    P.mm(pG, wgt[wi][:, kc, jj * 128:(jj + 1) * 128], ur[:, kc, t0:t0 + n], start=(kc == 0), stop=(kc == 15))
                                pB = P.pv(pst, "pB", bank * 512, n)
                                bank = (bank + 1) % 8
                                for kc in range(4):
                                    P.mm(pB, wbt[wi][:, kc, jj * 128:(jj + 1) * 128], yr[:, b * 4 + kc, t0:t0 + n], start=(kc == 0), stop=(kc == 3))
                                s_ = sg[ti % 2]
                                P.act(s_[:, :n], pG, AF.Sigmoid)
                                if b == 0:
                                    P.tt(ac[:, jj, t0:t0 + n], s_[:, :n], pB, ALU.mult)
                                else:
                                    p_ = pr[ti % 2]
                                    P.tt(p_[:, :n], s_[:, :n], pB, ALU.mult)
                                    P.tt(ac[:, jj, t0:t0 + n], ac[:, jj, t0:t0 + n], p_[:, :n], ALU.add, eng="pool")
                    P.dma(mT[jp * 256:(jp + 1) * 256, h0:h0 + hn].rearrange("(j p) t -> p j t", p=128), ac[:, :, :hn], q="act")


def build_program(TP, TS, DEPTH=2):
    T = TP + TS
    nc = bass.Bass("TRN2", target_bir_lowering=False)
    nc.dge_precook = False

    def dram(name, shape, kind="Internal"):
        return V(name, nc.dram_tensor(name, list(shape), F32, kind=kind).ap())
    I = {}
    ishapes = {
        "xin": [T, 2048], "sh_in": [DEPTH, 1792], "rw_in": [DEPTH, 8, 64, 64], "gla_in": [DEPTH, 4, 64, 128],
        "ck_in": [DEPTH, 512, 512], "cv_in": [DEPTH, 512, 512], "conv_in": [DEPTH, 3, 1536], "dn_in": [DEPTH, 4, 128, 128],
        "biasT": [DEPTH, 8, 128, 640],
        "norm_mix_pre": [DEPTH, 2048], "norm_mix_post": [DEPTH, 2048], "norm_ffn_pre": [DEPTH, 2048], "norm_ffn_post": [DEPTH, 2048],
        "w_in": [DEPTH, 2048, 6936], "w_merge_gate": [DEPTH, 4, 2048, 2048], "w_branch": [DEPTH, 4, 512, 2048],
        "w_out": [DEPTH, 2048, 2048], "w_ffn_up": [DEPTH, 2048, 8192], "w_ffn_down": [DEPTH, 8192, 2048],
        "a_mu": [DEPTH, 1792], "a_w0": [DEPTH, 512], "a_w_up": [DEPTH, 64, 512], "a_a0": [DEPTH, 512], "a_a_up": [DEPTH, 64, 512],
        "a_g_up": [DEPTH, 128, 512], "a_k_k": [DEPTH, 512], "a_k_a": [DEPTH, 512], "a_r_k": [DEPTH, 8, 64], "a_ln_w": [DEPTH, 512],
        "a_ln_b": [DEPTH, 512], "b_alpha_up": [DEPTH, 16, 256], "b_alpha_bias": [DEPTH, 256], "b_norm": [DEPTH, 512],
        "d_conv_w": [DEPTH, 4, 1536], "d_a_log": [DEPTH, 4], "d_dt_bias": [DEPTH, 4], "d_norm": [DEPTH, 128],
    }
    for k_, sh in ishapes.items():
        I[k_] = dram(k_, sh, "ExternalInput")
    keep = min(512, TP)
    oshapes = {
        "y": [T, 2048],
        "p_shift": [DEPTH, 1792], "p_rwkv": [DEPTH, 8, 64, 64], "p_gla": [DEPTH, 4, 64, 128], "p_bk": [DEPTH, keep, 512],
        "p_bv": [DEPTH, keep, 512], "p_conv": [DEPTH, 3, 1536], "p_dn": [DEPTH, 4, 128, 128],
        "s_shift": [DEPTH, 1792], "s_rwkv": [DEPTH, 8, 64, 64], "s_gla": [DEPTH, 4, 64, 128], "s_bk": [DEPTH, TS, 512],
        "s_bv": [DEPTH, TS, 512], "s_conv": [DEPTH, 3, 1536], "s_dn": [DEPTH, 4, 128, 128],
    }
    O = {k_: dram(k_, sh, "ExternalOutput") for k_, sh in oshapes.items()}
    uT = dram("uT", [2048, T])
    pT = dram("pT", [IN_TOTAL, T])
    ymixT = dram("ymixT", [2048, T])
    mT = dram("mT", [2048, T])
    mo = dram("mo", [T, 2048])
    hres = dram("hres", [T, 2048])
    xres = dram("xres", [T, 2048])
    hidT = dram("hidT", [D_FF, T])
    seqs = [(0, TP, False), (TP, TS, True)]
    with ExitStack() as es:
        P = Prog(nc, es)
        C = build_consts(P, es)
        xcur = I["xin"]
        for l in range(DEPTH):
            L = "L%d_" % l
            if l == 0:
                stage_norm(P, C, L + "n1", xcur, I["norm_mix_pre"][l], T, outT=uT)
            groups = []
            for g0 in range(0, IN_TOTAL, 256):
                gw = min(256, IN_TOTAL - g0)
                chunks = []
                for c0 in range(0, gw, 128):
                    m = min(128, gw - c0)
                    chunks.append((c0, m, pT[g0 + c0:g0 + c0 + m, :]))
                groups.append((I["w_in"][l][:, g0:g0 + gw], chunks))
            stage_gemm_A(P, C, L + "inproj", uT, 2048, T, groups)
            prm = {k_: I[k_][l] for k_ in ishapes if k_[:2] in ("a_", "b_", "d_")}
            stage_rwkv2(P, C, L + "rwkv", l, pT, ymixT, seqs, prm, (I["sh_in"][l], I["rw_in"][l]),
                       {0: (O["p_shift"][l], O["p_rwkv"][l]), 1: (O["s_shift"][l], O["s_rwkv"][l])})
            stage_gla(P, C, L + "gla", pT, ymixT, seqs, prm, I["gla_in"][l], {0: O["p_gla"][l], 1: O["s_gla"][l]})
            stage_band(P, C, L + "band", pT, ymixT, TP, TS, I["biasT"][l], I["ck_in"][l], I["cv_in"][l],
                       {"p_bk": O["p_bk"][l], "p_bv": O["p_bv"][l], "s_bk": O["s_bk"][l], "s_bv": O["s_bv"][l]})
            stage_dn(P, C, L + "dn", pT, ymixT, seqs, prm, (I["conv_in"][l], I["dn_in"][l]),
                     {0: (O["p_conv"][l], O["p_dn"][l]), 1: (O["s_conv"][l], O["s_dn"][l])})
            stage_merge(P, C, L + "merge", uT, ymixT, T, I["w_merge_gate"][l], I["w_branch"][l], mT)
            stage_gemm_B(P, C, L + "wout", mT, 2048, T, I["w_out"][l], 2048, 2048, mo)
            stage_norm(P, C, L + "n23", mo, I["norm_mix_post"][l], T, resid=xcur, out_tok=hres, outT=uT,
                       gain2=I["norm_ffn_pre"][l])
            groups = []
            for g0 in range(0, D_FF, 256):
                groups.append((I["w_ffn_up"][l][:, g0:g0 + 256],
                               [(0, 128, hidT[g0:g0 + 128, :]), (128, 128, hidT[g0 + 128:g0 + 256, :])]))
            stage_gemm_A(P, C, L + "ffnup", uT, 2048, T, groups, epi="relu2")
            stage_gemm_B(P, C, L + "ffndn", hidT, D_FF, T, I["w_ffn_down"][l], 2048, 512, mo)
            xnext = O["y"] if l == DEPTH - 1 else xres
            if l == DEPTH - 1:
                stage_norm(P, C, L + "n4", mo, I["norm_ffn_post"][l], T, resid=hres, out_tok=xnext)
            else:
                stage_norm(P, C, L + "n41", mo, I["norm_ffn_post"][l], T, resid=hres, out_tok=xnext, outT=uT,
                           gain2=I["norm_mix_pre"][l + 1])
            xcur = xnext
        n_instr = P.n_instr
    return nc, list(ishapes.keys()), list(oshapes.keys()), n_instr


_W_KEYS = ["norm_mix_pre", "norm_mix_post", "norm_ffn_pre", "norm_ffn_post", "w_in", "w_merge_gate", "w_branch", "w_out",
           "w_ffn_up", "w_ffn_down", "a_mu", "a_w0", "a_w_up", "a_a0", "a_a_up", "a_g_up", "a_k_k", "a_k_a", "a_r_k",
           "a_ln_w", "a_ln_b", "b_alpha_up", "b_alpha_bias", "b_norm", "d_conv_w", "d_a_log", "d_dt_bias", "d_norm"]


def make_in_maps(inputs, n_cores, DEPTH):
    f = lambda a: np.ascontiguousarray(np.asarray(a), dtype=np.float32)
    idx = band_bias_index()
    crb = np.asarray(inputs["c_rel_bias"], dtype=np.float32)
    biasT = np.ascontiguousarray(crb[:, :, idx].reshape(DEPTH, 8, 128, 640))
    shared = {k_: f(inputs[k_]) for k_ in _W_KEYS}
    shared["biasT"] = biasT
    maps = []
    for b in range(n_cores):
        m = dict(shared)
        m["xin"] = f(np.concatenate([np.asarray(inputs["x_prompt"][b]), np.asarray(inputs["x_sample"][b])], axis=0))
        m["sh_in"] = f(np.asarray(inputs["state_rwkv_shift"])[:, b, 0])
        m["rw_in"] = f(np.asarray(inputs["state_rwkv"])[:, b])
        m["gla_in"] = f(np.asarray(inputs["state_gla"])[:, b])
        ck = np.asarray(inputs["cache_band_k"])[:, b]
        cv = np.asarray(inputs["cache_band_v"])[:, b]
        m["ck_in"] = f(ck.reshape(ck.shape[0], ck.shape[1], -1))
        m["cv_in"] = f(cv.reshape(cv.shape[0], cv.shape[1], -1))
        m["conv_in"] = f(np.asarray(inputs["state_dn_conv"])[:, b])
        m["dn_in"] = f(np.asarray(inputs["state_dn"])[:, b])
        maps.append(m)
    return maps


def assemble(results, TP, TS, DEPTH):
    st = lambda k_: np.stack([r[k_] for r in results], axis=1)
    B = len(results)
    y = np.stack([r["y"] for r in results], axis=0)
    out = [np.ascontiguousarray(y[:, :TP]), np.ascontiguousarray(y[:, TP:])]
    for pre in ("p_", "s_"):
        out.append(st(pre + "shift").reshape(DEPTH, B, 1, 1792))
        out.append(st(pre + "rwkv"))
        out.append(st(pre + "gla"))
        bk = st(pre + "bk")
        out.append(bk.reshape(DEPTH, B, bk.shape[2], 8, 64))
        bv = st(pre + "bv")
        out.append(bv.reshape(DEPTH, B, bv.shape[2], 8, 64))
        out.append(st(pre + "conv"))
        out.append(st(pre + "dn"))
    return tuple(np.ascontiguousarray(o, dtype=np.float32) for o in out)


def kernel(**inputs):
    B, TP, _ = np.asarray(inputs["x_prompt"]).shape
    TS = np.asarray(inputs["x_sample"]).shape[1]
    DEPTH = np.asarray(inputs["w_in"]).shape[0]
    nc, inames, onames, n_instr = build_program(TP, TS, DEPTH)
    maps = make_in_maps(inputs, B, DEPTH)
    res = run_bass_kernel_spmd(nc, maps, core_ids=list(range(B)))
    return assemble(res.results, TP, TS, DEPTH)


def run_all(gen):
    for _ in gen:
        pass


def stage_rwkv2(P, C, name, l, pT, ymixT, seqs, prm, st_in, st_out):
    NT = 64
    with ExitStack() as es:
        def mkset():
            d = {}
            d["praw"] = P.sb(es, [64, 24, NT + 1], F32, "praw")
            d["pw"] = P.sb(es, [64, NT + 1], F32, "pw")
            d["pa"] = P.sb(es, [64, NT + 1], F32, "pa")
            d["pg"] = P.sb(es, [128, NT + 1], F32, "pg")
            d["xs"] = P.sb(es, [64, 24, NT], F32, "xs")
            d["dtl"] = P.sb(es, [64, 24, NT], F32, "dtl")
            d["xw"] = P.sb(es, [64, NT], F32, "xw")
            d["xa"] = P.sb(es, [64, NT], F32, "xa")
            d["xg"] = P.sb(es, [128, NT], F32, "xg")
            d["tmpg"] = P.sb(es, [128, NT], F32, "tmpg")
            for k_ in ["a", "g", "lw", "cum", "G", "Gm1", "Ginv", "Edec", "kk", "kh", "b", "bon", "KKdec", "Rdec", "Kinv",
                       "Binv", "Kdec", "nBdec", "yraw", "tmp1", "tmp2"]:
                d[k_] = P.sb(es, [64, 8, NT], F32, k_)
            return d
        sets = [mkset(), mkset()]
        post_t = [{k_: P.sb(es, [64, 8, NT], F32, "post_" + k_) for k_ in ["q1", "q2", "q3"]} for _ in range(2)]
        mask0 = P.sb(es, [64, 8 * NT], F32, "mask0")
        mu_rkv = P.sb(es, [64, 24], F32, "mu_rkv")
        mu_w = P.sb(es, [64, 1], F32, "mu_w")
        mu_a = P.sb(es, [64, 1], F32, "mu_a")
        mu_g = P.sb(es, [128, 1], F32, "mu_g")
        pr = {k: P.sb(es, [64, 8], F32, k) for k in ["w0", "a0", "k_k", "k_a", "r_k", "ln_w", "ln_b", "omka"]}
        w_up = P.sb(es, [64, 512], F32, "w_up")
        a_up = P.sb(es, [64, 512], F32, "a_up")
        g_up = P.sb(es, [128, 512], F32, "g_up")
        M5 = P.sb(es, [64, 320], F32, "M5")
        gneps = P.sb(es, [64, 1], F32, "gneps")
        H = [[P.sb(es, [64, 64], F32, "H%d" % h) for _ in range(2)] for h in range(8)]
        S5 = [P.sb(es, [64, 320], F32, "S5") for _ in range(8)]
        tk = [P.sb(es, [64, 192], F32, "tk") for _ in range(8)]
        W1s = [P.sb(es, [64, 64], F32, "W1s") for _ in range(8)]
        Us = [P.sb(es, [64, 64], F32, "Us") for _ in range(8)]
        inv_tiles = [{"pair": [P.sb(es, [64, 128], F32, "pair") for _ in range(2)],
                      "z": [P.sb(es, [64, 64], F32, "z") for _ in range(2)], "pcol": 0} for _ in range(8)]
        stg = P.sb(es, [64, 8, 64], F32, "stg")
        pst = P.psum(es)
        n = NT
        hcur = [0] * 8
        GR = [(0, 3), (3, 6), (6, 8)]

        def frb(bank):
            return P.pv(pst, "fr", bank * 512, 512, parts=64)

        def flat(t):
            return t.rearrange("p h n -> p (h n)")

        def prep_gen(D, t0, first, is_s):
            praw, pw, pa, pg, xs, dtl = D["praw"], D["pw"], D["pa"], D["pg"], D["xs"], D["dtl"]
            xw, xa, xg, tmpg = D["xw"], D["xa"], D["xg"], D["tmpg"]
            a_t, g_t, lw, cum, G, Gm1, Ginv, Edec = D["a"], D["g"], D["lw"], D["cum"], D["G"], D["Gm1"], D["Ginv"], D["Edec"]
            kk, kh, b_t, bon = D["kk"], D["kh"], D["b"], D["bon"]
            KKdec, Rdec, Kinv, Binv, Kdec, nBdec = D["KKdec"], D["Rdec"], D["Kinv"], D["Binv"], D["Kdec"], D["nBdec"]
            tmp1, tmp2 = D["tmp1"], D["tmp2"]
            lo = 1 if first else 0
            P.dma(praw[:, :, lo:n + 1], pT[0:1536, t0 - 1 + lo:t0 + n].rearrange("(s d) t -> d s t", d=64))
            P.dma(pw[:, lo:n + 1], pT[1536:1600, t0 - 1 + lo:t0 + n], q="act")
            P.dma(pa[:, lo:n + 1], pT[1600:1664, t0 - 1 + lo:t0 + n], q="act")
            P.dma(pg[:, lo:n + 1], pT[1664:1792, t0 - 1 + lo:t0 + n], q="act")
            if first:
                if is_s:
                    sh = st_in[0]
                    P.dma(praw[:, :, 0:1], sh[0:1536].rearrange("(s d o) -> d s o", d=64, o=1))
                    P.dma(pw[:, 0:1], sh[1536:1600].rearrange("(d o) -> d o", o=1))
                    P.dma(pa[:, 0:1], sh[1600:1664].rearrange("(d o) -> d o", o=1))
                    P.dma(pg[:, 0:1], sh[1664:1792].rearrange("(d o) -> d o", o=1))
                else:
                    P.memset(praw[:, :, 0:1], 0.0)
                    P.memset(pw[:, 0:1], 0.0)
                    P.memset(pa[:, 0:1], 0.0)
                    P.memset(pg[:, 0:1], 0.0)
            yield
            P.tt(dtl, praw[:, :, 0:n], praw[:, :, 1:n + 1], ALU.subtract)
            for (xx, pp_, mu_, np_) in ((xw, pw, mu_w, 64), (xa, pa, mu_a, 64), (xg, pg, mu_g, 128)):
                P.tt(tmpg[:np_, :], pp_[:, 0:n], pp_[:, 1:n + 1], ALU.subtract, eng="pool")
                yield
                P.stt(xx, tmpg[:np_, :], mu_[:, 0:1], pp_[:, 1:n + 1], ALU.mult, ALU.add)
            yield
            P.tt(dtl, dtl, bc3(mu_rkv, n), ALU.mult)
            P.act(xw, xw, AF.Tanh)
            P.act(xg, xg, AF.Sigmoid)
            yield
            P.tt(xs, dtl, praw[:, :, 1:n + 1], ALU.add)
            xr, xk, xv = xs[:, 0:8, :], xs[:, 8:16, :], xs[:, 16:24, :]
            yield
            for kind, wmat, rhs_, bank in (("w", w_up, xw, 4), ("a", a_up, xa, 5), ("g", g_up, xg, 4)):
                pb = frb(bank)
                for h in range(8):
                    P.mm(pb[:, h * 64:(h + 1) * 64], wmat[:, h * 64:(h + 1) * 64], rhs_)
                pb3 = pb.rearrange("p (h t) -> p h t", h=8)
                if kind == "w":
                    P.tt(lw, pb3, bc3(pr["w0"], n), ALU.add)
                elif kind == "a":
                    P.tt(a_t, pb3, bc3(pr["a0"], n), ALU.add)
                else:
                    P.copy(g_t, pb3, eng="act")
                yield
            P.act(lw, lw, AF.Sigmoid)
            P.act(a_t, a_t, AF.Sigmoid)
            P.tt(kk, xk, bc3(pr["k_k"], n), ALU.mult)
            yield
            P.ts(lw, lw, -0.6065306597, ALU.mult)
            P.tt(tmp1, kk, kk, ALU.mult)
            yield
            P.scan(cum.rearrange("p h n -> p (h n)"), mask0, lw.rearrange("p h n -> p (h n)"), 0.0, ALU.mult, ALU.add)
            P.mm(frb(5), C.ones_f[0:64, 0:64], flat(tmp1))
            P.act(flat(tmp2), frb(5), AF.Sqrt, bias=C.eps_col[0:64])
            yield
            P.act(G, cum, AF.Exp)
            P.act(Ginv, cum, AF.Exp, scale=-1.0)
            P.tt(tmp1, cum, lw, ALU.subtract)
            yield
            P.act(Gm1, tmp1, AF.Exp)
            yield
            P.tt(tmp1, cum[:, :, 63:64].bcast([64, 8, 64]), cum, ALU.subtract)
            P.recip(tmp2, tmp2)
            yield
            P.act(Edec, tmp1, AF.Exp)
            P.tt(kk, kk, tmp2, ALU.mult)
            yield
            P.tt(tmp1, a_t, bc3(pr["k_a"], n), ALU.mult)
            yield
            P.tt(tmp1, tmp1, bc3(pr["omka"], n), ALU.add)
            yield
            P.tt(kh, xk, tmp1, ALU.mult)
            P.tt(b_t, kk, a_t, ALU.mult, eng="pool")
            yield
            P.tt(tmp1, xr, kh, ALU.mult)
            P.tt(KKdec, kk, Gm1, ALU.mult, eng="pool")
            yield
            P.tt(tmp1, tmp1, bc3(pr["r_k"], n), ALU.mult)
            P.tt(Rdec, xr, G, ALU.mult, eng="pool")
            yield
            P.mm(frb(4), C.ones_f[0:64, 0:64], flat(tmp1))
            P.tt(bon, frb(4).rearrange("p (h t) -> p h t", h=8), xv, ALU.mult)
            P.tt(Kinv, kh, Ginv, ALU.mult)
            P.tt(Binv, b_t, Ginv, ALU.mult, eng="pool")
            yield
            P.tt(Kdec, kh, Edec, ALU.mult, eng="pool")
            yield
            P.stt(nBdec, b_t, -1.0, Edec, ALU.mult, ALU.mult)

        def unit_gen(D, h):
            xs, G = D["xs"], D["G"]
            KKdec, Rdec, Kinv, Binv, Kdec, nBdec, yraw = D["KKdec"], D["Rdec"], D["Kinv"], D["Binv"], D["Kdec"], D["nBdec"], D["yraw"]
            base = (h % 4) * 512
            ptk = P.pv(pst, "ptk", base, 192, parts=64)
            P.mm(ptk[:, 0:64], xs[:, 16 + h, :], C.ident_f[0:64, 0:64])
            P.mm(ptk[:, 64:128], Kdec[:, h, :], C.ident_f[0:64, 0:64])
            P.mm(ptk[:, 128:192], nBdec[:, h, :], C.ident_f[0:64, 0:64])
            yield
            P.copy(tk[h], ptk, eng="act")
            Vt, Kdt, nBdt = tk[h][:, 0:64], tk[h][:, 64:128], tk[h][:, 128:192]
            yield
            p5 = P.pv(pst, "p5", base, 320, parts=64)
            P.mm(p5[:, 0:64], Kinv[:, h, :], KKdec[:, h, :])
            P.mm(p5[:, 64:128], Kinv[:, h, :], Rdec[:, h, :])
            P.mm(p5[:, 128:192], Binv[:, h, :], KKdec[:, h, :])
            P.mm(p5[:, 192:256], Binv[:, h, :], Rdec[:, h, :])
            P.mm(p5[:, 256:320], KKdec[:, h, :], Binv[:, h, :])
            yield
            s5 = S5[h]
            P.tt(s5, p5, M5, ALU.mult)
            AT, PT, XT, nQT, X = s5[:, 0:64], s5[:, 64:128], s5[:, 128:192], s5[:, 192:256], s5[:, 256:320]
            yield
            it = inv_tiles[h]
            it["pcol"] = base
            ZT = yield from tri_inverse_gen(P, C, pst, "inv", X, XT, it)
            Hh = H[h][hcur[h]]
            pW = P.pv(pst, "pW", base + 192, 64, parts=64)
            P.mm(pW, KKdec[:, h, :], Hh, start=True, stop=False)
            P.mm(pW, AT, Vt, start=False, stop=True)
            yield
            P.copy(W1s[h], pW, eng="act")
            yield
            pU = P.pv(pst, "pU", base + 256, 64, parts=64)
            P.mm(pU, ZT, W1s[h])
            yield
            P.copy(Us[h], pU, eng="act")
            yield
            pY = P.pv(pst, "pY", base + 192, 64, parts=64)
            P.mm(pY, Hh, Rdec[:, h, :], start=True, stop=False)
            P.mm(pY, Vt, PT, start=False, stop=False)
            P.mm(pY, Us[h], nQT, start=False, stop=True)
            pH = P.pv(pst, "pH", base + 256, 64, parts=64)
            P.mm(pH, Kdt, Vt, start=True, stop=False)
            P.mm(pH, nBdt, Us[h], start=False, stop=True)
            yield
            P.copy(yraw[:, h, :], pY, eng="act")
            yield
            Hn = H[h][1 - hcur[h]]
            P.stt(Hn, Hh, G[:, h, 63:64], pH, ALU.mult, ALU.add)
            hcur[h] = 1 - hcur[h]

        def post_gen(D, Q, t0):
            yraw, bon, g_t = D["yraw"], D["bon"], D["g"]
            q1, q2, q3 = Q["q1"], Q["q2"], Q["q3"]
            P.mm(frb(6), C.ones_f[0:64, 0:64], flat(yraw))
            P.stt(flat(q1), frb(6), -1.0 / 64, flat(yraw), ALU.mult, ALU.add)
            yield
            P.tt(q2, q1, q1, ALU.mult, eng="pool")
            yield
            P.mm(frb(7), C.ones_f[0:64, 0:64], flat(q2))
            P.act(flat(q3), frb(7), AF.Sqrt, bias=gneps, scale=1.0 / 64)
            yield
            P.recip(q3, q3)
            yield
            P.tt(q1, q1, q3, ALU.mult)
            yield
            P.tt(q1, q1, bc3(pr["ln_w"], n), ALU.mult)
            yield
            P.tt(q1, q1, bc3(pr["ln_b"], n), ALU.add)
            yield
            P.tt(q1, q1, bon, ALU.add)
            yield
            P.tt(q2, q1, g_t, ALU.mult)
            yield
            P.dma(ymixT[0:512, t0:t0 + n].rearrange("(h d) t -> d h t", d=64), q2)

        with P.stage(name):
            P.dma(mu_rkv, prm["a_mu"][0:1536].rearrange("(s d) -> d s", d=64))
            P.dma(mu_w, prm["a_mu"][1536:1600].rearrange("(d o) -> d o", o=1))
            P.dma(mu_a, prm["a_mu"][1600:1664].rearrange("(d o) -> d o", o=1))
            P.dma(mu_g, prm["a_mu"][1664:1792].rearrange("(d o) -> d o", o=1))
            for k_ in ["w0", "a0", "k_k", "k_a", "ln_w", "ln_b"]:
                P.dma(pr[k_], prm["a_" + k_].rearrange("(h d) -> d h", d=64), q="act")
            P.dma(pr["r_k"], prm["a_r_k"].rearrange("h d -> d h"), q="act")
            P.dma(w_up, prm["a_w_up"])
            P.dma(a_up, prm["a_a_up"])
            P.dma(g_up, prm["a_g_up"])
            P.ts(pr["omka"], pr["k_a"], -1.0, ALU.mult, 1.0, ALU.add)
            P.memset(gneps, 64e-5)
            P.memset(mask0, 1.0)
            P.memset(mask0.rearrange("p (c k) -> p c k", k=64)[:, :, 0:1], 0.0)
            P.copy(M5[:, 0:64], C.mU_strict, eng="pool")
            P.copy(M5[:, 64:128], C.mU_incl, eng="pool")
            P.ts(M5[:, 128:192], C.mU_strict, -1.0, ALU.mult, eng="pool")
            P.ts(M5[:, 192:256], C.mU_incl, -1.0, ALU.mult, eng="pool")
            P.ts(M5[:, 256:320], C.mL_strict, -1.0, ALU.mult, eng="pool")
            tiles = []
            for si, (ts0, tl, is_s) in enumerate(seqs):
                for t0 in range(ts0, ts0 + tl, NT):
                    tiles.append((si, t0, t0 == ts0, t0 + NT >= ts0 + tl, is_s))
            run_all(prep_gen(sets[0], tiles[0][1], tiles[0][2], tiles[0][4]))
            pending_post = None
            for ti, (si, t0, first, last, is_s) in enumerate(tiles):
                D = sets[ti % 2]
                if first:
                    if is_s:
                        P.dma(stg, st_in[1].rearrange("h v k -> v h k"))
                        for h in range(8):
                            pp = P.pv(pst, "ptr", (h % 4) * 512, 64, parts=64)
                            P.mm(pp, stg[:, h, :], C.ident_f[0:64, 0:64])
                            P.copy(H[h][hcur[h]], pp)
                    else:
                        for h in range(8):
                            P.memset(H[h][hcur[h]], 0.0)
                def slot_gen(j, D=D):
                    yield from unit_gen(D, j)
                    yield from unit_gen(D, j + 4)
                gens = [slot_gen(j) for j in range(4)]
                if pending_post is not None:
                    gens.append(pending_post)
                if ti + 1 < len(tiles):
                    nx = tiles[ti + 1]
                    gens.append(prep_gen(sets[(ti + 1) % 2], nx[1], nx[2], nx[4]))
                lockstep(gens)
                pending_post = post_gen(D, post_t[ti % 2], t0)
                if last:
                    sh_out, s_out = st_out[si]
                    tl_last = t0 + NT - 1
                    P.dma(sh_out.rearrange("(f o) -> f o", o=1), pT[0:1792, tl_last:tl_last + 1])
                    for h in range(8):
                        pp = P.pv(pst, "ptr", (h % 4) * 512, 64, parts=64)
                        P.mm(pp, H[h][hcur[h]], C.ident_f[0:64, 0:64])
                        P.copy(stg[:, h, :], pp)
                    P.dma(s_out.rearrange("h v k -> v h k"), stg)
            run_all(pending_post)
```

```python
import numpy as np
from contextlib import ExitStack, contextmanager
import concourse.bass as bass
import concourse.mybir as mybir
from concourse.bass_utils import run_bass_kernel_spmd

F32 = mybir.dt.float32
F32R = mybir.dt.float32r
AF = mybir.ActivationFunctionType
ALU = mybir.AluOpType
AX = mybir.AxisListType

ENG = ["pe", "act", "dve", "pool", "sp"]
NDMA = 40


class V:
    __slots__ = ("key", "ap")

    def __init__(self, key, ap):
        self.key = key
        self.ap = ap

    def __getitem__(self, idx):
        return V(self.key, self.ap[idx])

    def bitcast(self, dt):
        return V(self.key, self.ap.bitcast(dt))

    def rearrange(self, pat, **kw):
        return V(self.key, self.ap.rearrange(pat, **kw))

    def bcast(self, shape):
        return V(self.key, self.ap.broadcast_to(shape))

    def pbcast(self, n):
        return V(self.key, self.ap.partition_broadcast(n))

    @property
    def shape(self):
        return self.ap.shape


def _ap(x):
    return x.ap if isinstance(x, V) else x


class Prog:
    def __init__(self, nc, es):
        self.nc = nc
        self.sem = {e: es.enter_context(nc.semaphore("sem_" + e)) for e in ENG}
        self.dsem = [es.enter_context(nc.semaphore("dsem%d" % i)) for i in range(NDMA)]
        self.cnt = {e: 0 for e in ENG}
        self.known = {e: {f: 0 for f in ENG} for e in ENG}
        self.dval = [0] * NDMA
        self.dknown = {e: [0] * NDMA for e in ENG}
        self.dnext = 0
        self.ops = None
        self.lastw = {}
        self.readers = {}
        self.n_instr = 0
        self.uid = 0

    def eng_obj(self, e):
        nc = self.nc
        return {"pe": nc.tensor, "act": nc.scalar, "dve": nc.vector, "pool": nc.gpsimd, "sp": nc.sync}[e]

    def _deps(self, reads, writes):
        toks = []
        for r in reads:
            t = self.lastw.get(r)
            if t is not None:
                toks.append(t)
        for w in writes:
            t = self.lastw.get(w)
            if t is not None:
                toks.append(t)
            toks.extend(self.readers.get(w, ()))
        return toks

    def _waits(self, e, toks):
        waits = []
        need_e = {}
        need_d = {}
        for t in toks:
            if t[0] == "eng":
                _, f, c = t
                if f == e and e == "pe":
                    continue
                if self.known[e][f] < c:
                    need_e[f] = max(need_e.get(f, 0), c)
            else:
                _, s, v = t
                if self.dknown[e][s] < v:
                    need_d[s] = max(need_d.get(s, 0), v)
        for f, c in need_e.items():
            self.known[e][f] = c
            waits.append((self.sem[f], c))
        for s, v in need_d.items():
            self.dknown[e][s] = v
            waits.append((self.dsem[s], v))
        return waits

    def _commit(self, tok, reads, writes):
        for w in writes:
            self.lastw[w] = tok
            self.readers[w] = []
        for r in reads:
            if r in writes:
                continue
            self.readers.setdefault(r, []).append(tok)

    def op(self, e, fn, reads=(), writes=()):
        reads = [r.key if isinstance(r, V) else r for r in reads]
        writes = [w.key if isinstance(w, V) else w for w in writes]
        for r in reads:
            if r.startswith("psbank") and r not in writes:
                writes.append(r)
        waits = self._waits(e, self._deps(reads, writes))
        self.cnt[e] += 1
        tok = ("eng", e, self.cnt[e])
        self.ops[e].append((waits, fn, (self.sem[e], 1)))
        self._commit(tok, reads, writes)
        self.n_instr += 1 + len(waits)

    def dma(self, out, in_, q="sp"):
        reads = [in_.key]
        writes = [out.key]
        s = self.dnext
        self.dnext = (self.dnext + 1) % NDMA
        toks = self._deps(reads, writes)
        if self.dval[s] > 0:
            toks.append(("dma", s, self.dval[s]))
        waits = self._waits(q, toks)
        self.dval[s] += 16
        tok = ("dma", s, self.dval[s])
        oa, ia = out.ap, in_.ap
        self.ops[q].append((waits, lambda eng: eng.dma_start(out=oa, in_=ia), (self.dsem[s], 16)))
        self._commit(tok, reads, writes)
        self.n_instr += 1 + len(waits)

    @contextmanager
    def stage(self, name):
        self.ops = {e: [] for e in ENG}
        self.lastw = {}
        self.readers = {}
        pre = {e: [] for e in ENG}
        for e in ENG:
            for f in ENG:
                if f != e and self.known[e][f] < self.cnt[f]:
                    pre[e].append((self.sem[f], self.cnt[f]))
                    self.known[e][f] = self.cnt[f]
            for s in range(NDMA):
                if self.dknown[e][s] < self.dval[s]:
                    pre[e].append((self.dsem[s], self.dval[s]))
                    self.dknown[e][s] = self.dval[s]
            self.n_instr += len(pre[e])
        yield
        fin = []
        for s in range(NDMA):
            if self.dknown["sp"][s] < self.dval[s]:
                fin.append((self.dsem[s], self.dval[s]))
        self.dknown["sp"] = list(self.dval)
        ops = self.ops
        nc = self.nc
        with nc.allow_non_contiguous_dma(reason="small strided param/state loads"), nc.Block(name, no_gpsimd_drain=True) as block:
            def mk(e):
                def body(eng):
                    for (s, v) in pre[e]:
                        eng.wait_ge(s, v)
                    for waits, fn, (sem, inc) in ops[e]:
                        for (s, v) in waits:
                            eng.wait_ge(s, v)
                        fn(eng).then_inc(sem, inc)
                    if e == "sp":
                        for (s, v) in fin:
                            eng.wait_ge(s, v)
                return body
            block.tensor(mk("pe"))
            block.scalar(mk("act"))
            block.vector(mk("dve"))
            block.gpsimd(mk("pool"))
            block.sync(mk("sp"))
        self.ops = None

    def sb(self, es, shape, dt=F32, name=None):
        self.uid += 1
        name = name or "t"
        t = es.enter_context(self.nc.sbuf_tensor("%s_%d" % (name, self.uid), list(shape), dt))
        return V("%s_%d" % (name, self.uid), t[:] if hasattr(t, "__getitem__") else t.ap())

    def psum(self, es, ncols=4096):
        self.uid += 1
        t = es.enter_context(self.nc.psum_tensor("ps_%d" % self.uid, [128, ncols], F32))
        return t

    def mm(self, out, lhsT, rhs, start=True, stop=True):
        o, l, r = out.ap, lhsT.ap, rhs.ap
        self.op("pe", lambda e: e.matmul(o, lhsT=l, rhs=r, start=start, stop=stop),
                reads=[lhsT, rhs], writes=[out])

    def transpose(self, out, in_, ident):
        o, i, d = out.ap, in_.ap, ident.ap
        self.op("pe", lambda e: e.transpose(o, i, d), reads=[in_, ident], writes=[out])

    def act(self, out, in_, func, bias=None, scale=None, accum=None, eng="act"):
        o, i = out.ap, in_.ap
        kw = {}
        reads = [in_]
        if bias is not None:
            kw["bias"] = _ap(bias)
            if isinstance(bias, V):
                reads.append(bias)
        if scale is not None:
            kw["scale"] = _ap(scale)
            if isinstance(scale, V):
                reads.append(scale)
        writes = [out]
        if accum is not None:
            kw["accum_out"] = accum.ap
            writes.append(accum)
        self.op("act", lambda e: e.activation(out=o, in_=i, func=func, **kw), reads=reads, writes=writes)

    def tt(self, out, a, b, op, eng="dve"):
        o, x, y = out.ap, a.ap, b.ap
        self.op(eng, lambda e: e.tensor_tensor(out=o, in0=x, in1=y, op=op), reads=[a, b], writes=[out])

    def ts(self, out, a, s1, op0, s2=None, op1=None, eng="dve", accum=None):
        o, x = out.ap, a.ap
        reads = [a]
        for s in (s1, s2):
            if isinstance(s, V):
                reads.append(s)
        writes = [out]
        kw = {}
        if op1 is not None:
            kw["op1"] = op1
        if accum is not None:
            kw["accum_out"] = accum.ap
            writes.append(accum)
        a1, a2 = _ap(s1), _ap(s2)
        self.op(eng, lambda e: e.tensor_scalar(out=o, in0=x, scalar1=a1, scalar2=a2, op0=op0, **kw),
                reads=reads, writes=writes)

    def stt(self, out, in0, scalar, in1, op0, op1):
        o, x, y = out.ap, in0.ap, in1.ap
        reads = [in0, in1]
        if isinstance(scalar, V):
            reads.append(scalar)
        sc = _ap(scalar)
        self.op("dve", lambda e: e.scalar_tensor_tensor(out=o, in0=x, scalar=sc, in1=y, op0=op0, op1=op1),
                reads=reads, writes=[out])

    def copy(self, out, in_, eng="dve"):
        o, i = out.ap, in_.ap
        if eng == "act":
            self.op("act", lambda e: e.copy(out=o, in_=i), reads=[in_], writes=[out])
        else:
            self.op(eng, lambda e: e.tensor_copy(out=o, in_=i), reads=[in_], writes=[out])

    def memset(self, out, val, eng="pool"):
        o = out.ap
        self.op(eng, lambda e: e.memset(o, val), reads=[], writes=[out])

    def recip(self, out, in_):
        o, i = out.ap, in_.ap
        self.op("dve", lambda e: e.reciprocal(out=o, in_=i), reads=[in_], writes=[out])

    def scan(self, out, d0, d1, init, op0, op1):
        o, a, b = out.ap, d0.ap, d1.ap
        self.op("dve", lambda e: e.tensor_tensor_scan(out=o, data0=a, data1=b, initial=init, op0=op0, op1=op1),
                reads=[d0, d1], writes=[out])

    def affsel(self, out, in_, pattern, cmp, fill, base, cm):
        o, i = out.ap, in_.ap
        self.op("pool", lambda e: e.affine_select(out=o, in_=i, pattern=pattern, compare_op=cmp, fill=fill,
                                                  base=base, channel_multiplier=cm), reads=[in_], writes=[out])

    def pv(self, pst, name, c0, n, parts=128, p0=0):
        assert (c0 // 512) == ((c0 + n - 1) // 512), (name, c0, n)
        return V("psbank%d" % (c0 // 512), pst[p0:p0 + parts, c0:c0 + n])


D_MODEL = 2048
D_FF = 8192
IN_TOTAL = 6936
EPS = 1e-6


def tok_blocks(T, bs=128):
    return [(t0, min(bs, T - t0)) for t0 in range(0, T, bs)]


class Consts:
    pass


def build_consts(P, es):
    c = Consts()
    c.ident_r = P.sb(es, [128, 128], F32R, "identr")
    c.ident_f = P.sb(es, [128, 128], F32, "identf")
    c.ones_f = P.sb(es, [128, 128], F32, "onesf")
    c.zeros_f = P.sb(es, [128, 128], F32, "zerosf")
    c.mU_incl = P.sb(es, [64, 64], F32, "mUi")
    c.mU_strict = P.sb(es, [64, 64], F32, "mUs")
    c.mL_strict = P.sb(es, [64, 64], F32, "mLs")
    c.nL_strict = P.sb(es, [64, 64], F32, "nLs")
    c.nU_incl = P.sb(es, [64, 64], F32, "nUi")
    c.eps_col = P.sb(es, [128, 1], F32, "epsc")
    with P.stage("init"):
        P.memset(c.zeros_f, 0.0)
        P.memset(c.ones_f, 1.0)
        P.memset(c.eps_col, EPS)
        P.affsel(c.ident_f, c.zeros_f, [[-1, 128]], ALU.not_equal, 1.0, 0, 1)
        P.affsel(c.ident_r, c.zeros_f, [[-1, 128]], ALU.not_equal, 1.0, 0, 1)
        P.affsel(c.mU_incl, c.ones_f[0:64, 0:64], [[1, 64]], ALU.is_ge, 0.0, 0, -1)
        P.affsel(c.mU_strict, c.ones_f[0:64, 0:64], [[1, 64]], ALU.is_ge, 0.0, -1, -1)
        P.affsel(c.nU_incl, c.zeros_f[0:64, 0:64], [[1, 64]], ALU.is_ge, -1e30, 0, -1)
        P.affsel(c.mL_strict, c.ones_f[0:64, 0:64], [[-1, 64]], ALU.is_ge, 0.0, -1, 1)
        P.affsel(c.nL_strict, c.zeros_f[0:64, 0:64], [[-1, 64]], ALU.is_ge, -1e30, -1, 1)
    return c


def stage_norm(P, C, name, src, gain, T, resid=None, out_tok=None, outT=None, gain2=None):
    two = gain2 is not None
    with ExitStack() as es:
        gbc = P.sb(es, [128, D_MODEL], F32, "gbc")
        gbc2 = P.sb(es, [128, D_MODEL], F32, "gbc2") if two else None
        xt = [P.sb(es, [128, D_MODEL], F32, "xt") for _ in range(2)]
        rt = [P.sb(es, [128, D_MODEL], F32, "rt") for _ in range(2)] if resid is not None else None
        junk = P.sb(es, [128, D_MODEL], F32, "junk")
        first_r = (outT is not None) and not two and resid is None
        ut = [P.sb(es, [128, D_MODEL], F32R if first_r else F32, "ut") for _ in range(2)]
        vt = [P.sb(es, [128, D_MODEL], F32, "vt") for _ in range(2)] if (resid is not None) else None
        zt = [P.sb(es, [128, D_MODEL], F32R, "zt") for _ in range(2)] if two else None
        oT = [P.sb(es, [128, 16, 128], F32, "oT") for _ in range(2)] if outT is not None else None
        ss = [P.sb(es, [128, 1], F32, "ss") for _ in range(4)]
        sd = [P.sb(es, [128, 1], F32, "sd") for _ in range(4)]
        rs = [P.sb(es, [128, 1], F32, "rs") for _ in range(4)]
        pst = P.psum(es)
        with P.stage(name):
            P.dma(gbc, gain.pbcast(128))
            if two:
                P.dma(gbc2, gain2.pbcast(128), q="act")
            for bi, (t0, nt) in enumerate(tok_blocks(T)):
                b = bi % 2
                P.dma(xt[b][:nt], src[t0:t0 + nt, :])
                if resid is not None:
                    P.dma(rt[b][:nt], resid[t0:t0 + nt, :], q="act")
                P.act(junk[:nt], xt[b][:nt], AF.Square, accum=ss[b][:nt])
                P.act(sd[b][:nt], ss[b][:nt], AF.Sqrt, bias=C.eps_col[:nt], scale=1.0 / D_MODEL)
                P.recip(rs[b][:nt], sd[b][:nt])
                P.stt(ut[b][:nt], xt[b][:nt], rs[b][:nt, 0:1], gbc[:nt], ALU.mult, ALU.mult)
                res = ut[b]
                if resid is not None:
                    P.tt(vt[b][:nt], ut[b][:nt], rt[b][:nt], ALU.add, eng="pool")
                    res = vt[b]
                if out_tok is not None:
                    P.dma(out_tok[t0:t0 + nt, :], res[:nt], q="act")
                if two:
                    P.act(junk[:nt], res[:nt], AF.Square, accum=ss[2 + b][:nt])
                    P.act(sd[2 + b][:nt], ss[2 + b][:nt], AF.Sqrt, bias=C.eps_col[:nt], scale=1.0 / D_MODEL)
                    P.recip(rs[2 + b][:nt], sd[2 + b][:nt])
                    P.stt(zt[b][:nt], res[:nt], rs[2 + b][:nt, 0:1], gbc2[:nt], ALU.mult, ALU.mult)
                    res = zt[b]
                if outT is not None:
                    for c in range(16):
                        pv = P.pv(pst, "pb", (b * 4 + c // 4) * 512 + (c % 4) * 128, nt)
                        P.transpose(pv.bitcast(F32R), res[:nt, c * 128:(c + 1) * 128], C.ident_r[:nt, :nt])
                    for g in range(4):
                        pv = P.pv(pst, "pb", (b * 4 + g) * 512, 512)
                        src_v = pv.rearrange("p (c n) -> p c n", c=4)[:, :, :nt]
                        P.copy(oT[b][:, g * 4:(g + 1) * 4, :nt], src_v, eng=("act" if g % 2 else "dve"))
                    P.dma(outT.rearrange("(c p) t -> p c t", p=128)[:, :, t0:t0 + nt], oT[b][:, :, :nt])


def stage_gemm_A(P, C, name, xT, K, T, groups, epi="copy"):
    KC = K // 128
    TT = [(t0, min(512, T - t0)) for t0 in range(0, T, 512)]
    with ExitStack() as es:
        KG = 4
        xrs = [[P.sb(es, [128, KG, n_], F32R, "xr") for _ in range(KC // KG)] for (_, n_) in TT]
        wt = [P.sb(es, [128, KC, 256], F32R, "wt") for _ in range(2)]
        ot = [P.sb(es, [128, T], F32, "ot") for _ in range(2)]
        tmp = P.sb(es, [128, 512], F32, "tmp") if epi == "relu2" else None
        pst = P.psum(es)
        with P.stage(name):
            xTv = xT.bitcast(F32R).rearrange("(c p) t -> p c t", p=128)
            di = 0
            for ti, (t0, n) in enumerate(TT):
                for g in range(KC // KG):
                    P.dma(xrs[ti][g], xTv[:, g * KG:(g + 1) * KG, t0:t0 + n], q=("sp" if di % 2 else "act"))
                    di += 1
            bank = 0
            oi = 0
            for gi, (w_ap, chunks) in enumerate(groups):
                ncols = w_ap.shape[1]
                wb = wt[gi % 2]
                P.dma(wb[:, :, :ncols], w_ap.bitcast(F32R).rearrange("(c p) n -> p c n", p=128))
                for (c0, m, orows) in chunks:
                    ob = ot[oi % 2]
                    oi += 1
                    for ti, (t0, n) in enumerate(TT):
                        pv = P.pv(pst, "bank%d" % bank, bank * 512, n, parts=m)
                        bank = (bank + 1) % 8
                        for kc in range(KC):
                            P.mm(pv, wb[:, kc, c0:c0 + m], xrs[ti][kc // KG][:, kc % KG, :], start=(kc == 0), stop=(kc == KC - 1))
                        dst = ob[:m, t0:t0 + n]
                        if epi == "copy":
                            if ti % 2:
                                P.act(dst, pv, AF.Copy)
                            else:
                                P.copy(dst, pv)
                        elif epi == "sigmoid":
                            P.act(dst, pv, AF.Sigmoid)
                        elif epi == "relu2":
                            P.act(tmp[:m, :n], pv, AF.Relu)
                            P.tt(dst, tmp[:m, :n], tmp[:m, :n], ALU.mult)
                    P.dma(orows, ob[:m, :], q="act")


def stage_gemm_B(P, C, name, xT, K, T, W, N, NT, out_tok):
    KC = K // 128
    with ExitStack() as es:
        wr = P.sb(es, [128, KC, NT], F32R, "wr")
        xb = [P.sb(es, [128, KC, 128], F32R, "xb") for _ in range(2)]
        ot = [P.sb(es, [128, NT], F32, "ot") for _ in range(2)]
        pst = P.psum(es)
        with P.stage(name):
            xTv = xT.bitcast(F32R).rearrange("(c p) t -> p c t", p=128)
            Wv = W.bitcast(F32R).rearrange("(c p) n -> p c n", p=128)
            bank = 0
            it = 0
            for n0 in range(0, N, NT):
                for k0 in range(0, KC, 8):
                    P.dma(wr[:, k0:k0 + 8, :], Wv[:, k0:k0 + 8, n0:n0 + NT], q=("sp" if (k0 // 8) % 2 else "act"))
                for (t0, nt) in tok_blocks(T):
                    b = it % 2
                    it += 1
                    P.dma(xb[b][:, :, :nt], xTv[:, :, t0:t0 + nt])
                    for j in range(NT // 512):
                        pv = P.pv(pst, "bank%d" % bank, bank * 512, 512, parts=nt)
                        bank = (bank + 1) % 8
                        for kc in range(KC):
                            P.mm(pv, xb[b][:, kc, :nt], wr[:, kc, j * 512:(j + 1) * 512], start=(kc == 0), stop=(kc == KC - 1))
                        if j % 2:
                            P.act(ot[b][:nt, j * 512:(j + 1) * 512], pv, AF.Copy)
                        else:
                            P.copy(ot[b][:nt, j * 512:(j + 1) * 512], pv)
                    P.dma(out_tok[t0:t0 + nt, n0:n0 + NT], ot[b][:nt, :], q="act")


def _v_unsq(self, d):
    return V(self.key, self.ap.unsqueeze(d))


V.unsq = _v_unsq


def bc3(par, n):
    p, h = par.shape
    return par.unsq(2).bcast([p, h, n])


def tri_inverse_gen(P, C, pst, nm, X, XT, tiles):
    zt = tiles["z"][0]
    P.tt(zt, XT, C.ident_f[0:64, 0:64], ALU.add)
    cur = 0
    x, xt = X, XT
    for k in range(5):
        pp = P.pv(pst, nm + "_sq", tiles["pcol"], 128, parts=64)
        P.mm(pp[:, 0:64], xt, x)
        P.mm(pp[:, 64:128], x, xt)
        yield
        pair = tiles["pair"][k % 2]
        P.copy(pair, pp, eng="act")
        x, xt = pair[:, 0:64], pair[:, 64:128]
        yield
        pz = P.pv(pst, nm + "_z", tiles["pcol"] + 128, 64, parts=64)
        P.mm(pz, x, zt)
        yield
        znew = tiles["z"][1 - cur]
        P.tt(znew, pz, zt, ALU.add)
        zt = znew
        cur = 1 - cur
        yield
    return zt


def lockstep(gens):
    gens = list(gens)
    while gens:
        alive = []
        for g in gens:
            try:
                next(g)
                alive.append(g)
            except StopIteration:
                pass
        gens = alive


def stage_rwkv(P, C, name, l, pT, ymixT, seqs, prm, st_in, st_out):
    NT = 128
    with ExitStack() as es:
        def t3(nm, h=8, n=NT):
            return P.sb(es, [64, h, n], F32, nm)
        praw = P.sb(es, [64, 24, NT + 1], F32, "praw")
        pw = P.sb(es, [64, NT + 1], F32, "pw")
        pa = P.sb(es, [64, NT + 1], F32, "pa")
        pg = P.sb(es, [128, NT + 1], F32, "pg")
        xs = P.sb(es, [64, 24, NT], F32, "xs")
        dtl = P.sb(es, [64, 24, NT], F32, "dtl")
        xw = P.sb(es, [64, NT], F32, "xw")
        xa = P.sb(es, [64, NT], F32, "xa")
        xg = P.sb(es, [128, NT], F32, "xg")
        tmpg = P.sb(es, [128, NT], F32, "tmpg")
        a_t, g_t, lw, cum = t3("a"), t3("g"), t3("lw"), t3("cum")
        G, Gm1, Ginv, Edec = t3("G"), t3("Gm1"), t3("Ginv"), t3("Edec")
        kk, kh, b_t, bon = t3("kk"), t3("kh"), t3("b"), t3("bon")
        KKdec, Rdec, Kinv, Binv, Kdec, nBdec = t3("KKdec"), t3("Rdec"), t3("Kinv"), t3("Binv"), t3("Kdec"), t3("nBdec")
        yraw, tmp1, tmp2 = t3("yraw"), t3("tmp1"), t3("tmp2")
        mask0 = P.sb(es, [64, 8 * NT], F32, "mask0")
        mu_rkv = P.sb(es, [64, 24], F32, "mu_rkv")
        mu_w = P.sb(es, [64, 1], F32, "mu_w")
        mu_a = P.sb(es, [64, 1], F32, "mu_a")
        mu_g = P.sb(es, [128, 1], F32, "mu_g")
        pr = {k: P.sb(es, [64, 8], F32, k) for k in ["w0", "a0", "k_k", "k_a", "r_k", "ln_w", "ln_b", "omka"]}
        w_up = P.sb(es, [64, 512], F32, "w_up")
        a_up = P.sb(es, [64, 512], F32, "a_up")
        g_up = P.sb(es, [128, 512], F32, "g_up")
        M5 = P.sb(es, [64, 320], F32, "M5")
        gneps = P.sb(es, [64, 1], F32, "gneps")
        H = [[P.sb(es, [64, 64], F32, "H%d" % h) for _ in range(2)] for h in range(8)]
        S5 = [P.sb(es, [64, 320], F32, "S5") for _ in range(8)]
        tk = [P.sb(es, [64, 192], F32, "tk") for _ in range(8)]
        W1s = [P.sb(es, [64, 64], F32, "W1s") for _ in range(8)]
        Us = [P.sb(es, [64, 64], F32, "Us") for _ in range(8)]
        inv_tiles = [{"pair": [P.sb(es, [64, 128], F32, "pair") for _ in range(2)],
                      "z": [P.sb(es, [64, 64], F32, "z") for _ in range(2)], "pcol": 0} for _ in range(8)]
        stg = P.sb(es, [64, 8, 64], F32, "stg")
        pst = P.psum(es)
        with P.stage(name):
            P.dma(mu_rkv, prm["a_mu"][0:1536].rearrange("(s d) -> d s", d=64))
            P.dma(mu_w, prm["a_mu"][1536:1600].rearrange("(d o) -> d o", o=1))
            P.dma(mu_a, prm["a_mu"][1600:1664].rearrange("(d o) -> d o", o=1))
            P.dma(mu_g, prm["a_mu"][1664:1792].rearrange("(d o) -> d o", o=1))
            for k_ in ["w0", "a0", "k_k", "k_a", "ln_w", "ln_b"]:
                P.dma(pr[k_], prm["a_" + k_].rearrange("(h d) -> d h", d=64), q="act")
            P.dma(pr["r_k"], prm["a_r_k"].rearrange("h d -> d h"), q="act")
            P.dma(w_up, prm["a_w_up"])
            P.dma(a_up, prm["a_a_up"])
            P.dma(g_up, prm["a_g_up"])
            P.ts(pr["omka"], pr["k_a"], -1.0, ALU.mult, 1.0, ALU.add)
            P.memset(gneps, 64e-5)
            P.memset(mask0, 1.0)
            P.memset(mask0.rearrange("p (c k) -> p c k", k=64)[:, :, 0:1], 0.0)
            P.copy(M5[:, 0:64], C.mU_strict, eng="pool")
            P.copy(M5[:, 64:128], C.mU_incl, eng="pool")
            P.ts(M5[:, 128:192], C.mU_strict, -1.0, ALU.mult, eng="pool")
            P.ts(M5[:, 192:256], C.mU_incl, -1.0, ALU.mult, eng="pool")
            P.ts(M5[:, 256:320], C.mL_strict, -1.0, ALU.mult, eng="pool")
            unit = 0
            for si, (ts0, tl, is_s) in enumerate(seqs):
                hcur = [0] * 8
                if is_s:
                    P.dma(stg, st_in[1].rearrange("h v k -> v h k"))
                    for h in range(8):
                        pp = P.pv(pst, "ptr", 7 * 512, 64, parts=64)
                        P.mm(pp, stg[:, h, :], C.ident_f[0:64, 0:64])
                        P.copy(H[h][0], pp)
                else:
                    for h in range(8):
                        P.memset(H[h][0], 0.0)
                for t0 in range(ts0, ts0 + tl, NT):
                    n = min(NT, ts0 + tl - t0)
                    nch = n // 64
                    first = (t0 == ts0)
                    lo = 1 if first else 0
                    P.dma(praw[:, :, lo:n + 1], pT[0:1536, t0 - 1 + lo:t0 + n].rearrange("(s d) t -> d s t", d=64))
                    P.dma(pw[:, lo:n + 1], pT[1536:1600, t0 - 1 + lo:t0 + n], q="act")
                    P.dma(pa[:, lo:n + 1], pT[1600:1664, t0 - 1 + lo:t0 + n], q="act")
                    P.dma(pg[:, lo:n + 1], pT[1664:1792, t0 - 1 + lo:t0 + n], q="act")
                    if first:
                        if is_s:
                            sh = st_in[0]
                            P.dma(praw[:, :, 0:1], sh[0:1536].rearrange("(s d o) -> d s o", d=64, o=1))
                            P.dma(pw[:, 0:1], sh[1536:1600].rearrange("(d o) -> d o", o=1))
                            P.dma(pa[:, 0:1], sh[1600:1664].rearrange("(d o) -> d o", o=1))
                            P.dma(pg[:, 0:1], sh[1664:1792].rearrange("(d o) -> d o", o=1))
                        else:
                            P.memset(praw[:, :, 0:1], 0.0)
                            P.memset(pw[:, 0:1], 0.0)
                            P.memset(pa[:, 0:1], 0.0)
                            P.memset(pg[:, 0:1], 0.0)
                    P.tt(dtl[:, :, :n], praw[:, :, 0:n], praw[:, :, 1:n + 1], ALU.subtract)
                    P.tt(dtl[:, :, :n], dtl[:, :, :n], bc3(mu_rkv, n), ALU.mult)
                    P.tt(xs[:, :, :n], dtl[:, :, :n], praw[:, :, 1:n + 1], ALU.add)
                    for (xx, pp_, mu_, np_) in ((xw, pw, mu_w, 64), (xa, pa, mu_a, 64), (xg, pg, mu_g, 128)):
                        P.tt(tmpg[:np_, :n], pp_[:, 0:n], pp_[:, 1:n + 1], ALU.subtract)
                        P.stt(xx[:, :n], tmpg[:np_, :n], mu_[:, 0:1], pp_[:, 1:n + 1], ALU.mult, ALU.add)
                    xr, xk, xv = xs[:, 0:8, :n], xs[:, 8:16, :n], xs[:, 16:24, :n]
                    P.act(xw[:, :n], xw[:, :n], AF.Tanh)
                    P.act(xg[:, :n], xg[:, :n], AF.Sigmoid)
                    for h in range(8):
                        p1 = P.pv(pst, "pw%d" % (h % 2), (h % 2) * 256, n, parts=64)
                        P.mm(p1, w_up[:, h * 64:(h + 1) * 64], xw[:, :n])
                        P.act(lw[:, h, :n], p1, AF.Sigmoid, bias=pr["w0"][:, h:h + 1])
                        p2 = P.pv(pst, "pa%d" % (h % 2), 512 + (h % 2) * 256, n, parts=64)
                        P.mm(p2, a_up[:, h * 64:(h + 1) * 64], xa[:, :n])
                        P.act(a_t[:, h, :n], p2, AF.Sigmoid, bias=pr["a0"][:, h:h + 1])
                        p3 = P.pv(pst, "pg%d" % (h % 2), 1024 + (h % 2) * 256, n, parts=64)
                        P.mm(p3, g_up[:, h * 64:(h + 1) * 64], xg[:, :n])
                        P.copy(g_t[:, h, :n], p3)
                    P.ts(lw[:, :, :n], lw[:, :, :n], -0.6065306597, ALU.mult)
                    if n == NT:
                        P.scan(cum.rearrange("p h n -> p (h n)"), mask0, lw.rearrange("p h n -> p (h n)"), 0.0, ALU.mult, ALU.add)
                    else:
                        for h in range(8):
                            P.scan(cum[:, h, :n], mask0[:, :n], lw[:, h, :n], 0.0, ALU.mult, ALU.add)
                    P.act(G[:, :, :n], cum[:, :, :n], AF.Exp)
                    P.act(Ginv[:, :, :n], cum[:, :, :n], AF.Exp, scale=-1.0)
                    P.tt(tmp1[:, :, :n], cum[:, :, :n], lw[:, :, :n], ALU.subtract)
                    P.act(Gm1[:, :, :n], tmp1[:, :, :n], AF.Exp)
                    for h in range(8):
                        cv = cum[:, h, :n].rearrange("p (c k) -> p c k", k=64)
                        P.tt(tmp1[:, h, :n].rearrange("p (c k) -> p c k", k=64), cv[:, :, 63:64].bcast([64, nch, 64]), cv, ALU.subtract)
                    P.act(Edec[:, :, :n], tmp1[:, :, :n], AF.Exp)
                    P.tt(kk[:, :, :n], xk, bc3(pr["k_k"], n), ALU.mult)
                    P.tt(tmp1[:, :, :n], kk[:, :, :n], kk[:, :, :n], ALU.mult)
                    for h in range(8):
                        p1 = P.pv(pst, "pw%d" % (h % 2), (h % 2) * 256, n, parts=64)
                        P.mm(p1, C.ones_f[0:64, 0:64], tmp1[:, h, :n])
                        P.act(tmp2[:, h, :n], p1, AF.Sqrt, bias=C.eps_col[0:64])
                    P.recip(tmp2[:, :, :n], tmp2[:, :, :n])
                    P.tt(kk[:, :, :n], kk[:, :, :n], tmp2[:, :, :n], ALU.mult)
                    P.tt(tmp1[:, :, :n], a_t[:, :, :n], bc3(pr["k_a"], n), ALU.mult)
                    P.tt(tmp1[:, :, :n], tmp1[:, :, :n], bc3(pr["omka"], n), ALU.add)
                    P.tt(kh[:, :, :n], xk, tmp1[:, :, :n], ALU.mult)
                    P.tt(b_t[:, :, :n], kk[:, :, :n], a_t[:, :, :n], ALU.mult)
                    P.tt(tmp1[:, :, :n], xr, kh[:, :, :n], ALU.mult)
                    P.tt(tmp1[:, :, :n], tmp1[:, :, :n], bc3(pr["r_k"], n), ALU.mult)
                    for h in range(8):
                        p1 = P.pv(pst, "pw%d" % (h % 2), (h % 2) * 256, n, parts=64)
                        P.mm(p1, C.ones_f[0:64, 0:64], tmp1[:, h, :n])
                        P.tt(bon[:, h, :n], p1, xs[:, 16 + h, :n], ALU.mult)
                    P.tt(KKdec[:, :, :n], kk[:, :, :n], Gm1[:, :, :n], ALU.mult)
                    P.tt(Rdec[:, :, :n], xr, G[:, :, :n], ALU.mult)
                    P.tt(Kinv[:, :, :n], kh[:, :, :n], Ginv[:, :, :n], ALU.mult)
                    P.tt(Binv[:, :, :n], b_t[:, :, :n], Ginv[:, :, :n], ALU.mult)
                    P.tt(Kdec[:, :, :n], kh[:, :, :n], Edec[:, :, :n], ALU.mult)
                    P.stt(nBdec[:, :, :n], b_t[:, :, :n], -1.0, Edec[:, :, :n], ALU.mult, ALU.mult)
                    def unit_gen(h, c):
                        cs = slice(c * 64, (c + 1) * 64)
                        import os as _os2
                        _L = int(_os2.environ.get('LAYOUT', '0'))
                        base = h * 512 if _L == 0 else 1536 + (h % 2) * 1024
                        b2 = base if _L == 0 else base + 512
                        ptk = P.pv(pst, "ptk", base, 192, parts=64)
                        P.mm(ptk[:, 0:64], xs[:, 16 + h, cs], C.ident_f[0:64, 0:64])
                        P.mm(ptk[:, 64:128], Kdec[:, h, cs], C.ident_f[0:64, 0:64])
                        P.mm(ptk[:, 128:192], nBdec[:, h, cs], C.ident_f[0:64, 0:64])
                        p5 = P.pv(pst, "p5", base + 192, 320, parts=64)
                        P.mm(p5[:, 0:64], Kinv[:, h, cs], KKdec[:, h, cs])
                        P.mm(p5[:, 64:128], Kinv[:, h, cs], Rdec[:, h, cs])
                        P.mm(p5[:, 128:192], Binv[:, h, cs], KKdec[:, h, cs])
                        P.mm(p5[:, 192:256], Binv[:, h, cs], Rdec[:, h, cs])
                        P.mm(p5[:, 256:320], KKdec[:, h, cs], Binv[:, h, cs])
                        yield
                        P.copy(tk[h], ptk, eng="act")
                        Vt, Kdt, nBdt = tk[h][:, 0:64], tk[h][:, 64:128], tk[h][:, 128:192]
                        s5 = S5[h]
                        P.tt(s5, p5, M5, ALU.mult)
                        AT, PT, XT, nQT, X = s5[:, 0:64], s5[:, 64:128], s5[:, 128:192], s5[:, 192:256], s5[:, 256:320]
                        yield
                        it = inv_tiles[h]
                        it["pcol"] = b2
                        ZT = yield from tri_inverse_gen(P, C, pst, "inv", X, XT, it)
                        Hh = H[h][hcur[h]]
                        pW = P.pv(pst, "pW", b2 + 192, 64, parts=64)
                        P.mm(pW, KKdec[:, h, cs], Hh, start=True, stop=False)
                        P.mm(pW, AT, Vt, start=False, stop=True)
                        yield
                        P.copy(W1s[h], pW, eng="act")
                        yield
                        pU = P.pv(pst, "pU", b2 + 256, 64, parts=64)
                        P.mm(pU, ZT, W1s[h])
                        yield
                        P.copy(Us[h], pU, eng="act")
                        yield
                        pY = P.pv(pst, "pY", b2 + 320, 64, parts=64)
                        P.mm(pY, Hh, Rdec[:, h, cs], start=True, stop=False)
                        P.mm(pY, Vt, PT, start=False, stop=False)
                        P.mm(pY, Us[h], nQT, start=False, stop=True)
                        pH = P.pv(pst, "pH", b2 + 384, 64, parts=64)
                        P.mm(pH, Kdt, Vt, start=True, stop=False)
                        P.mm(pH, nBdt, Us[h], start=False, stop=True)
                        yield
                        P.copy(yraw[:, h, cs], pY, eng="act")
                        Hn = H[h][1 - hcur[h]]
                        gc = G[:, h, c * 64 + 63:c * 64 + 64]
                        P.stt(Hn, Hh, gc, pH, ALU.mult, ALU.add)
                        hcur[h] = 1 - hcur[h]

                    import os as _os
                    _G = int(_os.environ.get("LOCKG", "8"))
                    for c in range(nch):
                        for h0 in range(0, 8, _G):
                            lockstep([unit_gen(h, c) for h in range(h0, h0 + _G)])
                    for h in range(8):
                        p1 = P.pv(pst, "pw%d" % (h % 2), (h % 2) * 256, n, parts=64)
                        P.mm(p1, C.ones_f[0:64, 0:64], yraw[:, h, :n])
                        P.stt(tmp1[:, h, :n], p1, -1.0 / 64, yraw[:, h, :n], ALU.mult, ALU.add)
                    P.tt(tmp2[:, :, :n], tmp1[:, :, :n], tmp1[:, :, :n], ALU.mult)
                    for h in range(8):
                        p1 = P.pv(pst, "pw%d" % (h % 2), (h % 2) * 256, n, parts=64)
                        P.mm(p1, C.ones_f[0:64, 0:64], tmp2[:, h, :n])
                        P.act(dtl[:, h, :n], p1, AF.Sqrt, bias=gneps, scale=1.0 / 64)
                    P.recip(dtl[:, 0:8, :n], dtl[:, 0:8, :n])
                    P.tt(tmp1[:, :, :n], tmp1[:, :, :n], dtl[:, 0:8, :n], ALU.mult)
                    P.tt(tmp1[:, :, :n], tmp1[:, :, :n], bc3(pr["ln_w"], n), ALU.mult)
                    P.tt(tmp1[:, :, :n], tmp1[:, :, :n], bc3(pr["ln_b"], n), ALU.add)
                    P.tt(tmp1[:, :, :n], tmp1[:, :, :n], bon[:, :, :n], ALU.add)
                    P.tt(tmp2[:, :, :n], tmp1[:, :, :n], g_t[:, :, :n], ALU.mult)
                    P.dma(ymixT[0:512, t0:t0 + n].rearrange("(h d) t -> d h t", d=64), tmp2[:, :, :n])
                sh_out, s_out = st_out[si]
                tl_last = ts0 + tl - 1
                P.dma(sh_out.rearrange("(f o) -> f o", o=1), pT[0:1792, tl_last:tl_last + 1])
                for h in range(8):
                    pp = P.pv(pst, "ptr", 7 * 512, 64, parts=64)
                    P.mm(pp, H[h][hcur[h]], C.ident_f[0:64, 0:64])
                    P.copy(stg[:, h, :], pp)
                P.dma(s_out.rearrange("h v k -> v h k"), stg)


def stage_gla(P, C, name, pT, ymixT, seqs, prm, st_in, st_out):
    NT = 256
    B0 = 1792
    with ExitStack() as es:
        q = P.sb(es, [64, 4, NT], F32, "q")
        k = P.sb(es, [64, 4, NT], F32, "k")
        v = P.sb(es, [128, 4, NT], F32, "v")
        al = P.sb(es, [16, NT], F32, "al")
        rg = P.sb(es, [128, 4, NT], F32, "rg")
        la = P.sb(es, [64, 4, NT], F32, "la")
        cum = P.sb(es, [64, 4, NT], F32, "cum")
        Eb = P.sb(es, [64, 4, NT], F32, "Eb")
        qe = P.sb(es, [64, 4, NT], F32, "qe")
        ke = P.sb(es, [64, 4, NT], F32, "ke")
        kd = P.sb(es, [64, 4, NT], F32, "kd")
        t64 = P.sb(es, [64, 4, NT], F32, "t64")
        oraw = P.sb(es, [128, 4, NT], F32, "oraw")
        t128 = P.sb(es, [128, 4, NT], F32, "t128")
        u128 = P.sb(es, [128, 4, NT], F32, "u128")
        mask0 = P.sb(es, [64, 4 * NT], F32, "mask0")
        aup = P.sb(es, [16, 256], F32, "aup")
        nbias = P.sb(es, [64, 4], F32, "nbias")
        bnorm = P.sb(es, [128, 4], F32, "bnorm")
        S = [[P.sb(es, [64, 128], F32, "S%d" % h) for _ in range(2)] for h in range(4)]
        attT = [P.sb(es, [64, 64], F32, "attT") for _ in range(4)]
        tkb = [P.sb(es, [64, 192], F32, "tkb") for _ in range(4)]
        sstage = P.sb(es, [64, 4, 128], F32, "sstage")
        pst = P.psum(es)
        with P.stage(name):
            P.dma(aup, prm["b_alpha_up"])
            P.dma(nbias, prm["b_alpha_bias"].rearrange("(h d) -> d h", d=64))
            P.dma(bnorm, prm["b_norm"].rearrange("(h v) -> v h", v=128))
            P.ts(nbias, nbias, -1.0, ALU.mult)
            P.memset(mask0, 1.0)
            P.memset(mask0.rearrange("p (c k) -> p c k", k=64)[:, :, 0:1], 0.0)
            unit = 0
            for si, (ts0, tl, is_s) in enumerate(seqs):
                scur = [0] * 4
                if is_s:
                    P.dma(sstage, st_in.rearrange("h d v -> d h v"))
                    for h in range(4):
                        P.copy(S[h][0], sstage[:, h, :])
                else:
                    for h in range(4):
                        P.memset(S[h][0], 0.0)
                for t0 in range(ts0, ts0 + tl, NT):
                    n = min(NT, ts0 + tl - t0)
                    nch = n // 64
                    tsl = slice(t0, t0 + n)
                    P.dma(q[:, :, :n], pT[B0:B0 + 256, tsl].rearrange("(h d) t -> d h t", d=64))
                    P.dma(k[:, :, :n], pT[B0 + 256:B0 + 512, tsl].rearrange("(h d) t -> d h t", d=64))
                    P.dma(v[:, :, :n], pT[B0 + 512:B0 + 1024, tsl].rearrange("(h d) t -> d h t", d=128), q="act")
                    P.dma(al[:, :n], pT[B0 + 1024:B0 + 1040, tsl], q="act")
                    P.dma(rg[:, :, :n], pT[B0 + 1040:B0 + 1552, tsl].rearrange("(h d) t -> d h t", d=128), q="act")
                    for h in range(4):
                        p1 = P.pv(pst, "x", (h % 2) * 512, n, parts=64)
                        P.mm(p1, aup[:, h * 64:(h + 1) * 64], al[:, :n])
                        P.act(t64[:, h, :n], p1, AF.Exp, bias=nbias[:, h:h + 1], scale=-1.0)
                    P.act(t64[:, :, :n], t64[:, :, :n], AF.Ln, bias=1.0)
                    P.ts(la[:, :, :n], t64[:, :, :n], -1.0 / 16.0, ALU.mult)
                    if n == NT:
                        P.scan(cum.rearrange("p h n -> p (h n)"), mask0, la.rearrange("p h n -> p (h n)"), 0.0, ALU.mult, ALU.add)
                    else:
                        for h in range(4):
                            P.scan(cum[:, h, :n], mask0[:, :n], la[:, h, :n], 0.0, ALU.mult, ALU.add)
                    P.act(Eb[:, :, :n], cum[:, :, :n], AF.Exp)
                    P.stt(qe[:, :, :n], q[:, :, :n], 0.125, Eb[:, :, :n], ALU.mult, ALU.mult)
                    P.act(t64[:, :, :n], cum[:, :, :n], AF.Exp, scale=-1.0)
                    P.tt(ke[:, :, :n], k[:, :, :n], t64[:, :, :n], ALU.mult)
                    for h in range(4):
                        cv = cum[:, h, :n].rearrange("p (c k) -> p c k", k=64)
                        P.tt(t64[:, h, :n].rearrange("p (c k) -> p c k", k=64), cv[:, :, 63:64].bcast([64, nch, 64]), cv, ALU.subtract)
                    P.act(t64[:, :, :n], t64[:, :, :n], AF.Exp)
                    P.tt(kd[:, :, :n], k[:, :, :n], t64[:, :, :n], ALU.mult)
                    def unit_gen(h, c):
                        cs = slice(c * 64, (c + 1) * 64)
                        base = (2 + h) * 512
                        pA = P.pv(pst, "pA", base, 64, parts=64)
                        P.mm(pA, ke[:, h, cs], qe[:, h, cs])
                        ptk = P.pv(pst, "ptk", base + 64, 192, parts=64)
                        P.mm(ptk[:, 0:128], v[:, h, cs], C.ident_f)
                        P.mm(ptk[:, 128:192], kd[:, h, cs], C.ident_f[0:64, 0:64])
                        yield
                        P.tt(attT[h], pA, C.mU_incl, ALU.mult)
                        yield
                        P.copy(tkb[h], ptk, eng="act")
                        yield
                        Sh = S[h][scur[h]]
                        pO = P.pv(pst, "pO", base + 256, 64, parts=128)
                        P.mm(pO, Sh, qe[:, h, cs], start=True, stop=False)
                        P.mm(pO, tkb[h][:, 0:128], attT[h], start=False, stop=True)
                        pS = P.pv(pst, "pS", base + 320, 128, parts=64)
                        P.mm(pS, tkb[h][:, 128:192], tkb[h][:, 0:128])
                        yield
                        P.copy(oraw[:, h, cs], pO, eng="act")
                        yield
                        Sn = S[h][1 - scur[h]]
                        P.stt(Sn, Sh, Eb[:, h, c * 64 + 63:c * 64 + 64], pS, ALU.mult, ALU.add)
                        scur[h] = 1 - scur[h]

                    for c in range(nch):
                        lockstep([unit_gen(h, c) for h in range(4)])
                    P.tt(t128[:, :, :n], oraw[:, :, :n], oraw[:, :, :n], ALU.mult)
                    for h in range(4):
                        p1 = P.pv(pst, "x", (h % 2) * 512, n, parts=128)
                        P.mm(p1, C.ones_f, t128[:, h, :n])
                        P.act(u128[:, h, :n], p1, AF.Sqrt, bias=C.eps_col, scale=1.0 / 128)
                    P.recip(u128[:, :, :n], u128[:, :, :n])
                    P.tt(t128[:, :, :n], oraw[:, :, :n], u128[:, :, :n], ALU.mult)
                    P.tt(t128[:, :, :n], t128[:, :, :n], bc3(bnorm, n), ALU.mult)
                    P.act(u128[:, :, :n], rg[:, :, :n], AF.Sigmoid)
                    P.tt(u128[:, :, :n], u128[:, :, :n], rg[:, :, :n], ALU.mult)
                    P.tt(t128[:, :, :n], t128[:, :, :n], u128[:, :, :n], ALU.mult)
                    P.dma(ymixT[512:1024, tsl].rearrange("(h d) t -> d h t", d=128), t128[:, :, :n])
                for h in range(4):
                    P.copy(sstage[:, h, :], S[h][scur[h]])
                P.dma(st_out[si].rearrange("h d v -> d h v"), sstage)


def stage_dn(P, C, name, pT, ymixT, seqs, prm, st_in, st_out):
    NT = 128
    B0 = 4880
    with ExitStack() as es:
        xc = P.sb(es, [128, 12, NT + 3], F32, "xc")
        acc = P.sb(es, [128, 12, NT], F32, "acc")
        tm = P.sb(es, [128, 12, NT], F32, "tm")
        gate = P.sb(es, [128, 4, NT], F32, "gate")
        bet = P.sb(es, [4, NT], F32, "bet")
        araw = P.sb(es, [4, NT], F32, "araw")
        gg = P.sb(es, [4, NT], F32, "gg")
        cum4 = P.sb(es, [4, NT], F32, "cum4")
        ncum4 = P.sb(es, [4, NT], F32, "ncum4")
        r4 = {k_: P.sb(es, [4, NT], F32, k_) for k_ in ["ecum", "bec", "edec"]}
        bc = {k_: P.sb(es, [128, 4, NT], F32, "bc_" + k_) for k_ in ["bet", "ecum", "bec", "edec"]}
        KbT = P.sb(es, [128, 4, NT], F32, "KbT")
        KbeT = P.sb(es, [128, 4, NT], F32, "KbeT")
        VbT = P.sb(es, [128, 4, NT], F32, "VbT")
        qdT = P.sb(es, [128, 4, NT], F32, "qdT")
        kdT = P.sb(es, [128, 4, NT], F32, "kdT")
        oraw = P.sb(es, [128, 4, NT], F32, "oraw")
        mask0 = P.sb(es, [4, NT], F32, "mask0")
        cw = P.sb(es, [128, 12, 4], F32, "cw")
        alog = P.sb(es, [4, 1], F32, "alog")
        dtb = P.sb(es, [4, 1], F32, "dtb")
        nea = P.sb(es, [4, 1], F32, "nea")
        dnorm = P.sb(es, [128, 1], F32, "dnorm")
        sel = P.sb(es, [4, 4, 128], F32, "sel")
        S = [[P.sb(es, [128, 128], F32, "S%d" % h) for _ in range(2)] for h in range(4)]
        dec = [{k_: P.sb(es, [64, 64], F32, k_) for k_ in ["mL", "mU", "decL", "decUi", "decUs", "X", "XT", "attT"]} for _ in range(4)]
        tk = [P.sb(es, [64, 384], F32, "tk") for _ in range(4)]
        nwT = [P.sb(es, [128, 64], F32, "nwT") for _ in range(4)]
        dl = [P.sb(es, [64, 128], F32, "dl") for _ in range(4)]
        inv_tiles = [{"pair": [P.sb(es, [64, 128], F32, "pair") for _ in range(2)],
                      "z": [P.sb(es, [64, 64], F32, "z") for _ in range(2)], "pcol": 0} for _ in range(4)]
        sstage = P.sb(es, [128, 4, 128], F32, "sstage")
        pst = P.psum(es)
        with P.stage(name):
            for i in range(4):
                P.dma(cw[:, :, i:i + 1], prm["d_conv_w"][i, :].rearrange("(s p o) -> p s o", p=128, o=1))
            P.dma(alog, prm["d_a_log"].rearrange("(h o) -> h o", o=1))
            P.dma(dtb, prm["d_dt_bias"].rearrange("(h o) -> h o", o=1))
            P.dma(dnorm, prm["d_norm"].rearrange("(d o) -> d o", o=1))
            P.act(nea, alog, AF.Exp)
            P.ts(nea, nea, -1.0, ALU.mult)
            P.memset(mask0, 1.0)
            P.memset(mask0.rearrange("p (c k) -> p c k", k=64)[:, :, 0:1], 0.0)
            P.affsel(sel, C.ones_f[0:4, :].unsq(1).bcast([4, 4, 128]), [[-1, 4], [0, 128]], ALU.is_equal, 0.0, 0, 1)
            unit = 0
            for si, (ts0, tl, is_s) in enumerate(seqs):
                scur = [0] * 4
                if is_s:
                    P.dma(sstage, st_in[1].rearrange("h k v -> k h v"))
                    for h in range(4):
                        P.copy(S[h][0], sstage[:, h, :])
                else:
                    for h in range(4):
                        P.memset(S[h][0], 0.0)
                for t0 in range(ts0, ts0 + tl, NT):
                    n = min(NT, ts0 + tl - t0)
                    nch = n // 64
                    tsl = slice(t0, t0 + n)
                    first = (t0 == ts0)
                    lo = 3 if first else 0
                    P.dma(xc[:, :, lo:n + 3], pT[B0:B0 + 1536, t0 - 3 + lo:t0 + n].rearrange("(s p) t -> p s t", p=128))
                    if first:
                        if is_s:
                            for r_ in range(3):
                                P.dma(xc[:, :, r_:r_ + 1], st_in[0][r_, :].rearrange("(s p o) -> p s o", p=128, o=1))
                        else:
                            P.memset(xc[:, :, 0:3], 0.0)
                    P.dma(bet[:, :n], pT[B0 + 1536:B0 + 1540, tsl], q="act")
                    P.dma(araw[:, :n], pT[B0 + 1540:B0 + 1544, tsl], q="act")
                    P.dma(gate[:, :, :n], pT[B0 + 1544:B0 + 2056, tsl].rearrange("(s p) t -> p s t", p=128), q="act")
                    P.tt(acc[:, :, :n], xc[:, :, 0:n], cw[:, :, 0:1].bcast([128, 12, n]), ALU.mult)
                    for i in range(1, 4):
                        P.tt(tm[:, :, :n], xc[:, :, i:i + n], cw[:, :, i:i + 1].bcast([128, 12, n]), ALU.mult)
                        P.tt(acc[:, :, :n], acc[:, :, :n], tm[:, :, :n], ALU.add)
                    P.act(tm[:, :, :n], acc[:, :, :n], AF.Sigmoid)
                    P.tt(acc[:, :, :n], acc[:, :, :n], tm[:, :, :n], ALU.mult)
                    P.tt(tm[:, 0:8, :n], acc[:, 0:8, :n], acc[:, 0:8, :n], ALU.mult)
                    for s_ in range(8):
                        p1 = P.pv(pst, "x", (s_ % 2) * 512, n)
                        P.mm(p1, C.ones_f, tm[:, s_, :n])
                        P.act(tm[:, s_, :n], p1, AF.Sqrt, bias=C.eps_col)
                    P.recip(tm[:, 0:8, :n], tm[:, 0:8, :n])
                    P.stt(acc[:, 0:4, :n], acc[:, 0:4, :n], 128 ** -0.5, tm[:, 0:4, :n], ALU.mult, ALU.mult)
                    P.tt(acc[:, 4:8, :n], acc[:, 4:8, :n], tm[:, 4:8, :n], ALU.mult)
                    P.act(bet[:, :n], bet[:, :n], AF.Sigmoid)
                    P.act(gg[:, :n], araw[:, :n], AF.Exp, bias=dtb[:, 0:1])
                    P.act(gg[:, :n], gg[:, :n], AF.Ln, bias=1.0)
                    P.ts(gg[:, :n], gg[:, :n], nea[:, 0:1], ALU.mult)
                    P.scan(cum4[:, :n], mask0[:, :n], gg[:, :n], 0.0, ALU.mult, ALU.add)
                    P.ts(ncum4[:, :n], cum4[:, :n], -1.0, ALU.mult)
                    P.act(r4["ecum"][:, :n], cum4[:, :n], AF.Exp)
                    P.tt(r4["bec"][:, :n], r4["ecum"][:, :n], bet[:, :n], ALU.mult)
                    cv = cum4[:, :n].rearrange("p (c k) -> p c k", k=64)
                    P.tt(r4["edec"][:, :n].rearrange("p (c k) -> p c k", k=64), cv[:, :, 63:64].bcast([4, nch, 64]), cv, ALU.subtract)
                    P.act(r4["edec"][:, :n], r4["edec"][:, :n], AF.Exp)
                    srcs = {"bet": bet, "ecum": r4["ecum"], "bec": r4["bec"], "edec": r4["edec"]}
                    bi = 0
                    for k_ in ["bet", "ecum", "bec", "edec"]:
                        for h in range(4):
                            p1 = P.pv(pst, "x", (bi % 2) * 512, n)
                            bi += 1
                            P.mm(p1, sel[:, h, :], srcs[k_][:, :n])
                            P.copy(bc[k_][:, h, :n], p1, eng=("act" if bi % 2 else "dve"))
                    Qn, Kn, Vn = acc[:, 0:4, :n], acc[:, 4:8, :n], acc[:, 8:12, :n]
                    P.tt(KbT[:, :, :n], Kn, bc["bet"][:, :, :n], ALU.mult)
                    P.tt(KbeT[:, :, :n], Kn, bc["bec"][:, :, :n], ALU.mult)
                    P.tt(VbT[:, :, :n], Vn, bc["bet"][:, :, :n], ALU.mult)
                    P.tt(qdT[:, :, :n], Qn, bc["ecum"][:, :, :n], ALU.mult)
                    P.tt(kdT[:, :, :n], Kn, bc["edec"][:, :, :n], ALU.mult)
                    def unit_gen(h, c):
                        cs = slice(c * 64, (c + 1) * 64)
                        base = h * 1024
                        d = dec[h]
                        pD = P.pv(pst, "pD", base, 64, parts=64)
                        P.mm(pD, cum4[:, cs], sel[:, h, 0:64], start=True, stop=False)
                        P.mm(pD, sel[:, h, 0:64], ncum4[:, cs], start=False, stop=True)
                        pN = P.pv(pst, "pN", base + 64, 192, parts=64)
                        P.mm(pN[:, 0:64], KbT[:, h, cs], acc[:, 4 + h, cs])
                        P.mm(pN[:, 64:128], acc[:, 4 + h, cs], KbT[:, h, cs])
                        P.mm(pN[:, 128:192], acc[:, 4 + h, cs], acc[:, h, cs])
                        ptk = P.pv(pst, "ptk", base + 512, 384, parts=64)
                        P.mm(ptk[:, 0:128], VbT[:, h, cs], C.ident_f)
                        P.mm(ptk[:, 128:256], KbeT[:, h, cs], C.ident_f)
                        P.mm(ptk[:, 256:384], kdT[:, h, cs], C.ident_f)
                        yield
                        P.tt(d["mL"], pD, C.nL_strict, ALU.add)
                        P.stt(d["mU"], pD, -1.0, C.nU_incl, ALU.mult, ALU.add)
                        P.copy(tk[h], ptk, eng="act")
                        Vb, Kbe, kdec = tk[h][:, 0:128], tk[h][:, 128:256], tk[h][:, 256:384]
                        yield
                        P.act(d["decL"], d["mL"], AF.Exp)
                        P.act(d["decUi"], d["mU"], AF.Exp)
                        yield
                        P.tt(d["decUs"], d["decUi"], C.mU_strict, ALU.mult, eng="pool")
                        P.stt(d["X"], pN[:, 0:64], -1.0, d["decL"], ALU.mult, ALU.mult)
                        P.tt(d["attT"], pN[:, 128:192], d["decUi"], ALU.mult)
                        yield
                        P.stt(d["XT"], pN[:, 64:128], -1.0, d["decUs"], ALU.mult, ALU.mult)
                        yield
                        it = inv_tiles[h]
                        it["pcol"] = base
                        ZT = yield from tri_inverse_gen(P, C, pst, "inv", d["X"], d["XT"], it)
                        pw = P.pv(pst, "pw", base + 192, 64, parts=128)
                        P.mm(pw, Kbe, ZT)
                        yield
                        P.ts(nwT[h], pw, -1.0, ALU.mult)
                        yield
                        Sh = S[h][scur[h]]
                        pdl = P.pv(pst, "pdl", base + 256, 128, parts=64)
                        P.mm(pdl, ZT, Vb, start=True, stop=False)
                        P.mm(pdl, nwT[h], Sh, start=False, stop=True)
                        yield
                        P.copy(dl[h], pdl, eng="act")
                        yield
                        pO = P.pv(pst, "pO", base + 384, 64, parts=128)
                        P.mm(pO, Sh, qdT[:, h, cs], start=True, stop=False)
                        P.mm(pO, dl[h], d["attT"], start=False, stop=True)
                        pS = P.pv(pst, "pS", base + 512 + 384, 128, parts=128)
                        P.mm(pS, kdec, dl[h])
                        yield
                        P.copy(oraw[:, h, cs], pO, eng="act")
                        Sn = S[h][1 - scur[h]]
                        P.stt(Sn, Sh, bc["ecum"][:, h, c * 64 + 63:c * 64 + 64], pS, ALU.mult, ALU.add)
                        scur[h] = 1 - scur[h]

                    for c in range(nch):
                        lockstep([unit_gen(h, c) for h in range(4)])
                    P.tt(tm[:, 0:4, :n], oraw[:, :, :n], oraw[:, :, :n], ALU.mult)
                    for h in range(4):
                        p1 = P.pv(pst, "x", (h % 2) * 512, n)
                        P.mm(p1, C.ones_f, tm[:, h, :n])
                        P.act(tm[:, 4 + h, :n], p1, AF.Sqrt, bias=C.eps_col, scale=1.0 / 128)
                    P.recip(tm[:, 4:8, :n], tm[:, 4:8, :n])
                    P.tt(tm[:, 0:4, :n], oraw[:, :, :n], tm[:, 4:8, :n], ALU.mult)
                    P.act(tm[:, 4:8, :n], gate[:, :, :n], AF.Sigmoid)
                    P.tt(tm[:, 4:8, :n], tm[:, 4:8, :n], gate[:, :, :n], ALU.mult)
                    P.stt(tm[:, 8:12, :n], tm[:, 0:4, :n], dnorm[:, 0:1], tm[:, 4:8, :n], ALU.mult, ALU.mult)
                    P.dma(ymixT[1536:2048, tsl].rearrange("(h d) t -> d h t", d=128), tm[:, 8:12, :n])
                cv_out, s_out = st_out[si]
                tlast = ts0 + tl - 1
                for r_ in range(3):
                    tt_ = tlast - 2 + r_
                    P.dma(cv_out[r_, :].rearrange("(f o) -> f o", o=1), pT[B0:B0 + 1536, tt_:tt_ + 1])
                for h in range(4):
                    P.copy(sstage[:, h, :], S[h][scur[h]])
                P.dma(s_out.rearrange("h k v -> k h v"), sstage)


def band_bias_index():
    a = np.arange(2)[:, None, None, None, None]
    jj = np.arange(64)[None, :, None, None, None]
    r = np.arange(5)[None, None, :, None, None]
    u = np.arange(2)[None, None, None, :, None]
    i = np.arange(64)[None, None, None, None, :]
    delta = u + 8 - 2 * r - a
    m = 128 + 64 * delta + i - jj
    idx = np.clip(m, 0, 256)
    return idx.reshape(128, 5, 128)


def stage_band(P, C, name, pT, ymixT, TP, TS, biasT, cache_k, cache_v, outs):
    CB = 3344
    NBP = TP // 128
    LK = max(TP, 640)
    with ExitStack() as es:
        kTh = P.sb(es, [64, 8, LK], F32, "kTh")
        Vtok = P.sb(es, [128, max(NBP, 5), 512], F32, "Vtok")
        qblk = [P.sb(es, [64, 8, 128], F32, "qblk") for _ in range(2)]
        bias = P.sb(es, [128, 8, 640], F32, "bias")
        E = [P.sb(es, [128, 5, 128], F32, "E") for _ in range(2)]
        yout = [P.sb(es, [128, 4, 128], F32, "yout") for _ in range(2)]
        rec = [P.sb(es, [128, 128], F32, "rec") for _ in range(2)]
        fblk = [P.sb(es, [128, 4, 128], F32, "fblk") for _ in range(2)]
        Ktok = [P.sb(es, [128, 512], F32, "Ktok") for _ in range(2)]
        kc = P.sb(es, [128, 4, 512], F32, "kc")
        pst = P.psum(es)
        bias4 = bias.rearrange("p h (r c) -> p h r c", r=5)
        with P.stage(name):
            P.dma(bias, biasT.rearrange("h p c -> p h c"))
            P.memset(bias4[64:128, :, 4, 0:64], -1e30)
            P.memset(bias4[0:64, :, 0, 64:128], -1e30)
            cnt = [0]

            def tok_major(dst, src_rows, t0, nt):
                b = cnt[0] % 2
                cnt[0] += 1
                P.dma(fblk[b][:, :, :nt], pT[src_rows:src_rows + 512, t0:t0 + nt].rearrange("(s p) t -> p s t", p=128), q="act")
                pp = P.pv(pst, "ptm", b * 512, 512, parts=nt)
                for s_ in range(4):
                    P.mm(pp[:, s_ * 128:(s_ + 1) * 128], fblk[b][:, s_, :nt], C.ident_f)
                P.copy(dst, pp, eng=("act" if b else "dve"))

            def qblock(qb, kb0, nq, nbk, tcol):
                yo = yout[cnt[0] % 2]
                cnt[0] += 1
                r_lo = max(0, -kb0)
                rs = [r for r in range(r_lo, 5) if kb0 + r < nbk]
                for h in range(8):
                    par = h % 2
                    m, hl = h // 2, h % 2
                    e = E[par]
                    groups = [[r for r in rs if r < 4], [r for r in rs if r == 4]]
                    for gi, grp in enumerate(groups):
                        if not grp:
                            continue
                        bank = par * 2 + gi
                        for r in grp:
                            kb = kb0 + r
                            pv = P.pv(pst, "pS", bank * 512 + (r % 4) * 128, nq)
                            P.mm(pv, kTh[:, h, kb * 128:(kb + 1) * 128], qb[:, h, :nq])
                        r0, r1 = grp[0], grp[-1] + 1
                        psv = P.pv(pst, "pS", bank * 512 + (r0 % 4) * 128, (r1 - r0) * 128).rearrange("p (r c) -> p r c", c=128)[:, :, :nq]
                        P.stt(e[:, r0:r1, :nq], psv, 0.125, bias4[:, h, r0:r1, :nq], ALU.mult, ALU.add)
                        P.act(e[:, r0:r1, :nq], e[:, r0:r1, :nq], AF.Exp)
                    pO = P.pv(pst, "pO", (4 + par) * 512, nq)
                    for ri, r in enumerate(rs):
                        P.mm(pO, Vtok[:, kb0 + r, m * 128:(m + 1) * 128], e[:, r, :nq], start=(ri == 0), stop=(ri == len(rs) - 1))
                    pDn = P.pv(pst, "pDn", (6 + par) * 512, nq)
                    for ri, r in enumerate(rs):
                        P.mm(pDn, C.ones_f, e[:, r, :nq], start=(ri == 0), stop=(ri == len(rs) - 1))
                    sl = slice(hl * 64, (hl + 1) * 64)
                    P.recip(rec[par][sl, :nq], pDn[sl, :nq])
                    P.tt(yo[sl, m, :nq], pO[sl, :nq], rec[par][sl, :nq], ALU.mult)
                P.dma(ymixT[1024:1536, tcol:tcol + nq].rearrange("(m p) t -> p m t", p=128), yo[:, :, :nq])

            P.dma(kTh[:, :, 0:TP], pT[CB + 512:CB + 1024, 0:TP].rearrange("(h d) t -> d h t", d=64))
            for tb in range(NBP):
                tok_major(Vtok[:, tb, :], CB + 1024, tb * 128, 128)
            keep = min(512, TP)
            nkb = keep // 128
            P.dma(outs["p_bv"].rearrange("(b p) f -> p b f", p=128), Vtok[:, NBP - nkb:NBP, :])
            for bi_ in range(nkb):
                tb = NBP - nkb + bi_
                kt = Ktok[bi_ % 2]
                tok_major(kt, CB + 512, tb * 128, 128)
                P.dma(outs["p_bk"][bi_ * 128:(bi_ + 1) * 128, :], kt)
            for n in range(NBP):
                qb = qblk[n % 2]
                P.dma(qb, pT[CB:CB + 512, n * 128:(n + 1) * 128].rearrange("(h d) t -> d h t", d=64))
                qblock(qb, n - 4, 128, NBP, n * 128)
            P.dma(Vtok[:, 0:4, :], cache_v.rearrange("(b p) f -> p b f", p=128))
            P.memset(Vtok[:, 4, :], 0.0)
            tok_major(Vtok[0:TS, 4, :], CB + 1024, TP, TS)
            P.dma(outs["s_bv"], Vtok[0:TS, 4, :])
            kt = Ktok[0]
            tok_major(kt[0:TS, :], CB + 512, TP, TS)
            P.dma(outs["s_bk"], kt[0:TS, :])
            P.dma(kc, cache_k.rearrange("(b p) f -> p b f", p=128))
            for b_ in range(4):
                for hg in range(2):
                    pp = P.pv(pst, "ptm", ((b_ * 2 + hg) % 2) * 512, 512, parts=64)
                    for hh in range(4):
                        h = hg * 4 + hh
                        P.mm(pp[:, hh * 128:(hh + 1) * 128], kc[:, b_, h * 64:(h + 1) * 64], C.ident_f)
                    P.copy(kTh[:, hg * 4:(hg + 1) * 4, b_ * 128:(b_ + 1) * 128], pp.rearrange("p (h t) -> p h t", h=4), eng=("act" if hg else "dve"))
            P.memset(kTh[:, :, 512 + TS:640], 0.0)
            P.dma(kTh[:, :, 512:512 + TS], pT[CB + 512:CB + 1024, TP:TP + TS].rearrange("(h d) t -> d h t", d=64))
            qb = qblk[0]
            P.dma(qb[:, :, 0:TS], pT[CB:CB + 512, TP:TP + TS].rearrange("(h d) t -> d h t", d=64))
            qblock(qb, 0, TS, 5, TP)


def stage_merge(P, C, name, uT, ymixT, T, wg, wb, mT):
    halves = []
    th = ((T // 2 + 31) // 32) * 32
    t0 = 0
    while t0 < T:
        halves.append((t0, min(th, T - t0)))
        t0 += th
    TH = max(n for _, n in halves)
    with ExitStack() as es:
        ur = P.sb(es, [128, 16, TH], F32R, "ur")
        yr = P.sb(es, [128, 16, TH], F32R, "yr")
        wgt = [P.sb(es, [128, 16, 256], F32R, "wgt") for _ in range(2)]
        wbt = [P.sb(es, [128, 4, 256], F32R, "wbt") for _ in range(2)]
        acc = [P.sb(es, [128, 2, TH], F32, "acc") for _ in range(2)]
        sg = [P.sb(es, [128, 512], F32, "sg") for _ in range(2)]
        pr = [P.sb(es, [128, 512], F32, "pr") for _ in range(2)]
        pst = P.psum(es)
        with P.stage(name):
            uTv = uT.bitcast(F32R).rearrange("(c p) t -> p c t", p=128)
            yTv = ymixT.bitcast(F32R).rearrange("(c p) t -> p c t", p=128)
            it = 0
            bank = 0
            for hi, (h0, hn) in enumerate(halves):
                for kc in range(16):
                    P.dma(ur[:, kc, :hn], uTv[:, kc, h0:h0 + hn], q=("sp" if kc % 2 else "act"))
                    P.dma(yr[:, kc, :hn], yTv[:, kc, h0:h0 + hn], q=("act" if kc % 2 else "sp"))
                TT = [(t0, min(512, hn - t0)) for t0 in range(0, hn, 512)]
                for jp in range(8):
                    ac = acc[jp % 2]
                    for b in range(4):
                        wi = it % 2
                        it += 1
                        P.dma(wgt[wi], wg[b, :, jp * 256:(jp + 1) * 256].bitcast(F32R).rearrange("(c p) n -> p c n", p=128))
                        P.dma(wbt[wi], wb[b, :, jp * 256:(jp + 1) * 256].bitcast(F32R).rearrange("(c p) n -> p c n", p=128), q="act")
                        for jj in range(2):
                            for ti, (t0, n) in enumerate(TT):
                                pG = P.pv(pst, "pG", bank * 512, n)
                                bank = (bank + 1) % 8
                                for kc in range(16):
                                    P.mm(pG, wgt[wi][:, kc, jj * 128:(jj + 1) * 128], ur[:, kc, t0:t0 + n], start=(kc == 0), stop=(kc == 15))
                                pB = P.pv(pst, "pB", bank * 512, n)
                                bank = (bank + 1) % 8
                                for kc in range(4):
                                    P.mm(pB, wbt[wi][:, kc, jj * 128:(jj + 1) * 128], yr[:, b * 4 + kc, t0:t0 + n], start=(kc == 0), stop=(kc == 3))
                                s_ = sg[ti % 2]
                                P.act(s_[:, :n], pG, AF.Sigmoid)
                                if b == 0:
                                    P.tt(ac[:, jj, t0:t0 + n], s_[:, :n], pB, ALU.mult)
                                else:
                                    p_ = pr[ti % 2]
                                    P.tt(p_[:, :n], s_[:, :n], pB, ALU.mult)
                                    P.tt(ac[:, jj, t0:t0 + n], ac[:, jj, t0:t0 + n], p_[:, :n], ALU.add, eng="pool")
                    P.dma(mT[jp * 256:(jp + 1) * 256, h0:h0 + hn].rearrange("(j p) t -> p j t", p=128), ac[:, :, :hn], q="act")


def build_program(TP, TS, DEPTH=2):
    T = TP + TS
    nc = bass.Bass("TRN2", target_bir_lowering=False)
    nc.dge_precook = False

    def dram(name, shape, kind="Internal"):
        return V(name, nc.dram_tensor(name, list(shape), F32, kind=kind).ap())
    I = {}
    ishapes = {
        "xin": [T, 2048], "sh_in": [DEPTH, 1792], "rw_in": [DEPTH, 8, 64, 64], "gla_in": [DEPTH, 4, 64, 128],
        "ck_in": [DEPTH, 512, 512], "cv_in": [DEPTH, 512, 512], "conv_in": [DEPTH, 3, 1536], "dn_in": [DEPTH, 4, 128, 128],
        "biasT": [DEPTH, 8, 128, 640],
        "norm_mix_pre": [DEPTH, 2048], "norm_mix_post": [DEPTH, 2048], "norm_ffn_pre": [DEPTH, 2048], "norm_ffn_post": [DEPTH, 2048],
        "w_in": [DEPTH, 2048, 6936], "w_merge_gate": [DEPTH, 4, 2048, 2048], "w_branch": [DEPTH, 4, 512, 2048],
        "w_out": [DEPTH, 2048, 2048], "w_ffn_up": [DEPTH, 2048, 8192], "w_ffn_down": [DEPTH, 8192, 2048],
        "a_mu": [DEPTH, 1792], "a_w0": [DEPTH, 512], "a_w_up": [DEPTH, 64, 512], "a_a0": [DEPTH, 512], "a_a_up": [DEPTH, 64, 512],
        "a_g_up": [DEPTH, 128, 512], "a_k_k": [DEPTH, 512], "a_k_a": [DEPTH, 512], "a_r_k": [DEPTH, 8, 64], "a_ln_w": [DEPTH, 512],
        "a_ln_b": [DEPTH, 512], "b_alpha_up": [DEPTH, 16, 256], "b_alpha_bias": [DEPTH, 256], "b_norm": [DEPTH, 512],
        "d_conv_w": [DEPTH, 4, 1536], "d_a_log": [DEPTH, 4], "d_dt_bias": [DEPTH, 4], "d_norm": [DEPTH, 128],
    }
    for k_, sh in ishapes.items():
        I[k_] = dram(k_, sh, "ExternalInput")
    keep = min(512, TP)
    oshapes = {
        "y": [T, 2048],
        "p_shift": [DEPTH, 1792], "p_rwkv": [DEPTH, 8, 64, 64], "p_gla": [DEPTH, 4, 64, 128], "p_bk": [DEPTH, keep, 512],
        "p_bv": [DEPTH, keep, 512], "p_conv": [DEPTH, 3, 1536], "p_dn": [DEPTH, 4, 128, 128],
        "s_shift": [DEPTH, 1792], "s_rwkv": [DEPTH, 8, 64, 64], "s_gla": [DEPTH, 4, 64, 128], "s_bk": [DEPTH, TS, 512],
        "s_bv": [DEPTH, TS, 512], "s_conv": [DEPTH, 3, 1536], "s_dn": [DEPTH, 4, 128, 128],
    }
    O = {k_: dram(k_, sh, "ExternalOutput") for k_, sh in oshapes.items()}
    uT = dram("uT", [2048, T])
    pT = dram("pT", [IN_TOTAL, T])
    ymixT = dram("ymixT", [2048, T])
    mT = dram("mT", [2048, T])
    mo = dram("mo", [T, 2048])
    hres = dram("hres", [T, 2048])
    xres = dram("xres", [T, 2048])
    hidT = dram("hidT", [D_FF, T])
    seqs = [(0, TP, False), (TP, TS, True)]
    with ExitStack() as es:
        P = Prog(nc, es)
        C = build_consts(P, es)
        xcur = I["xin"]
        for l in range(DEPTH):
            L = "L%d_" % l
            if l == 0:
                stage_norm(P, C, L + "n1", xcur, I["norm_mix_pre"][l], T, outT=uT)
            groups = []
            for g0 in range(0, IN_TOTAL, 256):
                gw = min(256, IN_TOTAL - g0)
                chunks = []
                for c0 in range(0, gw, 128):
                    m = min(128, gw - c0)
                    chunks.append((c0, m, pT[g0 + c0:g0 + c0 + m, :]))
                groups.append((I["w_in"][l][:, g0:g0 + gw], chunks))
            stage_gemm_A(P, C, L + "inproj", uT, 2048, T, groups)
            prm = {k_: I[k_][l] for k_ in ishapes if k_[:2] in ("a_", "b_", "d_")}
            stage_rwkv2(P, C, L + "rwkv", l, pT, ymixT, seqs, prm, (I["sh_in"][l], I["rw_in"][l]),
                       {0: (O["p_shift"][l], O["p_rwkv"][l]), 1: (O["s_shift"][l], O["s_rwkv"][l])})
            stage_gla(P, C, L + "gla", pT, ymixT, seqs, prm, I["gla_in"][l], {0: O["p_gla"][l], 1: O["s_gla"][l]})
            stage_band(P, C, L + "band", pT, ymixT, TP, TS, I["biasT"][l], I["ck_in"][l], I["cv_in"][l],
                       {"p_bk": O["p_bk"][l], "p_bv": O["p_bv"][l], "s_bk": O["s_bk"][l], "s_bv": O["s_bv"][l]})
            stage_dn(P, C, L + "dn", pT, ymixT, seqs, prm, (I["conv_in"][l], I["dn_in"][l]),
                     {0: (O["p_conv"][l], O["p_dn"][l]), 1: (O["s_conv"][l], O["s_dn"][l])})
            stage_merge(P, C, L + "merge", uT, ymixT, T, I["w_merge_gate"][l], I["w_branch"][l], mT)
            stage_gemm_B(P, C, L + "wout", mT, 2048, T, I["w_out"][l], 2048, 2048, mo)
            stage_norm(P, C, L + "n23", mo, I["norm_mix_post"][l], T, resid=xcur, out_tok=hres, outT=uT,
                       gain2=I["norm_ffn_pre"][l])
            groups = []
            for g0 in range(0, D_FF, 256):
                groups.append((I["w_ffn_up"][l][:, g0:g0 + 256],
                               [(0, 128, hidT[g0:g0 + 128, :]), (128, 128, hidT[g0 + 128:g0 + 256, :])]))
            stage_gemm_A(P, C, L + "ffnup", uT, 2048, T, groups, epi="relu2")
            stage_gemm_B(P, C, L + "ffndn", hidT, D_FF, T, I["w_ffn_down"][l], 2048, 512, mo)
            xnext = O["y"] if l == DEPTH - 1 else xres
            if l == DEPTH - 1:
                stage_norm(P, C, L + "n4", mo, I["norm_ffn_post"][l], T, resid=hres, out_tok=xnext)
            else:
                stage_norm(P, C, L + "n41", mo, I["norm_ffn_post"][l], T, resid=hres, out_tok=xnext, outT=uT,
                           gain2=I["norm_mix_pre"][l + 1])
            xcur = xnext
        n_instr = P.n_instr
    return nc, list(ishapes.keys()), list(oshapes.keys()), n_instr


_W_KEYS = ["norm_mix_pre", "norm_mix_post", "norm_ffn_pre", "norm_ffn_post", "w_in", "w_merge_gate", "w_branch", "w_out",
           "w_ffn_up", "w_ffn_down", "a_mu", "a_w0", "a_w_up", "a_a0", "a_a_up", "a_g_up", "a_k_k", "a_k_a", "a_r_k",
           "a_ln_w", "a_ln_b", "b_alpha_up", "b_alpha_bias", "b_norm", "d_conv_w", "d_a_log", "d_dt_bias", "d_norm"]


def make_in_maps(inputs, n_cores, DEPTH):
    f = lambda a: np.ascontiguousarray(np.asarray(a), dtype=np.float32)
    idx = band_bias_index()
    crb = np.asarray(inputs["c_rel_bias"], dtype=np.float32)
    biasT = np.ascontiguousarray(crb[:, :, idx].reshape(DEPTH, 8, 128, 640))
    shared = {k_: f(inputs[k_]) for k_ in _W_KEYS}
    shared["biasT"] = biasT
    maps = []
    for b in range(n_cores):
        m = dict(shared)
        m["xin"] = f(np.concatenate([np.asarray(inputs["x_prompt"][b]), np.asarray(inputs["x_sample"][b])], axis=0))
        m["sh_in"] = f(np.asarray(inputs["state_rwkv_shift"])[:, b, 0])
        m["rw_in"] = f(np.asarray(inputs["state_rwkv"])[:, b])
        m["gla_in"] = f(np.asarray(inputs["state_gla"])[:, b])
        ck = np.asarray(inputs["cache_band_k"])[:, b]
        cv = np.asarray(inputs["cache_band_v"])[:, b]
        m["ck_in"] = f(ck.reshape(ck.shape[0], ck.shape[1], -1))
        m["cv_in"] = f(cv.reshape(cv.shape[0], cv.shape[1], -1))
        m["conv_in"] = f(np.asarray(inputs["state_dn_conv"])[:, b])
        m["dn_in"] = f(np.asarray(inputs["state_dn"])[:, b])
        maps.append(m)
    return maps


def assemble(results, TP, TS, DEPTH):
    st = lambda k_: np.stack([r[k_] for r in results], axis=1)
    B = len(results)
    y = np.stack([r["y"] for r in results], axis=0)
    out = [np.ascontiguousarray(y[:, :TP]), np.ascontiguousarray(y[:, TP:])]
    for pre in ("p_", "s_"):
        out.append(st(pre + "shift").reshape(DEPTH, B, 1, 1792))
        out.append(st(pre + "rwkv"))
        out.append(st(pre + "gla"))
        bk = st(pre + "bk")
        out.append(bk.reshape(DEPTH, B, bk.shape[2], 8, 64))
        bv = st(pre + "bv")
        out.append(bv.reshape(DEPTH, B, bv.shape[2], 8, 64))
        out.append(st(pre + "conv"))
        out.append(st(pre + "dn"))
    return tuple(np.ascontiguousarray(o, dtype=np.float32) for o in out)


def kernel(**inputs):
    B, TP, _ = np.asarray(inputs["x_prompt"]).shape
    TS = np.asarray(inputs["x_sample"]).shape[1]
    DEPTH = np.asarray(inputs["w_in"]).shape[0]
    nc, inames, onames, n_instr = build_program(TP, TS, DEPTH)
    maps = make_in_maps(inputs, B, DEPTH)
    res = run_bass_kernel_spmd(nc, maps, core_ids=list(range(B)))
    return assemble(res.results, TP, TS, DEPTH)


def run_all(gen):
    for _ in gen:
        pass


def stage_rwkv2(P, C, name, l, pT, ymixT, seqs, prm, st_in, st_out):
    NT = 64
    with ExitStack() as es:
        def mkset():
            d = {}
            d["praw"] = P.sb(es, [64, 24, NT + 1], F32, "praw")
            d["pw"] = P.sb(es, [64, NT + 1], F32, "pw")
            d["pa"] = P.sb(es, [64, NT + 1], F32, "pa")
            d["pg"] = P.sb(es, [128, NT + 1], F32, "pg")
            d["xs"] = P.sb(es, [64, 24, NT], F32, "xs")
            d["dtl"] = P.sb(es, [64, 24, NT], F32, "dtl")
            d["xw"] = P.sb(es, [64, NT], F32, "xw")
            d["xa"] = P.sb(es, [64, NT], F32, "xa")
            d["xg"] = P.sb(es, [128, NT], F32, "xg")
            d["tmpg"] = P.sb(es, [128, NT], F32, "tmpg")
            for k_ in ["a", "g", "lw", "cum", "G", "Gm1", "Ginv", "Edec", "kk", "kh", "b", "bon", "KKdec", "Rdec", "Kinv",
                       "Binv", "Kdec", "nBdec", "yraw", "tmp1", "tmp2"]:
                d[k_] = P.sb(es, [64, 8, NT], F32, k_)
            return d
        sets = [mkset(), mkset()]
        post_t = [{k_: P.sb(es, [64, 8, NT], F32, "post_" + k_) for k_ in ["q1", "q2", "q3"]} for _ in range(2)]
        mask0 = P.sb(es, [64, 8 * NT], F32, "mask0")
        mu_rkv = P.sb(es, [64, 24], F32, "mu_rkv")
        mu_w = P.sb(es, [64, 1], F32, "mu_w")
        mu_a = P.sb(es, [64, 1], F32, "mu_a")
        mu_g = P.sb(es, [128, 1], F32, "mu_g")
        pr = {k: P.sb(es, [64, 8], F32, k) for k in ["w0", "a0", "k_k", "k_a", "r_k", "ln_w", "ln_b", "omka"]}
        w_up = P.sb(es, [64, 512], F32, "w_up")
        a_up = P.sb(es, [64, 512], F32, "a_up")
        g_up = P.sb(es, [128, 512], F32, "g_up")
        M5 = P.sb(es, [64, 320], F32, "M5")
        gneps = P.sb(es, [64, 1], F32, "gneps")
        H = [[P.sb(es, [64, 64], F32, "H%d" % h) for _ in range(2)] for h in range(8)]
        S5 = [P.sb(es, [64, 320], F32, "S5") for _ in range(8)]
        tk = [P.sb(es, [64, 192], F32, "tk") for _ in range(8)]
        W1s = [P.sb(es, [64, 64], F32, "W1s") for _ in range(8)]
        Us = [P.sb(es, [64, 64], F32, "Us") for _ in range(8)]
        inv_tiles = [{"pair": [P.sb(es, [64, 128], F32, "pair") for _ in range(2)],
                      "z": [P.sb(es, [64, 64], F32, "z") for _ in range(2)], "pcol": 0} for _ in range(8)]
        stg = P.sb(es, [64, 8, 64], F32, "stg")
        pst = P.psum(es)
        n = NT
        hcur = [0] * 8
        GR = [(0, 3), (3, 6), (6, 8)]

        def frb(bank):
            return P.pv(pst, "fr", bank * 512, 512, parts=64)

        def flat(t):
            return t.rearrange("p h n -> p (h n)")

        def prep_gen(D, t0, first, is_s):
            praw, pw, pa, pg, xs, dtl = D["praw"], D["pw"], D["pa"], D["pg"], D["xs"], D["dtl"]
            xw, xa, xg, tmpg = D["xw"], D["xa"], D["xg"], D["tmpg"]
            a_t, g_t, lw, cum, G, Gm1, Ginv, Edec = D["a"], D["g"], D["lw"], D["cum"], D["G"], D["Gm1"], D["Ginv"], D["Edec"]
            kk, kh, b_t, bon = D["kk"], D["kh"], D["b"], D["bon"]
            KKdec, Rdec, Kinv, Binv, Kdec, nBdec = D["KKdec"], D["Rdec"], D["Kinv"], D["Binv"], D["Kdec"], D["nBdec"]
            tmp1, tmp2 = D["tmp1"], D["tmp2"]
            lo = 1 if first else 0
            P.dma(praw[:, :, lo:n + 1], pT[0:1536, t0 - 1 + lo:t0 + n].rearrange("(s d) t -> d s t", d=64))
            P.dma(pw[:, lo:n + 1], pT[1536:1600, t0 - 1 + lo:t0 + n], q="act")
            P.dma(pa[:, lo:n + 1], pT[1600:1664, t0 - 1 + lo:t0 + n], q="act")
            P.dma(pg[:, lo:n + 1], pT[1664:1792, t0 - 1 + lo:t0 + n], q="act")
            if first:
                if is_s:
                    sh = st_in[0]
                    P.dma(praw[:, :, 0:1], sh[0:1536].rearrange("(s d o) -> d s o", d=64, o=1))
                    P.dma(pw[:, 0:1], sh[1536:1600].rearrange("(d o) -> d o", o=1))
                    P.dma(pa[:, 0:1], sh[1600:1664].rearrange("(d o) -> d o", o=1))
                    P.dma(pg[:, 0:1], sh[1664:1792].rearrange("(d o) -> d o", o=1))
                else:
                    P.memset(praw[:, :, 0:1], 0.0)
                    P.memset(pw[:, 0:1], 0.0)
                    P.memset(pa[:, 0:1], 0.0)
                    P.memset(pg[:, 0:1], 0.0)
            yield
            P.tt(dtl, praw[:, :, 0:n], praw[:, :, 1:n + 1], ALU.subtract)
            for (xx, pp_, mu_, np_) in ((xw, pw, mu_w, 64), (xa, pa, mu_a, 64), (xg, pg, mu_g, 128)):
                P.tt(tmpg[:np_, :], pp_[:, 0:n], pp_[:, 1:n + 1], ALU.subtract, eng="pool")
                yield
                P.stt(xx, tmpg[:np_, :], mu_[:, 0:1], pp_[:, 1:n + 1], ALU.mult, ALU.add)
            yield
            P.tt(dtl, dtl, bc3(mu_rkv, n), ALU.mult)
            P.act(xw, xw, AF.Tanh)
            P.act(xg, xg, AF.Sigmoid)
            yield
            P.tt(xs, dtl, praw[:, :, 1:n + 1], ALU.add)
            xr, xk, xv = xs[:, 0:8, :], xs[:, 8:16, :], xs[:, 16:24, :]
            yield
            for kind, wmat, rhs_, bank in (("w", w_up, xw, 4), ("a", a_up, xa, 5), ("g", g_up, xg, 4)):
                pb = frb(bank)
                for h in range(8):
                    P.mm(pb[:, h * 64:(h + 1) * 64], wmat[:, h * 64:(h + 1) * 64], rhs_)
                pb3 = pb.rearrange("p (h t) -> p h t", h=8)
                if kind == "w":
                    P.tt(lw, pb3, bc3(pr["w0"], n), ALU.add)
                elif kind == "a":
                    P.tt(a_t, pb3, bc3(pr["a0"], n), ALU.add)
                else:
                    P.copy(g_t, pb3, eng="act")
                yield
            P.act(lw, lw, AF.Sigmoid)
            P.act(a_t, a_t, AF.Sigmoid)
            P.tt(kk, xk, bc3(pr["k_k"], n), ALU.mult)
            yield
            P.ts(lw, lw, -0.6065306597, ALU.mult)
            P.tt(tmp1, kk, kk, ALU.mult)
            yield
            P.scan(cum.rearrange("p h n -> p (h n)"), mask0, lw.rearrange("p h n -> p (h n)"), 0.0, ALU.mult, ALU.add)
            P.mm(frb(5), C.ones_f[0:64, 0:64], flat(tmp1))
            P.act(flat(tmp2), frb(5), AF.Sqrt, bias=C.eps_col[0:64])
            yield
            P.act(G, cum, AF.Exp)
            P.act(Ginv, cum, AF.Exp, scale=-1.0)
            P.tt(tmp1, cum, lw, ALU.subtract)
            yield
            P.act(Gm1, tmp1, AF.Exp)
            yield
            P.tt(tmp1, cum[:, :, 63:64].bcast([64, 8, 64]), cum, ALU.subtract)
            P.recip(tmp2, tmp2)
            yield
            P.act(Edec, tmp1, AF.Exp)
            P.tt(kk, kk, tmp2, ALU.mult)
            yield
            P.tt(tmp1, a_t, bc3(pr["k_a"], n), ALU.mult)
            yield
            P.tt(tmp1, tmp1, bc3(pr["omka"], n), ALU.add)
            yield
            P.tt(kh, xk, tmp1, ALU.mult)
            P.tt(b_t, kk, a_t, ALU.mult, eng="pool")
            yield
            P.tt(tmp1, xr, kh, ALU.mult)
            P.tt(KKdec, kk, Gm1, ALU.mult, eng="pool")
            yield
            P.tt(tmp1, tmp1, bc3(pr["r_k"], n), ALU.mult)
            P.tt(Rdec, xr, G, ALU.mult, eng="pool")
            yield
            P.mm(frb(4), C.ones_f[0:64, 0:64], flat(tmp1))
            P.tt(bon, frb(4).rearrange("p (h t) -> p h t", h=8), xv, ALU.mult)
            P.tt(Kinv, kh, Ginv, ALU.mult)
            P.tt(Binv, b_t, Ginv, ALU.mult, eng="pool")
            yield
            P.tt(Kdec, kh, Edec, ALU.mult, eng="pool")
            yield
            P.stt(nBdec, b_t, -1.0, Edec, ALU.mult, ALU.mult)

        def unit_gen(D, h):
            xs, G = D["xs"], D["G"]
            KKdec, Rdec, Kinv, Binv, Kdec, nBdec, yraw = D["KKdec"], D["Rdec"], D["Kinv"], D["Binv"], D["Kdec"], D["nBdec"], D["yraw"]
            base = (h % 4) * 512
            ptk = P.pv(pst, "ptk", base, 192, parts=64)
            P.mm(ptk[:, 0:64], xs[:, 16 + h, :], C.ident_f[0:64, 0:64])
            P.mm(ptk[:, 64:128], Kdec[:, h, :], C.ident_f[0:64, 0:64])
            P.mm(ptk[:, 128:192], nBdec[:, h, :], C.ident_f[0:64, 0:64])
            yield
            P.copy(tk[h], ptk, eng="act")
            Vt, Kdt, nBdt = tk[h][:, 0:64], tk[h][:, 64:128], tk[h][:, 128:192]
            yield
            p5 = P.pv(pst, "p5", base, 320, parts=64)
            P.mm(p5[:, 0:64], Kinv[:, h, :], KKdec[:, h, :])
            P.mm(p5[:, 64:128], Kinv[:, h, :], Rdec[:, h, :])
            P.mm(p5[:, 128:192], Binv[:, h, :], KKdec[:, h, :])
            P.mm(p5[:, 192:256], Binv[:, h, :], Rdec[:, h, :])
            P.mm(p5[:, 256:320], KKdec[:, h, :], Binv[:, h, :])
            yield
            s5 = S5[h]
            P.tt(s5, p5, M5, ALU.mult)
            AT, PT, XT, nQT, X = s5[:, 0:64], s5[:, 64:128], s5[:, 128:192], s5[:, 192:256], s5[:, 256:320]
            yield
            it = inv_tiles[h]
            it["pcol"] = base
            ZT = yield from tri_inverse_gen(P, C, pst, "inv", X, XT, it)
            Hh = H[h][hcur[h]]
            pW = P.pv(pst, "pW", base + 192, 64, parts=64)
            P.mm(pW, KKdec[:, h, :], Hh, start=True, stop=False)
            P.mm(pW, AT, Vt, start=False, stop=True)
            yield
            P.copy(W1s[h], pW, eng="act")
            yield
            pU = P.pv(pst, "pU", base + 256, 64, parts=64)
            P.mm(pU, ZT, W1s[h])
            yield
            P.copy(Us[h], pU, eng="act")
            yield
            pY = P.pv(pst, "pY", base + 192, 64, parts=64)
            P.mm(pY, Hh, Rdec[:, h, :], start=True, stop=False)
            P.mm(pY, Vt, PT, start=False, stop=False)
            P.mm(pY, Us[h], nQT, start=False, stop=True)
            pH = P.pv(pst, "pH", base + 256, 64, parts=64)
            P.mm(pH, Kdt, Vt, start=True, stop=False)
            P.mm(pH, nBdt, Us[h], start=False, stop=True)
            yield
            P.copy(yraw[:, h, :], pY, eng="act")
            yield
            Hn = H[h][1 - hcur[h]]
            P.stt(Hn, Hh, G[:, h, 63:64], pH, ALU.mult, ALU.add)
            hcur[h] = 1 - hcur[h]

        def post_gen(D, Q, t0):
            yraw, bon, g_t = D["yraw"], D["bon"], D["g"]
            q1, q2, q3 = Q["q1"], Q["q2"], Q["q3"]
            P.mm(frb(6), C.ones_f[0:64, 0:64], flat(yraw))
            P.stt(flat(q1), frb(6), -1.0 / 64, flat(yraw), ALU.mult, ALU.add)
            yield
            P.tt(q2, q1, q1, ALU.mult, eng="pool")
            yield
            P.mm(frb(7), C.ones_f[0:64, 0:64], flat(q2))
            P.act(flat(q3), frb(7), AF.Sqrt, bias=gneps, scale=1.0 / 64)
            yield
            P.recip(q3, q3)
            yield
            P.tt(q1, q1, q3, ALU.mult)
            yield
            P.tt(q1, q1, bc3(pr["ln_w"], n), ALU.mult)
            yield
            P.tt(q1, q1, bc3(pr["ln_b"], n), ALU.add)
            yield
            P.tt(q1, q1, bon, ALU.add)
            yield
            P.tt(q2, q1, g_t, ALU.mult)
            yield
            P.dma(ymixT[0:512, t0:t0 + n].rearrange("(h d) t -> d h t", d=64), q2)

        with P.stage(name):
            P.dma(mu_rkv, prm["a_mu"][0:1536].rearrange("(s d) -> d s", d=64))
            P.dma(mu_w, prm["a_mu"][1536:1600].rearrange("(d o) -> d o", o=1))
            P.dma(mu_a, prm["a_mu"][1600:1664].rearrange("(d o) -> d o", o=1))
            P.dma(mu_g, prm["a_mu"][1664:1792].rearrange("(d o) -> d o", o=1))
            for k_ in ["w0", "a0", "k_k", "k_a", "ln_w", "ln_b"]:
                P.dma(pr[k_], prm["a_" + k_].rearrange("(h d) -> d h", d=64), q="act")
            P.dma(pr["r_k"], prm["a_r_k"].rearrange("h d -> d h"), q="act")
            P.dma(w_up, prm["a_w_up"])
            P.dma(a_up, prm["a_a_up"])
            P.dma(g_up, prm["a_g_up"])
            P.ts(pr["omka"], pr["k_a"], -1.0, ALU.mult, 1.0, ALU.add)
            P.memset(gneps, 64e-5)
            P.memset(mask0, 1.0)
            P.memset(mask0.rearrange("p (c k) -> p c k", k=64)[:, :, 0:1], 0.0)
            P.copy(M5[:, 0:64], C.mU_strict, eng="pool")
            P.copy(M5[:, 64:128], C.mU_incl, eng="pool")
            P.ts(M5[:, 128:192], C.mU_strict, -1.0, ALU.mult, eng="pool")
            P.ts(M5[:, 192:256], C.mU_incl, -1.0, ALU.mult, eng="pool")
            P.ts(M5[:, 256:320], C.mL_strict, -1.0, ALU.mult, eng="pool")
            tiles = []
            for si, (ts0, tl, is_s) in enumerate(seqs):
                for t0 in range(ts0, ts0 + tl, NT):
                    tiles.append((si, t0, t0 == ts0, t0 + NT >= ts0 + tl, is_s))
            run_all(prep_gen(sets[0], tiles[0][1], tiles[0][2], tiles[0][4]))
            pending_post = None
            for ti, (si, t0, first, last, is_s) in enumerate(tiles):
                D = sets[ti % 2]
                if first:
                    if is_s:
                        P.dma(stg, st_in[1].rearrange("h v k -> v h k"))
                        for h in range(8):
                            pp = P.pv(pst, "ptr", (h % 4) * 512, 64, parts=64)
                            P.mm(pp, stg[:, h, :], C.ident_f[0:64, 0:64])
                            P.copy(H[h][hcur[h]], pp)
                    else:
                        for h in range(8):
                            P.memset(H[h][hcur[h]], 0.0)
                def slot_gen(j, D=D):
                    yield from unit_gen(D, j)
                    yield from unit_gen(D, j + 4)
                gens = [slot_gen(j) for j in range(4)]
                if pending_post is not None:
                    gens.append(pending_post)
                if ti + 1 < len(tiles):
                    nx = tiles[ti + 1]
                    gens.append(prep_gen(sets[(ti + 1) % 2], nx[1], nx[2], nx[4]))
                lockstep(gens)
                pending_post = post_gen(D, post_t[ti % 2], t0)
                if last:
                    sh_out, s_out = st_out[si]
                    tl_last = t0 + NT - 1
                    P.dma(sh_out.rearrange("(f o) -> f o", o=1), pT[0:1792, tl_last:tl_last + 1])
                    for h in range(8):
                        pp = P.pv(pst, "ptr", (h % 4) * 512, 64, parts=64)
                        P.mm(pp, H[h][hcur[h]], C.ident_f[0:64, 0:64])
                        P.copy(stg[:, h, :], pp)
                    P.dma(s_out.rearrange("h v k -> v h k"), stg)
            run_all(pending_post)
```
